# Optimizing a Trainium2 kernel written in Bass

```python
import math
import jax, jax.numpy as jnp
from jax import lax
import numpy as np

D_MODEL = 1024
BATCH = 8
SEQ = 2048
DEPTH = 4
DEC_BATCH = 32
DEC_SEQ = 4
PAST_LEN = 8192
PAGE_SIZE = 128

N_EVEN = (DEPTH + 1) // 2
N_ODD = DEPTH // 2
MIX_HALF = D_MODEL // 2
GLA_HEADS = 4
GLA_DV = MIX_HALF // GLA_HEADS
GLA_DK = GLA_DV // 2
GLA_RANK = 16
GLA_TAU = 16.0
GLA_CHUNK = 64
FOX_HEADS = 4
FOX_DH = MIX_HALF // FOX_HEADS
FOX_BLOCK = 128
FOX_BIAS_LO = 3.0
FOX_BIAS_HI = 9.0
POOL_WINDOWS = (2, 4, 8, 16)
POOL_GROUPS = 4
POOL_GC = D_MODEL // POOL_GROUPS
POOL_BUF = 15
D_FF = 4 * D_MODEL
EPS = 1e-6
IN_SIZES = (GLA_HEADS * GLA_DK, GLA_HEADS * GLA_DK, GLA_HEADS * GLA_DV, GLA_RANK, GLA_HEADS * GLA_DV,
            FOX_HEADS * FOX_DH, FOX_HEADS * FOX_DH, FOX_HEADS * FOX_DH, FOX_HEADS)
IN_COLS = sum(IN_SIZES)

kernel_name = "gla_fox_pool_hybrid_step"


def rmsnorm(x, g):
    xf = x.astype(jnp.float32)
    y = xf * lax.rsqrt(jnp.mean(xf * xf, axis=-1, keepdims=True) + EPS)
    return (y * g.astype(jnp.float32)).astype(x.dtype)


def sq_relu_mlp(h, w_up, w_down):
    a = jax.nn.relu(h @ w_up)
    return (a * a) @ w_down


def even_project(h, w_in, w_gate_up, b_gate, b_forget):
    B, L, _ = h.shape
    splits = [int(c) for c in np.cumsum(IN_SIZES)[:-1]]
    q_a, k_a, v_a, r_a, g_a, q_b, k_b, v_b, f_b = jnp.split(h @ w_in, splits, axis=-1)
    log_alpha = jax.nn.log_sigmoid((r_a @ w_gate_up + b_gate).astype(jnp.float32)) / GLA_TAU
    logf = jax.nn.log_sigmoid((f_b + b_forget).astype(jnp.float32))
    rs = lambda t, H: t.reshape(B, L, H, -1)
    return (rs(q_a, GLA_HEADS), rs(k_a, GLA_HEADS), rs(v_a, GLA_HEADS), rs(log_alpha, GLA_HEADS), g_a,
            rs(q_b, FOX_HEADS), rs(k_b, FOX_HEADS), rs(v_b, FOX_HEADS), logf)


def gla_scan(q, k, v, log_a, S0):
    out_dtype, st_dtype = v.dtype, S0.dtype
    B, L, H, DK = q.shape
    C = GLA_CHUNK if L % GLA_CHUNK == 0 else L
    n = L // C
    f32 = jnp.float32

    def to_chunks(t):
        return t.astype(f32).reshape(B, n, C, H, t.shape[-1]).transpose(1, 0, 3, 2, 4)

    qc, kc, vc, gc = to_chunks(q * (DK ** -0.5)), to_chunks(k), to_chunks(v), to_chunks(log_a)
    tril = jnp.tril(jnp.ones((C, C), dtype=bool))[None, None, :, :, None]

    def step(S, inp):
        qi, ki, vi, gi = inp
        b = jnp.cumsum(gi, axis=2)
        o_inter = jnp.einsum('bhtk,bhkv->bhtv', qi * jnp.exp(b), S)
        diff = b[:, :, :, None, :] - b[:, :, None, :, :]
        decay = jnp.exp(jnp.where(tril, diff, -jnp.inf))
        attn = jnp.einsum('bhtk,bhsk,bhtsk->bhts', qi, ki, decay)
        o = o_inter + jnp.einsum('bhts,bhsv->bhtv', attn, vi)
        b_last = b[:, :, -1:, :]
        S_new = jnp.exp(b_last[:, :, 0, :])[..., None] * S + jnp.einsum(
            'bhsk,bhsv->bhkv', ki * jnp.exp(b_last - b), vi)
        return S_new, o

    S, o = lax.scan(step, S0.astype(f32), (qc, kc, vc, gc))
    o = o.transpose(1, 0, 3, 2, 4).reshape(B, L, H, -1)
    return o.astype(out_dtype), S.astype(st_dtype)


def fox_prompt(q, k, v, logf):
    B, L, H, D = q.shape
    Fh = jnp.cumsum(logf, axis=1).transpose(0, 2, 1)
    QB = FOX_BLOCK if L % FOX_BLOCK == 0 else L
    nb = L // QB
    qb = q.reshape(B, nb, QB, H, D).transpose(1, 0, 2, 3, 4)
    Fqb = Fh.reshape(B, H, nb, QB).transpose(2, 0, 1, 3)
    pos_k = jnp.arange(L)
    scale = D ** -0.5

    def block(args):
        i, qi, Fi = args
        s = jnp.einsum('bqhd,bkhd->bhqk', qi, k).astype(jnp.float32) * scale
        s = s + Fi[..., None] - Fh[:, :, None, :]
        pos_q = i * QB + jnp.arange(QB)
        s = jnp.where(pos_k[None, :] <= pos_q[:, None], s, -jnp.inf)
        p = jax.nn.softmax(s, axis=-1)
        return jnp.einsum('bhqk,bkhd->bqhd', p.astype(v.dtype), v)

    o = lax.map(block, (jnp.arange(nb), qb, Fqb))
    return o.transpose(1, 0, 2, 3, 4).reshape(B, L, H, D)


def fox_sample(q, k_new, v_new, logf_new, k_past, v_past, logf_past):
    B, T, H, D = q.shape
    P = k_past.shape[1]
    scale = D ** -0.5
    lp = logf_past.astype(jnp.float32)
    FnH = jnp.cumsum(logf_new.astype(jnp.float32), axis=1).transpose(0, 2, 1)
    RH = (lax.cumsum(lp, axis=1, reverse=True) - lp).transpose(0, 2, 1)
    s_past = jnp.einsum('bqhd,bkhd->bhqk', q, k_past).astype(jnp.float32) * scale
    s_past = s_past + FnH[..., None] + RH[:, :, None, :]
    s_new = jnp.einsum('bqhd,bkhd->bhqk', q, k_new).astype(jnp.float32) * scale
    s_new = s_new + FnH[..., None] - FnH[:, :, None, :]
    s_new = jnp.where(jnp.tril(jnp.ones((T, T), dtype=bool)), s_new, -jnp.inf)
    p = jax.nn.softmax(jnp.concatenate([s_past, s_new], axis=-1), axis=-1).astype(v_new.dtype)
    return (jnp.einsum('bhqk,bkhd->bqhd', p[..., :P], v_past)
            + jnp.einsum('bhqk,bkhd->bqhd', p[..., P:], v_new))


def even_merge(o_a, g_a, o_b, gnorm, w_out):
    B, L = o_a.shape[:2]
    o_a = rmsnorm(o_a, gnorm) * jax.nn.silu(g_a.reshape(B, L, GLA_HEADS, GLA_DV))
    o = jnp.concatenate([o_a.reshape(B, L, -1), o_b.reshape(B, L, -1)], axis=-1)
    return o @ w_out


def pool_mix(u, buf, pos0, w_pool, scale):
    B, L, D = u.shape
    ext = jnp.concatenate([buf.astype(u.dtype), u], axis=1)
    cs = jnp.concatenate([jnp.zeros((B, 1, D), jnp.float32),
                          jnp.cumsum(ext.astype(jnp.float32), axis=1)], axis=1)
    end = cs[:, POOL_BUF + 1:]
    pos = pos0 + jnp.arange(L) + 1
    outs = []
    for g, w in enumerate(POOL_WINDOWS):
        sl = slice(g * POOL_GC, (g + 1) * POOL_GC)
        start = cs[:, POOL_BUF + 1 - w: POOL_BUF + 1 - w + L, sl]
        cnt = jnp.minimum(w, pos).astype(jnp.float32)
        outs.append((end[..., sl] - start) / cnt[None, :, None])
    pooled = jnp.concatenate(outs, axis=-1) - u.astype(jnp.float32)
    y = jnp.einsum('blgc,gcd->blgd', pooled.reshape(B, L, POOL_GROUPS, POOL_GC).astype(u.dtype), w_pool)
    return y.reshape(B, L, D) * scale, ext[:, -POOL_BUF:]


def setup_inputs(seed: int = 0) -> dict:
    key = jax.random.key(seed)
    ks = jax.random.split(key, 24)
    n_pages = PAST_LEN // PAGE_SIZE
    n_pool = (DEC_BATCH * n_pages * 5) // 4
    nrm = lambda k, s: jax.random.normal(k, s, jnp.float32)
    page_table = jax.random.permutation(ks[0], n_pool)[:DEC_BATCH * n_pages].reshape(
        DEC_BATCH, n_pages).astype(jnp.int32)
    head_bias = jnp.linspace(FOX_BIAS_LO, FOX_BIAS_HI, FOX_HEADS).astype(jnp.float32)
    return {
        "x_prompt": nrm(ks[1], (BATCH, SEQ, D_MODEL)),
        "x_sample": nrm(ks[2], (DEC_BATCH, DEC_SEQ, D_MODEL)),
        "cache_fox_k": nrm(ks[3], (N_EVEN, n_pool, PAGE_SIZE, FOX_HEADS, FOX_DH)),
        "cache_fox_v": nrm(ks[4], (N_EVEN, n_pool, PAGE_SIZE, FOX_HEADS, FOX_DH)),
        "cache_fox_logf": jax.nn.log_sigmoid(
            head_bias + 0.5 * nrm(ks[5], (N_EVEN, n_pool, PAGE_SIZE, FOX_HEADS))),
        "state_gla": 0.5 * nrm(ks[6], (N_EVEN, DEC_BATCH, GLA_HEADS, GLA_DK, GLA_DV)),
        "state_pool": nrm(ks[7], (N_ODD, DEC_BATCH, POOL_BUF, D_MODEL)),
        "page_table": page_table,
        "norm_mix": 1.0 + 0.1 * nrm(ks[8], (DEPTH, D_MODEL)),
        "norm_mlp": 1.0 + 0.1 * nrm(ks[9], (DEPTH, D_MODEL)),
        "norm_final": 1.0 + 0.1 * nrm(ks[10], (D_MODEL,)),
        "w_in_even": nrm(ks[11], (N_EVEN, D_MODEL, IN_COLS)) * D_MODEL ** -0.5,
        "w_gate_up": nrm(ks[12], (N_EVEN, GLA_RANK, GLA_HEADS * GLA_DK)) * GLA_RANK ** -0.5,
        "b_gate": 0.1 * nrm(ks[13], (N_EVEN, GLA_HEADS * GLA_DK)),
        "gla_norm": 1.0 + 0.1 * nrm(ks[14], (N_EVEN, GLA_DV)),
        "b_forget": head_bias + 0.3 * nrm(ks[15], (N_EVEN, FOX_HEADS)),
        "w_out_even": nrm(ks[16], (N_EVEN, D_MODEL, D_MODEL)) * D_MODEL ** -0.5,
        "w_pool": nrm(ks[17], (N_ODD, POOL_GROUPS, POOL_GC, POOL_GC)) * POOL_GC ** -0.5,
        "pool_scale": 1.0 + 0.1 * nrm(ks[18], (N_ODD, D_MODEL)),
        "w_mlp_up": nrm(ks[19], (DEPTH, D_MODEL, D_FF)) * D_MODEL ** -0.5,
        "w_mlp_down": nrm(ks[20], (DEPTH, D_FF, D_MODEL)) * D_FF ** -0.5,
    }


def reference(x_prompt, x_sample, cache_fox_k, cache_fox_v, cache_fox_logf, state_gla, state_pool, page_table,
              norm_mix, norm_mlp, norm_final, w_in_even, w_gate_up, b_gate, gla_norm, b_forget, w_out_even,
              w_pool, pool_scale, w_mlp_up, w_mlp_down):
    xp, xs = x_prompt, x_sample
    Bp, Bs = xp.shape[0], xs.shape[0]
    past_len = page_table.shape[1] * PAGE_SIZE
    kp_l, vp_l, fp_l, ks_l, vs_l, fs_l = [], [], [], [], [], []
    gp_l, gs_l, pp_l, ps_l = [], [], [], []
    for layer in range(DEPTH):
        hp = rmsnorm(xp, norm_mix[layer])
        hs = rmsnorm(xs, norm_mix[layer])
        if layer % 2 == 0:
            e = layer // 2
            qa, ka, va, la, ga, qb, kb, vb, fb = even_project(hp, w_in_even[e], w_gate_up[e], b_gate[e], b_forget[e])
            S0 = jnp.zeros((Bp, GLA_HEADS, GLA_DK, GLA_DV), state_gla.dtype)
            oa, Sp = gla_scan(qa, ka, va, la, S0)
            ob = fox_prompt(qb, kb, vb, fb)
            xp = xp + even_merge(oa, ga, ob, gla_norm[e], w_out_even[e])
            kp_l.append(kb); vp_l.append(vb); fp_l.append(fb); gp_l.append(Sp)
            qa, ka, va, la, ga, qb, kb, vb, fb = even_project(hs, w_in_even[e], w_gate_up[e], b_gate[e], b_forget[e])
            oa, Ss = gla_scan(qa, ka, va, la, state_gla[e])
            k_past = cache_fox_k[e][page_table].reshape(Bs, past_len, FOX_HEADS, FOX_DH)
            v_past = cache_fox_v[e][page_table].reshape(Bs, past_len, FOX_HEADS, FOX_DH)
            f_past = cache_fox_logf[e][page_table].reshape(Bs, past_len, FOX_HEADS)
            ob = fox_sample(qb, kb, vb, fb, k_past, v_past, f_past)
            xs = xs + even_merge(oa, ga, ob, gla_norm[e], w_out_even[e])
            ks_l.append(kb); vs_l.append(vb); fs_l.append(fb); gs_l.append(Ss)
        else:
            o = layer // 2
            yp, bufp = pool_mix(hp, jnp.zeros((Bp, POOL_BUF, D_MODEL), hp.dtype), 0, w_pool[o], pool_scale[o])
            ys, bufs = pool_mix(hs, state_pool[o], past_len, w_pool[o], pool_scale[o])
            xp = xp + yp
            xs = xs + ys
            pp_l.append(bufp); ps_l.append(bufs)
        xp = xp + sq_relu_mlp(rmsnorm(xp, norm_mlp[layer]), w_mlp_up[layer], w_mlp_down[layer])
        xs = xs + sq_relu_mlp(rmsnorm(xs, norm_mlp[layer]), w_mlp_up[layer], w_mlp_down[layer])
    y_prompt = rmsnorm(xp, norm_final)
    y_sample = rmsnorm(xs, norm_final)
    return (y_prompt, y_sample,
            jnp.stack(kp_l), jnp.stack(vp_l), jnp.stack(fp_l),
            jnp.stack(ks_l), jnp.stack(vs_l), jnp.stack(fs_l),
            jnp.stack(gp_l), jnp.stack(gs_l),
            jnp.stack(pp_l), jnp.stack(ps_l))
```

```python
import numpy as np
import concourse.bass as bass
import concourse.mybir as mybir
from concourse.bass_utils import run_bass_kernel_spmd

F32 = mybir.dt.float32
BF16 = mybir.dt.bfloat16
I32 = mybir.dt.int32
AF = mybir.ActivationFunctionType
ALU = mybir.AluOpType


class Sched:
    ENGS = ('pe', 'act', 'dve', 'pool', 'sp')
    NDMA = 20

    def __init__(self, nc, same_engine_sync=True):
        self.nc = nc
        self.ops = []
        self.last_w = {}
        self.readers = {}
        self.same = same_engine_sync
        self.out_dmas = []
        self.dma_rr = {'sp': 0, 'pool': 0, 'act': 0}
        self.dma_last = {}

    def op(self, eng, fn, r=(), w=(), dma=False, out=False):
        idx = len(self.ops)
        w = list(w) + [k for k in r if isinstance(k, str) and k.startswith('ps') and k not in w]
        deps = set()
        for k in r:
            if k in self.last_w:
                deps.add(self.last_w[k])
        for k in w:
            if k in self.last_w:
                deps.add(self.last_w[k])
            deps |= self.readers.get(k, set())
        semslot = None
        if dma:
            slot = self.dma_rr[eng]
            self.dma_rr[eng] = (slot + 1) % self.NDMA
            semslot = (eng, slot)
            if semslot in self.dma_last:
                deps.add(self.dma_last[semslot])
            self.dma_last[semslot] = idx
        for k in r:
            self.readers.setdefault(k, set()).add(idx)
        for k in w:
            self.last_w[k] = idx
            self.readers[k] = set()
        self.ops.append(dict(eng=eng, fn=fn, deps=deps, dma=dma, semslot=semslot))
        if out:
            self.out_dmas.append(idx)
        return idx

    def dma(self, eng, out_ap, in_ap, r=(), w=(), out=False, **kw):
        return self.op(eng, lambda e: e.dma_start(out=out_ap, in_=in_ap, **kw), r=r, w=w, dma=True, out=out)

    def emit(self):
        nc = self.nc
        ops = self.ops
        self.ops.append(dict(eng='sp', fn=None, deps=set(self.out_dmas) | set(self.dma_last.values()), dma=False, semslot=None))
        import contextlib
        stack = contextlib.ExitStack()
        esem = {e: stack.enter_context(nc.semaphore("se_" + e)) for e in ('pe', 'act', 'dve', 'pool')}
        dsem = {}
        for e in ('sp', 'pool'):
            for s in range(self.NDMA):
                dsem[(e, s)] = stack.enter_context(nc.semaphore("sd_%s_%d" % (e, s)))
        ecount = {e: 0 for e in esem}
        dcount = {k: 0 for k in dsem}
        for o in ops:
            if o['dma']:
                dcount[o['semslot']] += 16
                o['sig'] = (dsem[o['semslot']], dcount[o['semslot']], o['semslot'])
            elif o['eng'] in esem and o['fn'] is not None:
                ecount[o['eng']] += 1
                o['sig'] = (esem[o['eng']], ecount[o['eng']], o['eng'])
            else:
                o['sig'] = None
        streams = {e: [] for e in self.ENGS}
        for i, o in enumerate(ops):
            streams[o['eng']].append(i)
        eobj = {'pe': 'tensor', 'act': 'scalar', 'dve': 'vector', 'pool': 'gpsimd', 'sp': 'sync'}

        def make(engname):
            def body(e):
                waited = {}
                for i in streams[engname]:
                    o = ops[i]
                    for d in sorted(o['deps']):
                        dd = ops[d]
                        if dd['sig'] is None:
                            continue
                        sem, val, key = dd['sig']
                        if (not dd['dma']) and dd['eng'] == engname and (engname == 'pe' or not self.same):
                            continue
                        if waited.get(key, 0) >= val:
                            continue
                        e.wait_ge(sem, val)
                        waited[key] = val
                    if o['fn'] is None:
                        continue
                    ins = o['fn'](e)
                    if o['sig'] is not None:
                        ins.then_inc(o['sig'][0], 16 if o['dma'] else 1)
            return body

        with nc.Block() as block:
            block.tensor(make('pe'))
            block.scalar(make('act'))
            block.vector(make('dve'))
            block.gpsimd(make('pool'))
            block.sync(make('sp'))
        stack.close()


D = 1024
DFF = 4096
KC = 8
FC = 32
EPS = 1e-6
NINC = 3092
C_QKA, C_VA, C_RA, C_GA, C_QB, C_KB, C_VB, C_FB = 0, 512, 1024, 1040, 1552, 2064, 2576, 3088
SHIFT_C = 8.0


class Cfg:
    def __init__(self, seq=2048, depth=4, tt=256, npg=64, npool=2560):
        self.seq = seq
        self.depth = depth
        self.tt = min(tt, seq)
        self.ntt = seq // self.tt
        self.npg = npg
        self.npool = npool
        self.nbs = 4
        import os
        self.flags = os.environ.get('KFLAGS', 'sample,gla,fox,past,pool,mlp')
        self.kstop = int(os.environ.get('KSTOP', '99'))
        self.ns = 16


def _cst_layout():
    names = [('ident', 128), ('triU', 128), ('sufU', 128), ('triC', 128), ('onesC', 128), ('maskC', 128),
             ('csel', 2), ('sel01', 2), ('triCs', 16), ('onesCs', 16), ('maskCs', 16), ('csels', 4), ('sel01s', 4),
             ('iota', 1), ('invc', 64)]
    off, o = {}, 0
    for n, w in names:
        off[n] = (o, w)
        o += w
    return off, o


CST_OFF, CST_W = _cst_layout()


def make_consts():
    c = np.zeros((128, CST_W), np.float32)
    def put(name, a):
        o, w = CST_OFF[name]
        c[:a.shape[0], o:o + a.shape[1]] = a
    i = np.arange(128)
    put('ident', np.eye(128, dtype=np.float32))
    put('triU', (i[:, None] <= i[None, :]).astype(np.float32))
    put('sufU', (i[:, None] > i[None, :]).astype(np.float32))
    same = (i[:, None] // 64) == (i[None, :] // 64)
    put('triC', np.where(same & (i[:, None] <= i[None, :]), -1.0 / 16, 0.0).astype(np.float32))
    put('onesC', np.where(same, -1.0 / 16, 0.0).astype(np.float32))
    put('maskC', (same & (i[:, None] <= i[None, :])).astype(np.float32))
    put('csel', np.where((i[:, None] // 64) == np.arange(2)[None, :], -1.0 / 16, 0.0).astype(np.float32))
    put('sel01', ((i[:, None] // 64) == np.arange(2)[None, :]).astype(np.float32))
    j = np.arange(16)
    sames = (j[:, None] // 4) == (j[None, :] // 4)
    put('triCs', np.where(sames & (j[:, None] <= j[None, :]), -1.0 / 16, 0.0).astype(np.float32))
    put('onesCs', np.where(sames, -1.0 / 16, 0.0).astype(np.float32))
    put('maskCs', (sames & (j[:, None] <= j[None, :])).astype(np.float32))
    put('csels', np.where((j[:, None] // 4) == np.arange(4)[None, :], -1.0 / 16, 0.0).astype(np.float32))
    put('sel01s', ((j[:, None] // 4) == np.arange(4)[None, :]).astype(np.float32))
    put('iota', i[:, None].astype(np.float32))
    pos = np.arange(16, dtype=np.float32) + 1.0
    inv = np.concatenate([1.0 / np.minimum(float(2 << g), pos) for g in range(4)])[None, :]
    put('invc', np.broadcast_to(inv, (128, 64)).astype(np.float32))
    return c


class Builder:
    def __init__(self, cfg):
        self.cfg = cfg
        nc = self.nc = bass.Bass("TRN2", target_bir_lowering=False)
        self.S = Sched(nc)
        T, L = cfg.seq, cfg.depth
        NS, NB, NPG = cfg.ns, cfg.nbs, cfg.npg
        di = lambda n, s, d=F32: nc.dram_tensor(n, s, d, kind="ExternalInput").ap()
        do = lambda n, s: nc.dram_tensor(n, s, F32, kind="ExternalOutput").ap()
        self.xp = di("xp", [T, D]); self.xs = di("xs", [NS, D])
        self.cst = di("cst", [128, CST_W])
        self.NV = (2 * L + 3) * KC + 2
        self.vecs = di("vecs", [128, self.NV])
        self.w_in = di("w_in", [2, D, NINC]); self.w_out = di("w_out", [2, D, D])
        self.w_gate = di("w_gate", [2, 16, 256]); self.b_gate = di("b_gate", [2, 1, 256]); self.b_forget = di("b_forget", [2, 1, 4])
        self.w_up = di("w_up", [L, D, DFF]); self.w_down = di("w_down", [L, DFF, D]); self.w_pool = di("w_pool", [2, 4, 256, 256])
        self.ktp = [di("ktp%d" % i, [cfg.npool * 128, 512]) for i in range(2)]
        self.ptab = di("ptab", [1, NB * NPG], I32)
        self.sgla = di("sgla", [2, NB, 4, 64, 128]); self.spool = di("spool", [2, NB, 15, D])
        self.y = do("y", [T, D]); self.ys = do("ys", [NS, D])
        self.fk = do("fk", [2, T, 512]); self.fv = do("fv", [2, T, 512]); self.fl = do("fl", [2, T, 4])
        self.fks = do("fks", [2, NS, 512]); self.fvs = do("fvs", [2, NS, 512]); self.fls = do("fls", [2, NS, 4])
        self.gp = do("gp", [2, 4, 64, 128]); self.gs = do("gs", [2, NB, 4, 64, 128])
        self.npool = do("npool", [2, 15, D]); self.nps = do("nps", [2, NB, 15, D])
        self.psn = 0
        self.wslot = 0
        self.ukeys = []

    def ps(self):
        i = self.psn
        self.psn = (self.psn + 1) % 6
        return self.pst[i], 'ps%d' % i

    def wb(self):
        i = self.wslot
        self.wslot = (self.wslot + 1) % len(self.wbufs)
        return self.wbufs[i], 'wb%d' % i

    def vec(self, idx):
        return self.vecs_sb[:, idx * KC:(idx + 1) * KC]

    def cs(self, name, rows=128, cols=None):
        o, w = CST_OFF[name]
        return self.cst_sb[0:rows, o:o + (cols if cols is not None else w)]

    def fence(self):
        ks = list(self.ukeys)
        self.S.op('dve', lambda e: e.memset(self.dummy[:], 0.0), r=ks, w=ks + ['dummy'])

    def carve(self, key, words, dt, pattern=None, **kw):
        off = self.uoff
        self.uoff += words
        assert self.uoff <= self.UW, (key, self.uoff)
        ap = self.U[:, off:off + words]
        if dt != F32:
            ap = ap.bitcast(dt)
        if pattern:
            ap = ap.rearrange(pattern, **kw)
        if key not in self.ukeys:
            self.ukeys.append(key)
        return ap

    def rmsnorm(self, x, xk, n, gidx, out, outk):
        S = self.S
        sq, rs = self.sq, self.rstd
        g = self.vec(gidx)
        pt, pk = self.ps()
        for kc in range(KC):
            S.op('act', lambda e, kc=kc: e.activation(sq[:, kc, :n], x[:, kc, :], AF.Square), r=[xk], w=['sq'])
        for kc in range(KC):
            S.op('pe', lambda e, kc=kc: e.matmul(pt[:, :n], self.ones_bf[:], sq[:, kc, :n], start=(kc == 0), stop=(kc == KC - 1)),
                 r=['sq', 'ones'], w=[pk])
        S.op('act', lambda e: e.activation(rs[:, :n], pt[:, :n], AF.Ln, scale=1.0 / D, bias=EPS), r=[pk], w=['rstd'])
        S.op('act', lambda e: e.activation(rs[:, :n], rs[:, :n], AF.Exp, scale=-0.5), r=['rstd'], w=['rstd'])
        for kc in range(KC):
            S.op('dve', lambda e, kc=kc: e.scalar_tensor_tensor(out[:, kc, :], x[:, kc, :], g[:, kc:kc + 1], rs[:, :n], ALU.mult, ALU.mult),
                 r=[xk, 'rstd', 'vecs'], w=[outk])

    def load_wcols(self, src2d, c0, ncols):
        wt, wk = self.wb()
        wv = wt[:, 0:KC * ncols].rearrange("p (kc f) -> p kc f", kc=KC)
        self.S.dma('pool', wv, src2d.rearrange("(kc p) f -> p kc f", p=128)[:, :, c0:c0 + ncols], w=[wk])
        return wv, wk

    def mlp(self, layer, x, xk, n):
        S = self.S
        hT, aT = self.hT, self.aT
        self.rmsnorm(x, xk, n, self.cfg.depth + layer, hT[:, :, :n], 'hT')
        wd = self.w_down[layer].rearrange("(fc p) d -> p fc d", p=128)
        for fb in range(8):
            wv, wk = self.load_wcols(self.w_up[layer], fb * 512, 512)
            for j in range(4):
                fc = fb * 4 + j
                pt, pk = self.ps()
                for kc in range(KC):
                    S.op('pe', lambda e, kc=kc, j=j, wv=wv, pt=pt: e.matmul(pt[:, :n], wv[:, kc, j * 128:(j + 1) * 128], hT[:, kc, :n],
                                                                   start=(kc == 0), stop=(kc == KC - 1)), r=[wk, 'hT'], w=[pk])
                S.op('act', lambda e, pt=pt: e.activation(self.relu[:, :n], pt[:, :n], AF.Relu), r=[pk], w=['relu'])
                S.op('dve', lambda e, fc=fc: e.tensor_tensor(aT[:, fc, :n], self.relu[:, :n], self.relu[:, :n], ALU.mult), r=['relu'], w=['aT'])
        for dc in range(KC):
            wt, wk = self.wb()
            wv = wt[:, 0:FC * 128].rearrange("p (fc d) -> p fc d", fc=FC)
            S.dma('pool', wv, wd[:, :, dc * 128:(dc + 1) * 128], w=[wk])
            pt, pk = self.ps()
            for fc in range(FC):
                S.op('pe', lambda e, fc=fc, wv=wv, pt=pt: e.matmul(pt[:, :n], wv[:, fc, :], aT[:, fc, :n], start=(fc == 0), stop=(fc == FC - 1)),
                     r=[wk, 'aT'], w=[pk])
            S.op('dve', lambda e, dc=dc, pt=pt: e.tensor_tensor(x[:, dc, :], x[:, dc, :], pt[:, :n], ALU.add), r=[pk, xk], w=[xk])

    def pool_mix(self, o, x, xk, n, first, nb=1, hb=15):
        S = self.S
        uF, A, B, pb = self.uF, self.wsA, self.wsB, self.hT
        npt = n // nb
        Lx = hb + npt
        uv = uF[:, :, 0:nb * Lx].rearrange("p k (b l) -> p k b l", b=nb)
        av = A[:, :, 0:nb * Lx].rearrange("p k (b l) -> p k b l", b=nb)
        bv = B[:, :, 0:nb * Lx].rearrange("p k (b l) -> p k b l", b=nb)
        pbv = pb[:, :, 0:n].rearrange("p k (b l) -> p k b l", b=nb)
        xv = x.rearrange("p k (b l) -> p k b l", b=nb)
        sq, rs = self.sq, self.rstd
        g = self.vec(2 * o + 1)
        pt, pk = self.ps()
        for kc in range(KC):
            S.op('act', lambda e, kc=kc: e.activation(sq[:, kc, :n], x[:, kc, :], AF.Square), r=[xk], w=['sq'])
        for kc in range(KC):
            S.op('pe', lambda e, kc=kc, pt=pt: e.matmul(pt[:, :n], self.ones_bf[:], sq[:, kc, :n], start=(kc == 0), stop=(kc == KC - 1)), r=['sq', 'ones'], w=[pk])
        S.op('act', lambda e, pt=pt: e.activation(rs[:, :n], pt[:, :n], AF.Ln, scale=1.0 / D, bias=EPS), r=[pk], w=['rstd'])
        S.op('act', lambda e: e.activation(rs[:, :n], rs[:, :n], AF.Exp, scale=-0.5), r=['rstd'], w=['rstd'])
        rsv = rs[:, :n].rearrange("p (b l) -> p b l", b=nb)
        for kc in range(KC):
            S.op('dve', lambda e, kc=kc: e.scalar_tensor_tensor(uv[:, kc, :, hb:Lx], xv[:, kc, :, :], g[:, kc:kc + 1], rsv, ALU.mult, ALU.mult),
                 r=[xk, 'rstd', 'vecs'], w=['uF'])
        wt, wk = self.wb()
        wv = wt[:, 0:2048].rearrange("p (g kc d) -> p g kc d", g=4, kc=2)
        S.dma('pool', wv, self.w_pool[o].rearrange("g (kc p) d -> p g kc d", p=128), w=[wk])
        for g_ in range(4):
            w = 2 << g_
            kparts = [(slice(2 * g_, 2 * g_ + 2), slice(0, 2))] if nb == 1 else [(2 * g_ + kk, kk) for kk in range(2)]
            for ku, kw in kparts:
                src, srck = None, 'uF'
                bufs = [(av, 'wsA'), (bv, 'wsB')]
                for st in range(g_ + 1):
                    sh = 1 << st
                    dst, dstk = bufs[st % 2]
                    lo = 2 * sh - 1
                    if src is None:
                        S.op('dve', lambda e, dst=dst, ku=ku, kw=kw, sh=sh, lo=lo: e.tensor_tensor(dst[:, kw, :, lo:Lx], uv[:, ku, :, lo:Lx], uv[:, ku, :, lo - sh:Lx - sh], ALU.add),
                             r=['uF'], w=[dstk])
                    else:
                        S.op('dve', lambda e, dst=dst, src=src, kw=kw, sh=sh, lo=lo: e.tensor_tensor(dst[:, kw, :, lo:Lx], src[:, kw, :, lo:Lx], src[:, kw, :, lo - sh:Lx - sh], ALU.add),
                             r=[srck], w=[dstk])
                    src, srck = dst, dstk
                S.op('dve', lambda e, src=src, ku=ku, kw=kw, w=w: e.scalar_tensor_tensor(pbv[:, ku, :, :], src[:, kw, :, hb:Lx], 1.0 / w, uv[:, ku, :, hb:Lx],
                     ALU.mult, ALU.subtract), r=[srck, 'uF'], w=['hT'])
            ksl = slice(2 * g_, 2 * g_ + 2)
            if first:
                ic = self.cs('invc')[:, g_ * 16:g_ * 16 + 15]
                S.op('dve', lambda e, src=src, ic=ic: e.tensor_tensor(src[:, :, 0, hb:hb + 15], src[:, :, 0, hb:hb + 15],
                     ic.unsqueeze(1).to_broadcast([128, 2, 15]), ALU.mult), r=[srck, 'cst'], w=[srck])
                S.op('dve', lambda e, src=src, ksl=ksl: e.tensor_tensor(pbv[:, ksl, 0, 0:15], src[:, :, 0, hb:hb + 15], uv[:, ksl, 0, hb:hb + 15], ALU.subtract),
                     r=[srck, 'uF'], w=['hT'])
            for oc in range(2):
                pt, pk = self.ps()
                for k2 in range(2):
                    S.op('pe', lambda e, g_=g_, oc=oc, k2=k2, pt=pt: e.matmul(pt[:, :n], wv[:, g_, k2, oc * 128:(oc + 1) * 128], pb[:, 2 * g_ + k2, :n],
                         start=(k2 == 0), stop=(k2 == 1)), r=[wk, 'hT'], w=[pk])
                dc = 2 * g_ + oc
                psc = self.vec(2 * self.cfg.depth + 1 + o)
                S.op('dve', lambda e, dc=dc, pt=pt, psc=psc: e.scalar_tensor_tensor(x[:, dc, :], pt[:, :n], psc[:, dc:dc + 1], x[:, dc, :], ALU.mult, ALU.add),
                     r=[pk, xk, 'vecs'], w=[xk])

    def transpose_out(self, src3, srck, ncol, dst_dram, rows):
        S = self.S
        for kc in range(KC):
            pt, pk = self.ps()
            S.op('pe', lambda e, kc=kc, pt=pt: e.transpose(pt[0:ncol, 0:128], src3[:, kc, :], self.cs('ident')), r=[srck, 'cst'], w=[pk])
            S.op('act', lambda e, kc=kc, pt=pt: e.activation(self.tio[0:ncol, kc * 128:(kc + 1) * 128], pt[0:ncol, 0:128], AF.Copy), r=[pk], w=['tio'])
        S.dma('sp', dst_dram, self.tio[0:ncol, :], r=['tio'], out=True)

    def load_tokens(self, src_dram, nrows, dst3, dstk):
        S = self.S
        S.dma('sp', self.tio[0:nrows, :], src_dram, w=['tio'])
        for kc in range(KC):
            pt, pk = self.ps()
            S.op('pe', lambda e, kc=kc, pt=pt: e.transpose(pt[:, 0:nrows], self.tio[0:nrows, kc * 128:(kc + 1) * 128], self.cs('ident', nrows, nrows)),
                 r=['tio', 'cst'], w=[pk])
            S.op('act', lambda e, kc=kc, pt=pt: e.activation(dst3[:, kc, :], pt[:, 0:nrows], AF.Copy), r=[pk], w=[dstk])

    def proj_feat(self, wv, wk, n, evac):
        S = self.S
        for j in range(4):
            pt, pk = self.ps()
            for kc in range(KC):
                S.op('pe', lambda e, kc=kc, j=j, pt=pt: e.matmul(pt[:, :n], wv[:, kc, j * 128:(j + 1) * 128], self.hT[:, kc, :n],
                     start=(kc == 0), stop=(kc == KC - 1)), r=[wk, 'hT'], w=[pk])
            evac(j, pt, pk)

    def proj_tok(self, wv, wk, P, t4, ncols, c0=0):
        S = self.S
        pt, pk = self.ps()
        for kc in range(KC):
            S.op('pe', lambda e, kc=kc, pt=pt: e.matmul(pt[0:P, 0:ncols], self.hT[:, kc, t4 * P:(t4 + 1) * P], wv[:, kc, c0:c0 + ncols],
                 start=(kc == 0), stop=(kc == KC - 1)), r=[wk, 'hT'], w=[pk])
        return pt, pk

    def even_tile(self, e, kind, x, xk, n, ti):
        S, cfg = self.S, self.cfg
        P = 128 if kind == 'p' else cfg.ns
        NTK = n // P
        hT = self.hT
        self.rmsnorm(x, xk, n, 2 * e, hT[:, :, :n], 'hT')
        win = self.w_in[e]
        gT, qbT, mixT = self.gT, self.qbT, self.mixT
        if kind == 'p':
            KTv = self.KT[:, :, ti * n:(ti + 1) * n]
            ktk = 'KT'
            row0 = ti * n
        else:
            KTv = self.KTs[:, :, 0:n]
            ktk = 'KTs'
            row0 = 0
        fk = self.fk if kind == 'p' else self.fks
        fv = self.fv if kind == 'p' else self.fvs
        fl = self.fl if kind == 'p' else self.fls
        wv, wk = self.load_wcols(win, C_QB, 512)
        self.proj_feat(wv, wk, n, lambda j, pt, pk: S.op('act', lambda e_: e_.activation(qbT[:, j, :n], pt[:, :n], AF.Copy), r=[pk], w=['qbT']))
        if cfg.kstop <= 1:
            return
        if kind == 's' and 'past' in cfg.flags and 'fox' in cfg.flags:
            self.fence()
            self.fox_sample_past(e)
            self.fence()
        wv, wk = self.load_wcols(win, C_GA, 512)
        self.proj_feat(wv, wk, n, lambda j, pt, pk: S.op('act', lambda e_: e_.activation(gT[:, j, :n], pt[:, :n], AF.Silu), r=[pk], w=['gT']))
        if cfg.kstop <= 2:
            return
        S.op('dve', lambda e_: e_.memset(self.raug[0:32, :], 1.0), w=['raug'])
        pt, pk = self.ps()
        for kc in range(KC):
            S.op('pe', lambda e_, kc=kc, pt=pt: e_.matmul(pt[0:16, :n], self.wsmall[:, kc, 0:16], hT[:, kc, :n], start=(kc == 0), stop=(kc == KC - 1)),
                 r=['wsmall', 'hT'], w=[pk])
        S.op('act', lambda e_, pt=pt: e_.activation(self.raug[0:16, :n], pt[0:16, :n], AF.Copy), r=[pk], w=['raug'])
        wv, wk = self.load_wcols(win, C_KB, 512)
        self.proj_feat(wv, wk, n, lambda j, pt, pk: S.op('act', lambda e_: e_.activation(KTv[:, j, :], pt[:, :n], AF.Copy), r=[pk], w=[ktk]))
        for t4 in range(NTK):
            pt, pk = self.proj_tok(wv, wk, P, t4, 512)
            S.op('act', lambda e_, pt=pt: e_.activation(self.stg[0:P, :], pt[0:P, :], AF.Copy), r=[pk], w=['g_oa'])
            S.dma('sp', fk[e, row0 + t4 * P:row0 + (t4 + 1) * P, :], self.stg[0:P, :], r=['g_oa'], out=True)
        if cfg.kstop <= 3:
            return
        wv, wk = self.load_wcols(win, C_VB, 512)
        for t4 in range(NTK):
            pt, pk = self.proj_tok(wv, wk, P, t4, 512)
            S.op('act', lambda e_, pt=pt: e_.activation(self.stg[0:P, :], pt[0:P, :], AF.Copy), r=[pk], w=['g_oa'])
            S.dma('sp', fv[e, row0 + t4 * P:row0 + (t4 + 1) * P, :], self.stg[0:P, :], r=['g_oa'], out=True)
            if kind == 'p':
                vdst, vk = self.Vst[:, (row0 // 128) + t4, :], 'Vst'
            else:
                vdst, vk = self.Vs[0:P, :], 'Vs'
            S.op('dve', lambda e_, pt=pt, vdst=vdst: e_.tensor_copy(vdst, pt[0:P, :]), r=[pk], w=[vk])
        if cfg.kstop <= 4:
            return
        wv, wk = self.load_wcols(win, C_QKA, 512)
        for t4 in range(NTK):
            pt, pk = self.proj_tok(wv, wk, P, t4, 512)
            S.op('act', lambda e_, pt=pt, t4=t4: e_.activation(self.qk_tok[0:P, t4, :], pt[0:P, :], AF.Copy), r=[pk], w=['qk_tok'])
        wv, wk = self.load_wcols(win, C_VA, 512)
        for t4 in range(NTK):
            pt, pk = self.proj_tok(wv, wk, P, t4, 512)
            S.op('act', lambda e_, pt=pt, t4=t4: e_.activation(self.va_tok[0:P, t4, :], pt[0:P, :], AF.Copy), r=[pk], w=['va_tok'])
        if cfg.kstop <= 5:
            return
        if kind == 'p':
            S.op('dve', lambda e_: e_.tensor_copy(self.Fref[:], self.carry[:]), r=['carry'], w=['Fref'])
        for t4 in range(NTK):
            pf, pfk = self.ps()
            for kc in range(KC):
                S.op('pe', lambda e_, kc=kc, pf=pf, t4=t4: e_.matmul(pf[0:P, 0:4], hT[:, kc, t4 * P:(t4 + 1) * P], self.wsmall[:, kc, 16:20],
                     start=(kc == 0), stop=(kc == KC - 1)), r=['wsmall', 'hT'], w=[pfk])
            lf = self.lf[0:P, t4, :]
            S.op('dve', lambda e_, pf=pf, lf=lf: e_.tensor_tensor(lf, pf[0:P, 0:4], self.bfg[0:P, :], ALU.add), r=[pfk, 'bfg'], w=['lf'])
            S.op('act', lambda e_, lf=lf: e_.activation(lf, lf, AF.Exp, scale=-1.0), r=['lf'], w=['lf'])
            S.op('act', lambda e_, lf=lf: e_.activation(lf, lf, AF.Ln, bias=1.0), r=['lf'], w=['lf'])
            S.op('dve', lambda e_, lf=lf: e_.tensor_scalar(lf, lf, -1.0, None, ALU.mult), r=['lf'], w=['lf'])
            S.dma('sp', fl[e, row0 + t4 * P:row0 + (t4 + 1) * P, :], lf, r=['lf'], out=True)
            pF, pFk = self.ps()
            if kind == 'p':
                tile_idx = row0 // 128 + t4
                S.op('pe', lambda e_, pF=pF, lf=lf: e_.matmul(pF[:, 0:4], self.cs('triU'), lf, start=True, stop=True), r=['lf', 'cst'], w=[pFk])
                S.op('pe', lambda e_, pF=pF, lf=lf: e_.matmul(pF[:, 4:8], self.ones_f[:], lf, start=True, stop=True), r=['lf', 'ones'], w=[pFk])
                S.op('dve', lambda e_, pF=pF, tile_idx=tile_idx: e_.tensor_tensor(self.Fcol[:, tile_idx, :], pF[:, 0:4], self.carry[:], ALU.add),
                     r=[pFk, 'carry'], w=['Fcol'])
                S.op('dve', lambda e_, pF=pF: e_.tensor_tensor(self.carry[:], self.carry[:], pF[:, 4:8], ALU.add), r=[pFk, 'carry'], w=['carry'])
            else:
                S.op('pe', lambda e_, pF=pF, lf=lf: e_.matmul(pF[0:P, 0:4], self.cs('maskCs', P), lf, start=True, stop=True), r=['lf', 'cst'], w=[pFk])
                S.op('dve', lambda e_, pF=pF: e_.tensor_scalar(self.Fn[0:P, :], pF[0:P, 0:4], -1.0, -SHIFT_C, ALU.mult, ALU.add), r=[pFk], w=['Fn'])
        if cfg.kstop <= 6:
            return
        fl_ = cfg.flags
        if 'gla' not in fl_ or 'fox' not in fl_:
            S.op('dve', lambda e_: e_.memset(mixT[:, :, :n], 0.0), w=['mixT'])
        if 'gla' in fl_:
            for t4 in range(NTK):
                self.gla_tile(e, kind, t4, n, ti)
        if 'fox' in fl_:
            if kind == 'p':
                self.fox_prompt(e, ti, n)
            elif 'past' in fl_:
                self.fox_sample(e)
        for blk in range(2):
            wv, wk = self.load_wcols(self.w_out[e], blk * 512, 512)
            for j in range(4):
                dc = blk * 4 + j
                pt, pk = self.ps()
                for mc in range(KC):
                    S.op('pe', lambda e_, mc=mc, j=j, pt=pt, wv=wv: e_.matmul(pt[:, :n], wv[:, mc, j * 128:(j + 1) * 128], mixT[:, mc, :n],
                         start=(mc == 0), stop=(mc == KC - 1)), r=[wk, 'mixT'], w=[pk])
                S.op('dve', lambda e_, dc=dc, pt=pt: e_.tensor_tensor(x[:, dc, :], x[:, dc, :], pt[:, :n], ALU.add), r=[pk, xk], w=[xk])

    def gla_tile(self, e, kind, t4, n, ti):
        S, cfg = self.S, self.cfg
        if kind == 'p':
            P, Cs, nch = 128, 64, 2
            triC, onesC, maskC, csel, sel01 = (self.cs(k) for k in ('triC', 'onesC', 'maskC', 'csel', 'sel01'))
        else:
            P, Cs, nch = 16, 4, 4
            triC, onesC, maskC, csel, sel01 = (self.cs(k, 16) for k in ('triCs', 'onesCs', 'maskCs', 'csels', 'sel01s'))
        cols = slice(t4 * P, (t4 + 1) * P)
        qk = self.qk_tok[0:P, t4, :]
        va = self.va_tok[0:P, t4, :]
        sp, bsb, eb, enb, edb, qe, ke, kd, kdm = (t[0:P, :] for t in (self.g_sp, self.g_b, self.g_eb, self.g_enb, self.g_edb, self.g_qe, self.g_ke, self.g_kd, self.g_kdm))
        Sst, Sbf, ebl = self.Sst, self.Sbf, self.ebl
        pt, pk = self.ps()
        S.op('pe', lambda e_, pt=pt: e_.matmul(pt[0:P, 0:256], self.raug[0:17, cols], self.wg_aug[0:17, :], start=True, stop=True), r=['raug', 'wg'], w=[pk])
        S.op('act', lambda e_, pt=pt: e_.activation(sp, pt[0:P, 0:256], AF.Exp, scale=-1.0), r=[pk], w=['g_sp'])
        S.op('act', lambda e_: e_.activation(sp, sp, AF.Ln, bias=1.0), r=['g_sp'], w=['g_sp'])
        p2, p2k = self.ps()
        S.op('pe', lambda e_, p2=p2: e_.matmul(p2[0:P, 0:256], triC, sp, start=True, stop=True), r=['g_sp', 'cst'], w=[p2k])
        S.op('pe', lambda e_, p2=p2: e_.matmul(p2[0:P, 256:512], onesC, sp, start=True, stop=True), r=['g_sp', 'cst'], w=[p2k])
        S.op('act', lambda e_, p2=p2: e_.activation(bsb, p2[0:P, 0:256], AF.Copy), r=[p2k], w=['g_b'])
        S.op('act', lambda e_: e_.activation(eb, bsb, AF.Exp), r=['g_b'], w=['g_eb'])
        S.op('act', lambda e_: e_.activation(enb, bsb, AF.Exp, scale=-1.0), r=['g_b'], w=['g_enb'])
        S.op('dve', lambda e_, p2=p2: e_.tensor_tensor(edb, p2[0:P, 256:512], bsb, ALU.subtract), r=[p2k, 'g_b'], w=['g_edb'])
        S.op('act', lambda e_: e_.activation(edb, edb, AF.Exp), r=['g_edb'], w=['g_edb'])
        S.op('dve', lambda e_: e_.scalar_tensor_tensor(qe, qk[:, 0:256], 0.125, eb, ALU.mult, ALU.mult), r=['qk_tok', 'g_eb'], w=['g_qe'])
        S.op('dve', lambda e_: e_.tensor_tensor(ke, qk[:, 256:512], enb, ALU.mult), r=['qk_tok', 'g_enb'], w=['g_ke'])
        S.op('dve', lambda e_: e_.tensor_tensor(kd, qk[:, 256:512], edb, ALU.mult), r=['qk_tok', 'g_edb'], w=['g_kd'])
        idn = self.cs('ident', P, P)
        for src, srck, dst, dstk in ((qe, 'g_qe', self.qeT, 'qeT'), (ke, 'g_ke', self.keT, 'keT')):
            p3, p3k = self.ps()
            for h in range(4):
                S.op('pe', lambda e_, h=h, p3=p3, src=src: e_.transpose(p3[0:64, h * P:(h + 1) * P], src[:, h * 64:(h + 1) * 64], idn), r=[srck, 'cst'], w=[p3k])
            S.op('act', lambda e_, p3=p3, dst=dst: e_.activation(dst[:, :, 0:P], p3[0:64, 0:4 * P].rearrange("p (h t) -> p h t", h=4), AF.Copy), r=[p3k], w=[dstk])
        qeT, keT = self.qeT, self.keT
        p5, p5k = self.ps()
        for h in range(4):
            S.op('pe', lambda e_, h=h, p5=p5: e_.matmul(p5[0:P, h * P:(h + 1) * P], keT[:, h, 0:P], qeT[:, h, 0:P], start=True, stop=True), r=['qeT', 'keT'], w=[p5k])
        attnT = self.attnT
        S.op('dve', lambda e_, p5=p5: e_.tensor_tensor(attnT[0:P, :, 0:P], p5[0:P, 0:4 * P].rearrange("p (h t) -> p h t", h=4),
             maskC.unsqueeze(1).to_broadcast([P, 4, P]), ALU.mult), r=[p5k, 'cst'], w=['attnT'])
        p6, p6k = self.ps()
        for h in range(4):
            S.op('pe', lambda e_, h=h, p6=p6: e_.matmul(p6[0:64, h * nch:(h + 1) * nch], sp[:, h * 64:(h + 1) * 64], csel[:, 0:nch], start=True, stop=True),
                 r=['g_sp', 'cst'], w=[p6k])
        S.op('act', lambda e_, p6=p6: e_.activation(ebl[:, 0:4 * nch], p6[0:64, 0:4 * nch], AF.Exp), r=[p6k], w=['ebl'])
        p7, p7k = self.ps()
        for h in range(4):
            S.op('pe', lambda e_, h=h, p7=p7: e_.matmul(p7[:, h * P:(h + 1) * P], va[:, h * 128:(h + 1) * 128], attnT[0:P, h, 0:P], start=True, stop=True),
                 r=['va_tok', 'attnT'], w=[p7k])
        p8, p8k = self.ps()
        for c in range(nch):
            if kind == 's':
                S.dma('sp', Sst[:], self.sgla[e, c].rearrange("h k v -> k h v"), w=['Sst'])
                S.op('act', lambda e_: e_.activation(Sbf[:], Sst[:], AF.Copy), r=['Sst'], w=['Sbf'])
            for h in range(4):
                S.op('pe', lambda e_, h=h, c=c, p8=p8: e_.matmul(p8[:, h * P + c * Cs:h * P + (c + 1) * Cs], Sbf[:, h, :], qeT[:, h, c * Cs:(c + 1) * Cs],
                     start=True, stop=True), r=['Sbf', 'qeT'], w=[p8k])
            S.op('dve', lambda e_, c=c: e_.tensor_scalar(kdm, kd, sel01[:, c:c + 1], None, ALU.mult), r=['g_kd', 'cst'], w=['g_kdm'])
            p9, p9k = self.ps()
            for h in range(4):
                S.op('pe', lambda e_, h=h, p9=p9: e_.matmul(p9[0:64, h * 128:(h + 1) * 128], kdm[:, h * 64:(h + 1) * 64], va[:, h * 128:(h + 1) * 128],
                     start=True, stop=True), r=['g_kdm', 'va_tok'], w=[p9k])
            for h in range(4):
                S.op('dve', lambda e_, h=h, c=c, p9=p9: e_.scalar_tensor_tensor(Sst[:, h, :], Sst[:, h, :], ebl[:, h * nch + c:h * nch + c + 1],
                     p9[0:64, h * 128:(h + 1) * 128], ALU.mult, ALU.add), r=[p9k, 'Sst', 'ebl'], w=['Sst'])
            S.op('act', lambda e_: e_.activation(Sbf[:], Sst[:], AF.Copy), r=['Sst'], w=['Sbf'])
            if kind == 's':
                S.dma('sp', self.gs[e, c].rearrange("h k v -> k h v"), Sst[:], r=['Sst'], out=True)
        if kind == 'p' and ti == cfg.ntt - 1 and t4 == n // P - 1:
            S.dma('sp', self.gp[e].rearrange("h k v -> k h v"), Sst[:], r=['Sst'], out=True)
        inter, oa, sq2, rs2 = self.g_inter, self.g_oa, self.g_sq2, self.g_rs2
        W4 = 4 * P
        S.op('act', lambda e_, p8=p8: e_.activation(inter[:, 0:W4], p8[:, 0:W4], AF.Copy), r=[p8k], w=['g_inter'])
        S.op('dve', lambda e_, p7=p7: e_.tensor_tensor(oa[:, 0:W4], p7[:, 0:W4], inter[:, 0:W4], ALU.add), r=[p7k, 'g_inter'], w=['g_oa'])
        S.op('act', lambda e_: e_.activation(sq2[:, 0:W4], oa[:, 0:W4], AF.Square), r=['g_oa'], w=['g_sq2'])
        p10, p10k = self.ps()
        S.op('pe', lambda e_, p10=p10: e_.matmul(p10[:, 0:W4], self.ones_bf[:], sq2[:, 0:W4], start=True, stop=True), r=['g_sq2', 'ones'], w=[p10k])
        S.op('act', lambda e_, p10=p10: e_.activation(rs2[:, 0:W4], p10[:, 0:W4], AF.Ln, scale=1.0 / 128, bias=EPS), r=[p10k], w=['g_rs2'])
        S.op('act', lambda e_: e_.activation(rs2[:, 0:W4], rs2[:, 0:W4], AF.Exp, scale=-0.5), r=['g_rs2'], w=['g_rs2'])
        S.op('dve', lambda e_: e_.tensor_tensor(oa[:, 0:W4], oa[:, 0:W4], rs2[:, 0:W4], ALU.mult), r=['g_oa', 'g_rs2'], w=['g_oa'])
        gn = self.vecs_sb[:, self.NV - 2 + e:self.NV - 1 + e]
        S.op('dve', lambda e_: e_.scalar_tensor_tensor(self.mixT[:, 0:4, cols], oa[:, 0:W4].rearrange("p (h t) -> p h t", h=4), gn,
             self.gT[:, :, cols], ALU.mult, ALU.mult), r=['g_oa', 'gT', 'vecs'], w=['mixT'])

    def fox_prompt(self, e, ti, n):
        S = self.S
        scale = 128.0 ** -0.5
        njt = (ti + 1) * n // 128
        bT = self.biasT
        S.op('dve', lambda e_: e_.tensor_tensor(bT[:, 0:njt, :], self.Fref[:].unsqueeze(1).to_broadcast([128, njt, 4]), self.Fcol[:, 0:njt, :], ALU.subtract),
             r=['Fref', 'Fcol'], w=['biasT'])
        S.op('dve', lambda e_: e_.tensor_scalar(bT[:, 0:njt, :], bT[:, 0:njt, :], -SHIFT_C, None, ALU.add), r=['biasT'], w=['biasT'])
        for h in range(4):
            po, pok = self.pst[6], 'ps6'
            pl, plk = self.pst[7], 'ps7'
            for j in range(njt):
                c0 = max(0, j * 128 - ti * n)
                diag = j * 128 >= ti * n
                pt, pk = self.ps()
                PT, PTk = (self.PTa, 'PTa') if j % 2 == 0 else (self.PTb, 'PTb')
                S.op('pe', lambda e_, h=h, j=j, c0=c0, pt=pt: e_.matmul(pt[:, c0:n], self.KT[:, h, j * 128:(j + 1) * 128], self.qbT[:, h, c0:n], start=True, stop=True),
                     r=['KT', 'qbT'], w=[pk])
                S.op('act', lambda e_, h=h, j=j, c0=c0, pt=pt, PT=PT: e_.activation(PT[:, c0:n], pt[:, c0:n], AF.Exp, bias=bT[:, j, h:h + 1], scale=scale),
                     r=[pk, 'biasT'], w=[PTk])
                if diag:
                    S.op('dve', lambda e_, c0=c0, PT=PT: e_.tensor_tensor(PT[:, c0:c0 + 128], PT[:, c0:c0 + 128], self.triU_bf[:], ALU.mult), r=[PTk, 'triUbf'], w=[PTk])
                last = (j == njt - 1)
                S.op('pe', lambda e_, j=j, c0=c0, pl=pl, PT=PT, last=last: e_.matmul(pl[:, c0:n], self.ones_bf[:], PT[:, c0:n], start=(j == 0), stop=last),
                     r=[PTk, 'ones'], w=[plk])
                S.op('pe', lambda e_, h=h, j=j, c0=c0, po=po, PT=PT, last=last: e_.matmul(po[:, c0:n], self.Vst[:, j, h * 128:(h + 1) * 128], PT[:, c0:n], start=(j == 0), stop=last),
                     r=[PTk, 'Vst'], w=[pok])
            S.op('dve', lambda e_, pl=pl: e_.reciprocal(self.rl[:, :n], pl[:, :n]), r=[plk], w=['rl'])
            S.op('dve', lambda e_, h=h, po=po: e_.tensor_tensor(self.mixT[:, 4 + h, :n], po[:, :n], self.rl[:, :n], ALU.mult), r=[pok, 'rl'], w=['mixT'])

    def fox_sample_past(self, e):
        S, cfg = self.S, self.cfg
        NPG, NB = cfg.npg, cfg.nbs
        scale = 128.0 ** -0.5
        G = 2
        psO, psOk = self.pst[6], 'ps6'
        psL, psLk = self.pst[7], 'ps7'
        lfpg, sa, sb_, RHs, RHc, s_sb, PTs = self.s_lfpg, self.s_sa, self.s_sb, self.s_RHs, self.s_RHc, self.s_ssb, self.s_PTs
        for b in range(NB):
            S.op('pool', lambda e_, b=b: e_.indirect_dma_start(out=lfpg[0:NPG, :], out_offset=None, in_=self.lfp[e],
                 in_offset=bass.IndirectOffsetOnAxis(ap=self.idx_pg[0:NPG, b:b + 1], axis=0)), r=['idx_pg'], w=['s_lfpg'], dma=True)
            X, Xk = lfpg, 's_lfpg'
            bufs = [(sa, 's_sa'), (sb_, 's_sb')]
            for st in range(7):
                sh = 1 << st
                Y, Yk = bufs[st % 2]
                xv = X[0:NPG, :].rearrange("p (r h) -> p r h", h=4)
                yv = Y[0:NPG, :].rearrange("p (r h) -> p r h", h=4)
                S.op('dve', lambda e_, xv=xv, yv=yv, sh=sh: e_.tensor_tensor(yv[:, 0:128 - sh, :], xv[:, 0:128 - sh, :], xv[:, sh:128, :], ALU.add), r=[Xk], w=[Yk])
                S.op('dve', lambda e_, xv=xv, yv=yv, sh=sh: e_.tensor_copy(yv[:, 128 - sh:128, :], xv[:, 128 - sh:128, :]), r=[Xk], w=[Yk])
                X, Xk = Y, Yk
            pl, plk = self.ps()
            S.op('pe', lambda e_, pl=pl, X=X: e_.matmul(pl[0:NPG, 0:4], self.cs('sufU', NPG, NPG), X[0:NPG, 0:4], start=True, stop=True), r=[Xk, 'cst'], w=[plk])
            S.op('act', lambda e_, pl=pl: e_.activation(self.s_later[0:NPG, :], pl[0:NPG, 0:4], AF.Copy), r=[plk], w=['s_later'])
            S.op('dve', lambda e_, X=X: e_.tensor_tensor(RHs[0:NPG, :], X[0:NPG, :], lfpg[0:NPG, :], ALU.subtract), r=[Xk, 's_lfpg'], w=['s_RHs'])
            rv = RHs[0:NPG, :].rearrange("p (r h) -> p r h", h=4)
            S.op('dve', lambda e_, rv=rv: e_.tensor_tensor(rv, rv, self.s_later[0:NPG, :].unsqueeze(1).to_broadcast([NPG, 128, 4]), ALU.add),
                 r=['s_RHs', 's_later'], w=['s_RHs'])
            pr, prk = self.ps()
            for h in range(4):
                S.op('pe', lambda e_, h=h, pr=pr, rv=rv: e_.transpose(pr[:, h * NPG:(h + 1) * NPG], rv[:, :, h], self.cs('ident', NPG, NPG)), r=['s_RHs', 'cst'], w=[prk])
            S.op('act', lambda e_, pr=pr: e_.activation(RHc[:, 0:4 * NPG], pr[:, 0:4 * NPG], AF.Copy), r=[prk], w=['s_RHc'])
            psa, psak = self.ps()
            psb, psbk = self.ps()
            for s0 in range(0, NPG, G):
                for g in range(G):
                    slot = s0 + g
                    S.op('pool', lambda e_, g=g, slot=slot, b=b: e_.indirect_dma_start(out=self.s_Kst[:, g, :], out_offset=None, in_=self.ktp[e],
                         in_offset=bass.IndirectOffsetOnAxis(ap=self.idx[:, b * NPG + slot:b * NPG + slot + 1], axis=0)), r=['idx'], w=['s_Kst%d' % g], dma=True)
                    eng = 'act' if g % 2 == 0 else 'dve'
                    if eng == 'act':
                        S.op('act', lambda e_, g=g: e_.activation(self.s_Kbf[:, g, :], self.s_Kst[:, g, :], AF.Copy), r=['s_Kst%d' % g], w=['s_Kbf%d' % g])
                    else:
                        S.op('dve', lambda e_, g=g: e_.tensor_copy(self.s_Kbf[:, g, :], self.s_Kst[:, g, :]), r=['s_Kst%d' % g], w=['s_Kbf%d' % g])
                    for h in range(4):
                        pp, ppk = (psa, psak) if h < 2 else (psb, psbk)
                        c0 = ((h % 2) * NPG + slot) * 4
                        S.op('pe', lambda e_, g=g, h=h, b=b, pp=pp, c0=c0: e_.matmul(pp[:, c0:c0 + 4], self.s_Kbf[:, g, h * 128:(h + 1) * 128], self.qbT[:, h, b * 4:(b + 1) * 4],
                             start=True, stop=True), r=['s_Kbf%d' % g, 'qbT'], w=[ppk])
            for h in range(4):
                pp, ppk = (psa, psak) if h < 2 else (psb, psbk)
                c0 = (h % 2) * NPG * 4
                S.op('dve', lambda e_, h=h, pp=pp, c0=c0: e_.scalar_tensor_tensor(s_sb[:, 0:NPG * 4].rearrange("p (s q) -> p s q", q=4),
                     pp[:, c0:c0 + NPG * 4].rearrange("p (s q) -> p s q", q=4), scale,
                     RHc[:, h * NPG:(h + 1) * NPG].unsqueeze(2).to_broadcast([128, NPG, 4]), ALU.mult, ALU.add), r=[ppk, 's_RHc'], w=['s_ssb'])
                S.op('act', lambda e_, h=h: e_.activation(PTs[:, h, 0:NPG * 4], s_sb[:, 0:NPG * 4], AF.Exp, bias=-SHIFT_C), r=['s_ssb'], w=['s_PTs'])
            for s0 in range(0, NPG, G):
                for g in range(G):
                    slot = s0 + g
                    S.op('pool', lambda e_, g=g, slot=slot, b=b: e_.indirect_dma_start(out=self.s_Vst[:, g, :], out_offset=None, in_=self.vp[e],
                         in_offset=bass.IndirectOffsetOnAxis(ap=self.idx[:, b * NPG + slot:b * NPG + slot + 1], axis=0)), r=['idx'], w=['s_Vst%d' % g], dma=True)
                    if g % 2 == 0:
                        S.op('act', lambda e_, g=g: e_.activation(self.s_Vbf[:, g, :], self.s_Vst[:, g, :], AF.Copy), r=['s_Vst%d' % g], w=['s_Vbf%d' % g])
                    else:
                        S.op('dve', lambda e_, g=g: e_.tensor_copy(self.s_Vbf[:, g, :], self.s_Vst[:, g, :]), r=['s_Vst%d' % g], w=['s_Vbf%d' % g])
                    for h in range(4):
                        c0 = (h * 4 + b) * 4
                        S.op('pe', lambda e_, g=g, h=h, slot=slot, c0=c0, b=b: e_.matmul(psO[:, c0:c0 + 4], self.s_Vbf[:, g, h * 128:(h + 1) * 128], PTs[:, h, slot * 4:(slot + 1) * 4],
                             start=(slot == 0 and h == 0 and b == 0), stop=(slot == NPG - 1 and h == 3 and b == NB - 1)), r=['s_Vbf%d' % g, 's_PTs'], w=[psOk])
                        S.op('pe', lambda e_, h=h, slot=slot, c0=c0, b=b: e_.matmul(psL[:, c0:c0 + 4], self.ones_bf[:], PTs[:, h, slot * 4:(slot + 1) * 4],
                             start=(slot == 0 and h == 0 and b == 0), stop=(slot == NPG - 1 and h == 3 and b == NB - 1)), r=['s_PTs', 'ones'], w=[psLk])
        S.op('act', lambda e_: e_.activation(self.Opast[:], psO[:, 0:64], AF.Copy), r=[psOk], w=['Opast'])
        S.op('act', lambda e_: e_.activation(self.Lpast[:], psL[:, 0:64], AF.Copy), r=[psLk], w=['Lpast'])

    def fox_sample(self, e):
        S, cfg = self.S, self.cfg
        scale = 128.0 ** -0.5
        NSK = cfg.ns
        ptn, ptnk = self.ps()
        for h in range(4):
            S.op('pe', lambda e_, h=h: e_.matmul(ptn[0:NSK, h * NSK:(h + 1) * NSK], self.KTs[:, h, 0:NSK], self.qbT[:, h, 0:NSK], start=True, stop=True),
                 r=['KTs', 'qbT'], w=[ptnk])
        Pnf, Pn = self.Pnf, self.Pn
        for h in range(4):
            S.op('act', lambda e_, h=h: e_.activation(Pnf[0:NSK, h, :], ptn[0:NSK, h * NSK:(h + 1) * NSK], AF.Exp, bias=self.Fn[0:NSK, h:h + 1], scale=scale),
                 r=[ptnk, 'Fn'], w=['Pnf'])
        S.op('dve', lambda e_: e_.tensor_tensor(Pn[0:NSK, :, :], Pnf[0:NSK, :, :], self.cs('maskCs', NSK).unsqueeze(1).to_broadcast([NSK, 4, NSK]), ALU.mult),
             r=['Pnf', 'cst'], w=['Pn'])
        pon, ponk = self.ps()
        pln, plnk = self.ps()
        for h in range(4):
            S.op('pe', lambda e_, h=h: e_.matmul(pon[:, h * NSK:(h + 1) * NSK], self.Vs[0:NSK, h * 128:(h + 1) * 128], Pn[0:NSK, h, :], start=True, stop=True),
                 r=['Vs', 'Pn'], w=[ponk])
            S.op('pe', lambda e_, h=h: e_.matmul(pln[:, h * NSK:(h + 1) * NSK], self.ones_bf[0:NSK, :], Pn[0:NSK, h, :], start=True, stop=True),
                 r=['ones', 'Pn'], w=[plnk])
        Ot, Lt = self.Ot, self.Lt
        S.op('dve', lambda e_: e_.tensor_tensor(Ot[:], pon[:, 0:64], self.Opast[:], ALU.add), r=[ponk, 'Opast'], w=['Ot'])
        S.op('dve', lambda e_: e_.tensor_tensor(Lt[:], pln[:, 0:64], self.Lpast[:], ALU.add), r=[plnk, 'Lpast'], w=['Lt'])
        S.op('dve', lambda e_: e_.reciprocal(Lt[:], Lt[:]), r=['Lt'], w=['Lt'])
        S.op('dve', lambda e_: e_.tensor_tensor(self.mixT[:, 4:8, 0:NSK], Ot[:].rearrange("p (h t) -> p h t", h=4), Lt[:].rearrange("p (h t) -> p h t", h=4), ALU.mult),
             r=['Ot', 'Lt'], w=['mixT'])

    def build(self):
        import contextlib
        nc, S, cfg = self.nc, self.S, self.cfg
        T, L, TT, NS, NB, NPG = cfg.seq, cfg.depth, cfg.tt, cfg.ns, cfg.nbs, cfg.npg
        di = lambda n, s, d=F32: nc.dram_tensor(n, s, d, kind="ExternalInput").ap()
        self.lfp = [di("lfp%d" % i, [cfg.npool, 512]) for i in range(2)]
        self.vp = [di("vp%d" % i, [cfg.npool * 128, 512]) for i in range(2)]
        self.ptabT = di("ptabT", [NPG, NB], I32)
        st = contextlib.ExitStack()
        sb = lambda name, shape, dt: st.enter_context(nc.sbuf_tensor(name, shape, dt))
        self.xT = sb("xT", [128, KC, T], F32); self.xsT = sb("xsT", [128, KC, NS], F32)
        self.hT = sb("hT", [128, KC, TT], BF16); self.sq = sb("sq", [128, KC, TT], BF16); self.rstd = sb("rstd", [128, TT], F32)
        self.wbufs = [sb("wbuf%d" % i, [128, 4096], BF16) for i in range(2)]
        self.cst_sb = sb("cst_sb", [128, CST_W], F32); self.vecs_sb = sb("vecs_sb", [128, self.NV], F32)
        self.ones_bf = sb("ones_bf", [128, 128], BF16); self.ones_f = sb("ones_f", [128, 128], F32); self.triU_bf = sb("triU_bf", [128, 128], BF16)
        self.KT = sb("KT", [128, 4, T], BF16); self.Vst = sb("Vst", [128, T // 128, 512], BF16)
        self.Fcol = sb("Fcol", [128, T // 128, 4], F32); self.carry = sb("carry", [128, 4], F32); self.Fref = sb("Fref", [128, 4], F32)
        self.bfg = sb("bfg", [128, 4], F32); self.wg_aug = sb("wg_aug", [32, 256], BF16); self.wsmall = sb("wsmall", [128, KC, 20], BF16)
        self.Sst = sb("Sst", [64, 4, 128], F32); self.Sbf = sb("Sbf", [64, 4, 128], BF16)
        self.KTs = sb("KTs", [128, 4, NS], BF16); self.Vs = sb("Vs", [NS, 512], BF16); self.Fn = sb("Fn", [NS, 4], F32)
        self.Opast = sb("Opast", [128, 64], F32); self.Lpast = sb("Lpast", [128, 64], F32); self.Ot = sb("Ot", [128, 64], F32); self.Lt = sb("Lt", [128, 64], F32)
        self.Pnf = sb("Pnf", [NS, 4, NS], F32); self.Pn = sb("Pn", [NS, 4, NS], BF16)
        self.idx = sb("idx", [128, NB * NPG], I32); self.idx_f = sb("idx_f", [128, NB * NPG], F32); self.idx_pg = sb("idx_pg", [NPG, NB], I32)
        self.ebl = sb("ebl", [64, 16], F32); self.lf = sb("lf", [128, max(1, TT // 128), 4], F32); self.biasT = sb("biasT", [128, T // 128, 4], F32)
        self.dummy = sb("dummy_t", [128, 1], F32)
        UFW = max(15 + TT, NB * 19)
        self.UW = 9216
        self.U = sb("U", [128, self.UW], F32)
        self.pst = [st.enter_context(nc.psum_tensor("ps%d" % i, [128, 512], F32)) for i in range(8)]
        self.uoff = 0
        self.aT = self.carve('aT', 16 * TT, BF16, "p (f t) -> p f t", f=FC)
        self.uF = self.carve('uF', 8 * UFW, F32, "p (k l) -> p k l", k=KC)
        self.wsA = self.carve('wsA', 2 * UFW, F32, "p (k l) -> p k l", k=2)
        self.wsB = self.carve('wsB', 2 * UFW, F32, "p (k l) -> p k l", k=2)
        self.tio = self.carve('tio', 1024, F32)
        self.relu = self.carve('relu', TT // 2, BF16)
        self.uoff = 0
        self.qbT = self.carve('qbT', 2 * TT, BF16, "p (h t) -> p h t", h=4)
        e_start = self.uoff
        NTK = max(1, TT // 128)
        self.qk_tok = self.carve('qk_tok', NTK * 512, F32, "p (t c) -> p t c", c=512)
        self.va_tok = self.carve('va_tok', NTK * 256, BF16, "p (t c) -> p t c", c=512)
        self.gT = self.carve('gT', 2 * TT, BF16, "p (h t) -> p h t", h=4)
        self.mixT = self.carve('mixT', 4 * TT, BF16, "p (h t) -> p h t", h=8)
        self.raug = self.carve('raug', TT // 2, BF16)
        self.g_sp, self.g_b, self.g_eb, self.g_enb, self.g_edb, self.g_qe, self.g_ke = (self.carve(k, 256, F32) for k in ('g_sp', 'g_b', 'g_eb', 'g_enb', 'g_edb', 'g_qe', 'g_ke'))
        self.g_kd = self.carve('g_kd', 128, BF16); self.g_kdm = self.carve('g_kdm', 128, BF16)
        self.attnT = self.carve('attnT', 256, BF16, "p (h t) -> p h t", h=4)
        self.qeT = self.carve('qeT', 256, BF16, "p (h t) -> p h t", h=4)[0:64]
        self.keT = self.carve('keT', 256, BF16, "p (h t) -> p h t", h=4)[0:64]
        self.g_inter = self.carve('g_inter', 512, F32); self.g_oa = self.carve('g_oa', 512, F32); self.g_rs2 = self.carve('g_rs2', 512, F32)
        self.stg = self.g_oa
        self.g_sq2 = self.carve('g_sq2', 256, BF16)
        self.PTa = self.carve('PTa', TT // 2, BF16); self.PTb = self.carve('PTb', TT // 2, BF16); self.rl = self.carve('rl', TT, F32)
        self.uoff = e_start
        self.s_lfpg = self.carve('s_lfpg', 512, F32); self.s_sa = self.carve('s_sa', 512, F32); self.s_sb = self.carve('s_sb', 512, F32)
        self.s_RHs = self.carve('s_RHs', 512, F32); self.s_RHc = self.carve('s_RHc', 4 * NPG, F32); self.s_ssb = self.carve('s_ssb', 4 * NPG, F32)
        self.s_PTs = self.carve('s_PTs', 8 * NPG, BF16, "p (h c) -> p h c", h=4)
        self.s_later = self.carve('s_later', 4, F32)
        self.s_Kst = self.carve('s_Kst', 1024, F32, "p (g c) -> p g c", g=2); self.s_Kbf = self.carve('s_Kbf', 512, BF16, "p (g c) -> p g c", g=2)
        self.s_Vst = self.carve('s_Vst', 1024, F32, "p (g c) -> p g c", g=2); self.s_Vbf = self.carve('s_Vbf', 512, BF16, "p (g c) -> p g c", g=2)
        for g in range(2):
            self.ukeys += ['s_Kst%d' % g, 's_Kbf%d' % g, 's_Vst%d' % g, 's_Vbf%d' % g]
        xT, xsT = self.xT, self.xsT
        S.dma('sp', self.cst_sb[:], self.cst, w=['cst'])
        S.dma('sp', self.vecs_sb[:], self.vecs, w=['vecs'])
        S.op('dve', lambda e: e.memset(self.ones_bf[:], 1.0), w=['ones'])
        S.op('dve', lambda e: e.memset(self.ones_f[:], 1.0), w=['ones'])
        S.op('dve', lambda e: e.tensor_copy(self.triU_bf[:], self.cs('triU')), r=['cst'], w=['triUbf'])
        S.dma('sp', self.idx_pg[:], self.ptabT, w=['idx_pg'])
        S.dma('sp', self.idx[:], self.ptab.partition_broadcast(128), w=['idx'])
        S.op('dve', lambda e: e.tensor_copy(self.idx_f[:], self.idx[:]), r=['idx'], w=['idx_f'])
        S.op('dve', lambda e: e.scalar_tensor_tensor(self.idx_f[:], self.idx_f[:], 128.0, self.cs('iota').to_broadcast([128, NB * NPG]), ALU.mult, ALU.add),
             r=['idx_f', 'cst'], w=['idx_f'])
        S.op('dve', lambda e: e.tensor_copy(self.idx[:], self.idx_f[:]), r=['idx_f'], w=['idx'])
        for tt in range(T // 128):
            self.load_tokens(self.xp[tt * 128:(tt + 1) * 128, :], 128, xT[:, :, tt * 128:(tt + 1) * 128], 'x%d' % (tt * 128 // TT))
        self.load_tokens(self.xs, NS, xsT[:, :, :], 'xs')
        xt = lambda ti: (xT[:, :, ti * TT:(ti + 1) * TT], 'x%d' % ti)
        for l in range(L):
            if l % 2 == 0 and 'noeven' in cfg.flags:
                for ti in range(cfg.ntt):
                    x, xk = xt(ti)
                    self.mlp(l, x, xk, TT)
                continue
            if l % 2 == 0:
                e = l // 2
                S.dma('pool', self.wsmall[:, :, 0:16], self.w_in[e].rearrange("(kc p) f -> p kc f", p=128)[:, :, C_RA:C_RA + 16], w=['wsmall'])
                S.dma('pool', self.wsmall[:, :, 16:20], self.w_in[e].rearrange("(kc p) f -> p kc f", p=128)[:, :, C_FB:C_FB + 4], w=['wsmall'])
                S.dma('pool', self.wg_aug[0:16, :], self.w_gate[e], w=['wg'])
                S.dma('pool', self.wg_aug[16:17, :], self.b_gate[e], w=['wg'])
                S.dma('sp', self.bfg[:], self.b_forget[e].partition_broadcast(128), w=['bfg'])
                S.op('dve', lambda e_: e_.memset(self.carry[:], 0.0), w=['carry'])
                S.op('dve', lambda e_: e_.memset(self.Sst[:], 0.0), w=['Sst'])
                S.op('dve', lambda e_: e_.memset(self.Sbf[:], 0.0), w=['Sbf'])
                for ti in range(cfg.ntt):
                    x, xk = xt(ti)
                    self.fence()
                    self.even_tile(e, 'p', x, xk, TT, ti)
                    self.fence()
                    self.mlp(l, x, xk, TT)
                if 'sample' in cfg.flags:
                    self.fence()
                    self.even_tile(e, 's', xsT[:, :, :], 'xs', NS, 0)
                    self.fence()
                    self.mlp(l, xsT[:, :, :], 'xs', NS)
            else:
                o = l // 2
                self.fence()
                S.op('dve', lambda e_: e_.memset(self.uF[:, :, 0:15], 0.0), w=['uF'])
                for ti in range(cfg.ntt):
                    x, xk = xt(ti)
                    self.pool_mix(o, x, xk, TT, first=(ti == 0))
                    if ti == cfg.ntt - 1:
                        self.transpose_out(self.uF[:, :, TT:TT + 15], 'uF', 15, self.npool[o], 15)
                    else:
                        S.op('dve', lambda e_: e_.tensor_copy(self.uF[:, :, 0:15], self.uF[:, :, TT:TT + 15]), r=['uF'], w=['uF'])
                    self.mlp(l, x, xk, TT)
                if 'sample' not in cfg.flags:
                    continue
                uvs = self.uF[:, :, 0:NB * 19].rearrange("p k (b l) -> p k b l", b=NB)
                S.dma('sp', self.tio[0:NB * 15, :], self.spool[o].rearrange("b r d -> (b r) d"), w=['tio'])
                for kc in range(KC):
                    pt, pk = self.ps()
                    S.op('pe', lambda e_, kc=kc, pt=pt: e_.transpose(pt[:, 0:NB * 15], self.tio[0:NB * 15, kc * 128:(kc + 1) * 128], self.cs('ident', NB * 15, NB * 15)),
                         r=['tio', 'cst'], w=[pk])
                    S.op('act', lambda e_, kc=kc, pt=pt: e_.activation(uvs[:, kc, :, 0:15], pt[:, 0:NB * 15].rearrange("p (b r) -> p b r", b=NB), AF.Copy), r=[pk], w=['uF'])
                S.dma('sp', self.nps[o, :, 0:11, :], self.spool[o, :, 4:15, :], out=True)
                self.pool_mix(o, xsT[:, :, :], 'xs', NS, first=False, nb=NB)
                tmp = self.wsA[:, :, :].rearrange("p k l -> p (k l)")[:, 0:KC * NS].rearrange("p (k t) -> p k t", k=KC)
                for kc in range(KC):
                    S.op('dve', lambda e_, kc=kc: e_.tensor_copy(tmp[:, kc, :].rearrange("p (b t) -> p b t", b=NB), uvs[:, kc, :, 15:19]), r=['uF'], w=['wsA'])
                for kc in range(KC):
                    pt, pk = self.ps()
                    S.op('pe', lambda e_, kc=kc, pt=pt: e_.transpose(pt[0:NS, 0:128], tmp[:, kc, :], self.cs('ident')), r=['wsA', 'cst'], w=[pk])
                    S.op('act', lambda e_, kc=kc, pt=pt: e_.activation(self.tio[0:NS, kc * 128:(kc + 1) * 128], pt[0:NS, 0:128], AF.Copy), r=[pk], w=['tio'])
                for b in range(NB):
                    S.dma('sp', self.nps[o, b, 11:15, :], self.tio[b * 4:(b + 1) * 4, :], r=['tio'], out=True)
                self.mlp(l, xsT[:, :, :], 'xs', NS)
        self.fence()
        for ti in range(cfg.ntt):
            x, xk = xt(ti)
            self.rmsnorm(x, xk, TT, 2 * L, self.uF[:, :, 0:TT], 'uF')
            for t4 in range(TT // 128):
                r0 = ti * TT + t4 * 128
                self.transpose_out(self.uF[:, :, t4 * 128:(t4 + 1) * 128], 'uF', 128, self.y[r0:r0 + 128, :], 128)
        self.rmsnorm(xsT[:, :, :], 'xs', NS, 2 * L, self.uF[:, :, 0:NS], 'uF')
        self.transpose_out(self.uF[:, :, 0:NS], 'uF', NS, self.ys, NS)
        S.emit()
        print("sbuf bytes remaining:", nc.sbuf_bytes_remaining, "ops:", len(S.ops))
        st.close()
        return nc


def shared_inputs(cfg, inp):
    L = cfg.depth
    f = lambda a: np.ascontiguousarray(np.asarray(a), dtype=np.float32)
    cols = [inp["norm_mix"][l] for l in range(L)] + [inp["norm_mlp"][l] for l in range(L)] + [inp["norm_final"]] + \
           [inp["pool_scale"][o] for o in range(2)]
    vecs = np.concatenate([np.asarray(c, np.float32).reshape(KC, 128).T for c in cols] + [np.asarray(inp["gla_norm"], np.float32).T], axis=1)
    ck = np.asarray(inp["cache_fox_k"], np.float32)
    npool = ck.shape[1]
    ktp = np.ascontiguousarray(ck.transpose(0, 1, 4, 3, 2)).reshape(2, npool * 128, 512)
    vp = f(inp["cache_fox_v"]).reshape(2, npool * 128, 512)
    lfp = f(inp["cache_fox_logf"]).reshape(2, npool, 512)
    return dict(cst=make_consts(), vecs=np.ascontiguousarray(vecs), w_in=f(inp["w_in_even"]), w_out=f(inp["w_out_even"]),
                w_gate=f(inp["w_gate_up"]), b_gate=f(inp["b_gate"]).reshape(2, 1, 256), b_forget=f(inp["b_forget"]).reshape(2, 1, 4),
                w_up=f(inp["w_mlp_up"]), w_down=f(inp["w_mlp_down"]), w_pool=f(inp["w_pool"]),
                ktp0=ktp[0], ktp1=ktp[1], vp0=vp[0], vp1=vp[1], lfp0=lfp[0], lfp1=lfp[1])


def core_inputs(cfg, c, inp, shared):
    NB = cfg.nbs
    f = lambda a: np.ascontiguousarray(np.asarray(a), dtype=np.float32)
    pt = np.asarray(inp["page_table"])[NB * c:NB * (c + 1)].astype(np.int32)
    m = dict(shared)
    m.update(xp=f(inp["x_prompt"][c]), xs=f(np.asarray(inp["x_sample"])[NB * c:NB * (c + 1)]).reshape(cfg.ns, D),
             ptab=np.ascontiguousarray(pt.reshape(1, -1)), ptabT=np.ascontiguousarray(pt.T),
             sgla=f(np.asarray(inp["state_gla"])[:, NB * c:NB * (c + 1)]), spool=f(np.asarray(inp["state_pool"])[:, NB * c:NB * (c + 1)]))
    return m


def assemble(cfg, results, B, Bs):
    NB = cfg.nbs
    T = cfg.seq
    r = results
    cat = lambda k, ax: np.concatenate([r[c][k] for c in range(len(r))], axis=ax)
    y_prompt = np.stack([r[c]["y"] for c in range(B)], 0)
    y_sample = np.concatenate([r[c]["ys"].reshape(NB, 4, D) for c in range(B)], 0)
    fk = np.stack([r[c]["fk"].reshape(2, T, 4, 128) for c in range(B)], 1)
    fv = np.stack([r[c]["fv"].reshape(2, T, 4, 128) for c in range(B)], 1)
    fl = np.stack([r[c]["fl"] for c in range(B)], 1)
    fks = np.concatenate([r[c]["fks"].reshape(2, NB, 4, 4, 128) for c in range(B)], 1)
    fvs = np.concatenate([r[c]["fvs"].reshape(2, NB, 4, 4, 128) for c in range(B)], 1)
    fls = np.concatenate([r[c]["fls"].reshape(2, NB, 4, 4) for c in range(B)], 1)
    gp = np.stack([r[c]["gp"] for c in range(B)], 1)
    gs = np.concatenate([r[c]["gs"] for c in range(B)], 1)
    npool = np.stack([r[c]["npool"] for c in range(B)], 1)
    nps = np.concatenate([r[c]["nps"] for c in range(B)], 1)
    return tuple(np.ascontiguousarray(a, dtype=np.float32) for a in (y_prompt, y_sample, fk, fv, fl, fks, fvs, fls, gp, gs, npool, nps))


_NC_CACHE = {}


def kernel(**inp):
    B, T, _ = np.asarray(inp["x_prompt"]).shape
    Bs = np.asarray(inp["x_sample"]).shape[0]
    npg = np.asarray(inp["page_table"]).shape[1]
    npool = np.asarray(inp["cache_fox_k"]).shape[1]
    cfg = Cfg(seq=T, depth=4, tt=256, npg=npg, npool=npool)
    key = (T, npg, npool)
    if key not in _NC_CACHE:
        _NC_CACHE[key] = Builder(cfg).build()
    nc = _NC_CACHE[key]
    shared = shared_inputs(cfg, inp)
    maps = [core_inputs(cfg, c, inp, shared) for c in range(B)]
    res = run_bass_kernel_spmd(nc, maps, core_ids=list(range(B)))
    return assemble(cfg, res.results, B, Bs)
```

```python
import numpy as np
import concourse.bass as bass
import concourse.mybir as mybir
from concourse.bass_utils import run_bass_kernel_spmd

F32 = mybir.dt.float32
BF16 = mybir.dt.bfloat16
I32 = mybir.dt.int32
AF = mybir.ActivationFunctionType
ALU = mybir.AluOpType


class Sched:
    ENGS = ('pe', 'act', 'dve', 'pool', 'sp')
    NDMA = 20

    def __init__(self, nc, same_engine_sync=True):
        self.nc = nc
        self.ops = []
        self.last_w = {}
        self.readers = {}
        self.same = same_engine_sync
        self.out_dmas = []
        self.dma_rr = {'sp': 0, 'pool': 0, 'act': 0}
        self.dma_last = {}

    def op(self, eng, fn, r=(), w=(), dma=False, out=False):
        idx = len(self.ops)
        w = list(w) + [k for k in r if isinstance(k, str) and k.startswith('ps') and k not in w]
        deps = set()
        for k in r:
            if k in self.last_w:
                deps.add(self.last_w[k])
        for k in w:
            if k in self.last_w:
                deps.add(self.last_w[k])
            deps |= self.readers.get(k, set())
        semslot = None
        if dma:
            slot = self.dma_rr[eng]
            self.dma_rr[eng] = (slot + 1) % self.NDMA
            semslot = (eng, slot)
            if semslot in self.dma_last:
                deps.add(self.dma_last[semslot])
            self.dma_last[semslot] = idx
        for k in r:
            self.readers.setdefault(k, set()).add(idx)
        for k in w:
            self.last_w[k] = idx
            self.readers[k] = set()
        self.ops.append(dict(eng=eng, fn=fn, deps=deps, dma=dma, semslot=semslot))
        if out:
            self.out_dmas.append(idx)
        return idx

    def dma(self, eng, out_ap, in_ap, r=(), w=(), out=False, **kw):
        return self.op(eng, lambda e: e.dma_start(out=out_ap, in_=in_ap, **kw), r=r, w=w, dma=True, out=out)

    def emit(self):
        nc = self.nc
        ops = self.ops
        self.ops.append(dict(eng='sp', fn=None, deps=set(self.out_dmas) | set(self.dma_last.values()), dma=False, semslot=None))
        import contextlib
        stack = contextlib.ExitStack()
        esem = {e: stack.enter_context(nc.semaphore("se_" + e)) for e in ('pe', 'act', 'dve', 'pool')}
        dsem = {}
        for e in ('sp', 'pool'):
            for s in range(self.NDMA):
                dsem[(e, s)] = stack.enter_context(nc.semaphore("sd_%s_%d" % (e, s)))
        ecount = {e: 0 for e in esem}
        dcount = {k: 0 for k in dsem}
        for o in ops:
            if o['dma']:
                dcount[o['semslot']] += 16
                o['sig'] = (dsem[o['semslot']], dcount[o['semslot']], o['semslot'])
            elif o['eng'] in esem and o['fn'] is not None:
                ecount[o['eng']] += 1
                o['sig'] = (esem[o['eng']], ecount[o['eng']], o['eng'])
            else:
                o['sig'] = None
        streams = {e: [] for e in self.ENGS}
        for i, o in enumerate(ops):
            streams[o['eng']].append(i)
        eobj = {'pe': 'tensor', 'act': 'scalar', 'dve': 'vector', 'pool': 'gpsimd', 'sp': 'sync'}

        def make(engname):
            def body(e):
                waited = {}
                for i in streams[engname]:
                    o = ops[i]
                    for d in sorted(o['deps']):
                        dd = ops[d]
                        if dd['sig'] is None:
                            continue
                        sem, val, key = dd['sig']
                        if (not dd['dma']) and dd['eng'] == engname and (engname == 'pe' or not self.same):
                            continue
                        if waited.get(key, 0) >= val:
                            continue
                        e.wait_ge(sem, val)
                        waited[key] = val
                    if o['fn'] is None:
                        continue
                    ins = o['fn'](e)
                    if o['sig'] is not None:
                        ins.then_inc(o['sig'][0], 16 if o['dma'] else 1)
            return body

        with nc.Block() as block:
            block.tensor(make('pe'))
            block.scalar(make('act'))
            block.vector(make('dve'))
            block.gpsimd(make('pool'))
            block.sync(make('sp'))
        stack.close()


D = 1024
DFF = 4096
KC = 8
FC = 32
EPS = 1e-6
NINC = 3092
C_QKA, C_VA, C_RA, C_GA, C_QB, C_KB, C_VB, C_FB = 0, 512, 1024, 1040, 1552, 2064, 2576, 3088
SHIFT_C = 8.0


class Cfg:
    def __init__(self, seq=2048, depth=4, tt=256, npg=64, npool=2560):
        self.seq = seq
        self.depth = depth
        self.tt = min(tt, seq)
        self.ntt = seq // self.tt
        self.npg = npg
        self.npool = npool
        self.nbs = 4
        import os
        self.flags = os.environ.get('KFLAGS', 'sample,gla,fox,past,pool,mlp')
        self.kstop = int(os.environ.get('KSTOP', '99'))
        self.ns = 16


def _cst_layout():
    names = [('ident', 128), ('triU', 128), ('sufU', 128), ('triC', 128), ('onesC', 128), ('maskC', 128),
             ('csel', 2), ('sel01', 2), ('triCs', 16), ('onesCs', 16), ('maskCs', 16), ('csels', 4), ('sel01s', 4),
             ('iota', 1), ('invc', 64)]
    off, o = {}, 0
    for n, w in names:
        off[n] = (o, w)
        o += w
    return off, o


CST_OFF, CST_W = _cst_layout()


def make_consts():
    c = np.zeros((128, CST_W), np.float32)
    def put(name, a):
        o, w = CST_OFF[name]
        c[:a.shape[0], o:o + a.shape[1]] = a
    i = np.arange(128)
    put('ident', np.eye(128, dtype=np.float32))
    put('triU', (i[:, None] <= i[None, :]).astype(np.float32))
    put('sufU', (i[:, None] > i[None, :]).astype(np.float32))
    same = (i[:, None] // 64) == (i[None, :] // 64)
    put('triC', np.where(same & (i[:, None] <= i[None, :]), -1.0 / 16, 0.0).astype(np.float32))
    put('onesC', np.where(same, -1.0 / 16, 0.0).astype(np.float32))
    put('maskC', (same & (i[:, None] <= i[None, :])).astype(np.float32))
    put('csel', np.where((i[:, None] // 64) == np.arange(2)[None, :], -1.0 / 16, 0.0).astype(np.float32))
    put('sel01', ((i[:, None] // 64) == np.arange(2)[None, :]).astype(np.float32))
    j = np.arange(16)
    sames = (j[:, None] // 4) == (j[None, :] // 4)
    put('triCs', np.where(sames & (j[:, None] <= j[None, :]), -1.0 / 16, 0.0).astype(np.float32))
    put('onesCs', np.where(sames, -1.0 / 16, 0.0).astype(np.float32))
    put('maskCs', (sames & (j[:, None] <= j[None, :])).astype(np.float32))
    put('csels', np.where((j[:, None] // 4) == np.arange(4)[None, :], -1.0 / 16, 0.0).astype(np.float32))
    put('sel01s', ((j[:, None] // 4) == np.arange(4)[None, :]).astype(np.float32))
    put('iota', i[:, None].astype(np.float32))
    pos = np.arange(16, dtype=np.float32) + 1.0
    inv = np.concatenate([1.0 / np.minimum(float(2 << g), pos) for g in range(4)])[None, :]
    put('invc', np.broadcast_to(inv, (128, 64)).astype(np.float32))
    return c


class Builder:
    def __init__(self, cfg):
        self.cfg = cfg
        nc = self.nc = bass.Bass("TRN2", target_bir_lowering=False)
        self.S = Sched(nc)
        T, L = cfg.seq, cfg.depth
        NS, NB, NPG = cfg.ns, cfg.nbs, cfg.npg
        di = lambda n, s, d=F32: nc.dram_tensor(n, s, d, kind="ExternalInput").ap()
        do = lambda n, s: nc.dram_tensor(n, s, F32, kind="ExternalOutput").ap()
        self.xp = di("xp", [T, D]); self.xs = di("xs", [NS, D])
        self.cst = di("cst", [128, CST_W])
        self.NV = (2 * L + 3) * KC + 2
        self.vecs = di("vecs", [128, self.NV])
        self.w_in = di("w_in", [2, D, NINC]); self.w_out = di("w_out", [2, D, D])
        self.w_gate = di("w_gate", [2, 16, 256]); self.b_gate = di("b_gate", [2, 1, 256]); self.b_forget = di("b_forget", [2, 1, 4])
        self.w_up = di("w_up", [L, D, DFF]); self.w_down = di("w_down", [L, DFF, D]); self.w_pool = di("w_pool", [2, 4, 256, 256])
        self.ktp = [di("ktp%d" % i, [cfg.npool * 128, 512]) for i in range(2)]
        self.ptab = di("ptab", [1, NB * NPG], I32)
        self.sgla = di("sgla", [2, NB, 4, 64, 128]); self.spool = di("spool", [2, NB, 15, D])
        self.y = do("y", [T, D]); self.ys = do("ys", [NS, D])
        self.fk = do("fk", [2, T, 512]); self.fv = do("fv", [2, T, 512]); self.fl = do("fl", [2, T, 4])
        self.fks = do("fks", [2, NS, 512]); self.fvs = do("fvs", [2, NS, 512]); self.fls = do("fls", [2, NS, 4])
        self.gp = do("gp", [2, 4, 64, 128]); self.gs = do("gs", [2, NB, 4, 64, 128])
        self.npool = do("npool", [2, 15, D]); self.nps = do("nps", [2, NB, 15, D])
        self.psn = 0
        self.wslot = 0
        self.ukeys = []
        self.scr = {}
        self.scr_t = nc.dram_tensor("wscr", [16 * L + 16, 128, 4096], BF16, kind="Internal").ap()

    def ps(self):
        i = self.psn
        self.psn = (self.psn + 1) % 6
        return self.pst[i], 'ps%d' % i

    def wb(self):
        i = self.wslot
        self.wslot = (self.wslot + 1) % len(self.wbufs)
        return self.wbufs[i], 'wb%d' % i

    def vec(self, idx):
        return self.vecs_sb[:, idx * KC:(idx + 1) * KC]

    def cs(self, name, rows=128, cols=None):
        o, w = CST_OFF[name]
        return self.cst_sb[0:rows, o:o + (cols if cols is not None else w)]

    def fence(self):
        ks = list(self.ukeys)
        self.S.op('dve', lambda e: e.memset(self.dummy[:], 0.0), r=ks, w=ks + ['dummy'])

    def carve(self, key, words, dt, pattern=None, **kw):
        off = self.uoff
        self.uoff += words
        assert self.uoff <= self.UW, (key, self.uoff)
        ap = self.U[:, off:off + words]
        if dt != F32:
            ap = ap.bitcast(dt)
        if pattern:
            ap = ap.rearrange(pattern, **kw)
        if key not in self.ukeys:
            self.ukeys.append(key)
        return ap

    def rmsnorm(self, x, xk, n, gidx, out, outk):
        S = self.S
        sq, rs = self.sq, self.rstd
        g = self.vec(gidx)
        pt, pk = self.ps()
        for kc in range(KC):
            S.op('act', lambda e, kc=kc: e.activation(sq[:, kc, :n], x[:, kc, :], AF.Square), r=[xk], w=['sq'])
        for kc in range(KC):
            S.op('pe', lambda e, kc=kc: e.matmul(pt[:, :n], self.ones_bf[:], sq[:, kc, :n], start=(kc == 0), stop=(kc == KC - 1)),
                 r=['sq', 'ones'], w=[pk])
        S.op('act', lambda e: e.activation(rs[:, :n], pt[:, :n], AF.Ln, scale=1.0 / D, bias=EPS), r=[pk], w=['rstd'])
        S.op('act', lambda e: e.activation(rs[:, :n], rs[:, :n], AF.Exp, scale=-0.5), r=['rstd'], w=['rstd'])
        for kc in range(KC):
            S.op('dve', lambda e, kc=kc: e.scalar_tensor_tensor(out[:, kc, :], x[:, kc, :], g[:, kc:kc + 1], rs[:, :n], ALU.mult, ALU.mult),
                 r=[xk, 'rstd', 'vecs'], w=[outk])

    def wcache(self, ckey):
        first = ckey not in self.scr
        if first:
            self.scr[ckey] = len(self.scr)
        return self.scr_t[self.scr[ckey]], 'scr_%d' % self.scr[ckey], first

    def load_wblock(self, wt, wk, dst_view, src_view, ckey):
        S = self.S
        scr, sk, first = self.wcache(ckey)
        if first:
            S.dma('pool', dst_view, src_view, w=[wk])
            S.dma('sp', scr, wt[:, 0:4096], r=[wk], w=[sk])
        else:
            S.dma('pool', wt[:, 0:4096], scr, r=[sk], w=[wk])

    def load_wcols(self, src2d, c0, ncols, ckey):
        assert ncols == 512
        wt, wk = self.wb()
        wv = wt[:, 0:KC * ncols].rearrange("p (kc f) -> p kc f", kc=KC)
        self.load_wblock(wt, wk, wv, src2d.rearrange("(kc p) f -> p kc f", p=128)[:, :, c0:c0 + ncols], ckey)
        return wv, wk

    def mlp(self, layer, x, xk, n):
        S = self.S
        hT, aT = self.hT, self.aT
        self.rmsnorm(x, xk, n, self.cfg.depth + layer, hT[:, :, :n], 'hT')
        wd = self.w_down[layer].rearrange("(fc p) d -> p fc d", p=128)
        for fb in range(8):
            wv, wk = self.load_wcols(self.w_up[layer], fb * 512, 512, ('up', layer, fb))
            for j in range(4):
                fc = fb * 4 + j
                pt, pk = self.ps()
                for kc in range(KC):
                    S.op('pe', lambda e, kc=kc, j=j, wv=wv, pt=pt: e.matmul(pt[:, :n], wv[:, kc, j * 128:(j + 1) * 128], hT[:, kc, :n],
                                                                   start=(kc == 0), stop=(kc == KC - 1)), r=[wk, 'hT'], w=[pk])
                S.op('act', lambda e, pt=pt: e.activation(self.relu[:, :n], pt[:, :n], AF.Relu), r=[pk], w=['relu'])
                S.op('dve', lambda e, fc=fc: e.tensor_tensor(aT[:, fc, :n], self.relu[:, :n], self.relu[:, :n], ALU.mult), r=['relu'], w=['aT'])
        for dc in range(KC):
            wt, wk = self.wb()
            wv = wt[:, 0:FC * 128].rearrange("p (fc d) -> p fc d", fc=FC)
            self.load_wblock(wt, wk, wv, wd[:, :, dc * 128:(dc + 1) * 128], ('dn', layer, dc))
            pt, pk = self.ps()
            for fc in range(FC):
                S.op('pe', lambda e, fc=fc, wv=wv, pt=pt: e.matmul(pt[:, :n], wv[:, fc, :], aT[:, fc, :n], start=(fc == 0), stop=(fc == FC - 1)),
                     r=[wk, 'aT'], w=[pk])
            S.op('dve', lambda e, dc=dc, pt=pt: e.tensor_tensor(x[:, dc, :], x[:, dc, :], pt[:, :n], ALU.add), r=[pk, xk], w=[xk])

    def pool_mix(self, o, x, xk, n, first, nb=1, hb=15):
        S = self.S
        uF, A, B, pb = self.uF, self.wsA, self.wsB, self.hT
        npt = n // nb
        Lx = hb + npt
        uv = uF[:, :, 0:nb * Lx].rearrange("p k (b l) -> p k b l", b=nb)
        av = A[:, :, 0:nb * Lx].rearrange("p k (b l) -> p k b l", b=nb)
        bv = B[:, :, 0:nb * Lx].rearrange("p k (b l) -> p k b l", b=nb)
        pbv = pb[:, :, 0:n].rearrange("p k (b l) -> p k b l", b=nb)
        xv = x.rearrange("p k (b l) -> p k b l", b=nb)
        sq, rs = self.sq, self.rstd
        g = self.vec(2 * o + 1)
        pt, pk = self.ps()
        for kc in range(KC):
            S.op('act', lambda e, kc=kc: e.activation(sq[:, kc, :n], x[:, kc, :], AF.Square), r=[xk], w=['sq'])
        for kc in range(KC):
            S.op('pe', lambda e, kc=kc, pt=pt: e.matmul(pt[:, :n], self.ones_bf[:], sq[:, kc, :n], start=(kc == 0), stop=(kc == KC - 1)), r=['sq', 'ones'], w=[pk])
        S.op('act', lambda e, pt=pt: e.activation(rs[:, :n], pt[:, :n], AF.Ln, scale=1.0 / D, bias=EPS), r=[pk], w=['rstd'])
        S.op('act', lambda e: e.activation(rs[:, :n], rs[:, :n], AF.Exp, scale=-0.5), r=['rstd'], w=['rstd'])
        rsv = rs[:, :n].rearrange("p (b l) -> p b l", b=nb)
        for kc in range(KC):
            S.op('dve', lambda e, kc=kc: e.scalar_tensor_tensor(uv[:, kc, :, hb:Lx], xv[:, kc, :, :], g[:, kc:kc + 1], rsv, ALU.mult, ALU.mult),
                 r=[xk, 'rstd', 'vecs'], w=['uF'])
        wt, wk = self.wb()
        wv = wt[:, 0:2048].rearrange("p (g kc d) -> p g kc d", g=4, kc=2)
        S.dma('pool', wv, self.w_pool[o].rearrange("g (kc p) d -> p g kc d", p=128), w=[wk])
        for g_ in range(4):
            w = 2 << g_
            kparts = [(slice(2 * g_, 2 * g_ + 2), slice(0, 2))] if nb == 1 else [(2 * g_ + kk, kk) for kk in range(2)]
            for ku, kw in kparts:
                src, srck = None, 'uF'
                bufs = [(av, 'wsA'), (bv, 'wsB')]
                for st in range(g_ + 1):
                    sh = 1 << st
                    dst, dstk = bufs[st % 2]
                    lo = 2 * sh - 1
                    if src is None:
                        S.op('dve', lambda e, dst=dst, ku=ku, kw=kw, sh=sh, lo=lo: e.tensor_tensor(dst[:, kw, :, lo:Lx], uv[:, ku, :, lo:Lx], uv[:, ku, :, lo - sh:Lx - sh], ALU.add),
                             r=['uF'], w=[dstk])
                    else:
                        S.op('dve', lambda e, dst=dst, src=src, kw=kw, sh=sh, lo=lo: e.tensor_tensor(dst[:, kw, :, lo:Lx], src[:, kw, :, lo:Lx], src[:, kw, :, lo - sh:Lx - sh], ALU.add),
                             r=[srck], w=[dstk])
                    src, srck = dst, dstk
                S.op('dve', lambda e, src=src, ku=ku, kw=kw, w=w: e.scalar_tensor_tensor(pbv[:, ku, :, :], src[:, kw, :, hb:Lx], 1.0 / w, uv[:, ku, :, hb:Lx],
                     ALU.mult, ALU.subtract), r=[srck, 'uF'], w=['hT'])
            ksl = slice(2 * g_, 2 * g_ + 2)
            if first:
                ic = self.cs('invc')[:, g_ * 16:g_ * 16 + 15]
                S.op('dve', lambda e, src=src, ic=ic: e.tensor_tensor(src[:, :, 0, hb:hb + 15], src[:, :, 0, hb:hb + 15],
                     ic.unsqueeze(1).to_broadcast([128, 2, 15]), ALU.mult), r=[srck, 'cst'], w=[srck])
                S.op('dve', lambda e, src=src, ksl=ksl: e.tensor_tensor(pbv[:, ksl, 0, 0:15], src[:, :, 0, hb:hb + 15], uv[:, ksl, 0, hb:hb + 15], ALU.subtract),
                     r=[srck, 'uF'], w=['hT'])
            for oc in range(2):
                pt, pk = self.ps()
                for k2 in range(2):
                    S.op('pe', lambda e, g_=g_, oc=oc, k2=k2, pt=pt: e.matmul(pt[:, :n], wv[:, g_, k2, oc * 128:(oc + 1) * 128], pb[:, 2 * g_ + k2, :n],
                         start=(k2 == 0), stop=(k2 == 1)), r=[wk, 'hT'], w=[pk])
                dc = 2 * g_ + oc
                psc = self.vec(2 * self.cfg.depth + 1 + o)
                S.op('dve', lambda e, dc=dc, pt=pt, psc=psc: e.scalar_tensor_tensor(x[:, dc, :], pt[:, :n], psc[:, dc:dc + 1], x[:, dc, :], ALU.mult, ALU.add),
                     r=[pk, xk, 'vecs'], w=[xk])

    def transpose_out(self, src3, srck, ncol, dst_dram, rows):
        S = self.S
        for kc in range(KC):
            pt, pk = self.ps()
            S.op('pe', lambda e, kc=kc, pt=pt: e.transpose(pt[0:ncol, 0:128], src3[:, kc, :], self.cs('ident')), r=[srck, 'cst'], w=[pk])
            S.op('act', lambda e, kc=kc, pt=pt: e.activation(self.tio[0:ncol, kc * 128:(kc + 1) * 128], pt[0:ncol, 0:128], AF.Copy), r=[pk], w=['tio'])
        S.dma('sp', dst_dram, self.tio[0:ncol, :], r=['tio'], out=True)

    def load_tokens(self, src_dram, nrows, dst3, dstk):
        S = self.S
        S.dma('sp', self.tio[0:nrows, :], src_dram, w=['tio'])
        for kc in range(KC):
            pt, pk = self.ps()
            S.op('pe', lambda e, kc=kc, pt=pt: e.transpose(pt[:, 0:nrows], self.tio[0:nrows, kc * 128:(kc + 1) * 128], self.cs('ident', nrows, nrows)),
                 r=['tio', 'cst'], w=[pk])
            S.op('act', lambda e, kc=kc, pt=pt: e.activation(dst3[:, kc, :], pt[:, 0:nrows], AF.Copy), r=[pk], w=[dstk])

    def proj_feat(self, wv, wk, n, evac):
        S = self.S
        for j in range(4):
            pt, pk = self.ps()
            for kc in range(KC):
                S.op('pe', lambda e, kc=kc, j=j, pt=pt: e.matmul(pt[:, :n], wv[:, kc, j * 128:(j + 1) * 128], self.hT[:, kc, :n],
                     start=(kc == 0), stop=(kc == KC - 1)), r=[wk, 'hT'], w=[pk])
            evac(j, pt, pk)

    def proj_tok(self, wv, wk, P, t4, ncols, c0=0):
        S = self.S
        pt, pk = self.ps()
        for kc in range(KC):
            S.op('pe', lambda e, kc=kc, pt=pt: e.matmul(pt[0:P, 0:ncols], self.hT[:, kc, t4 * P:(t4 + 1) * P], wv[:, kc, c0:c0 + ncols],
                 start=(kc == 0), stop=(kc == KC - 1)), r=[wk, 'hT'], w=[pk])
        return pt, pk

    def even_tile(self, e, kind, x, xk, n, ti):
        S, cfg = self.S, self.cfg
        P = 128 if kind == 'p' else cfg.ns
        NTK = n // P
        hT = self.hT
        self.rmsnorm(x, xk, n, 2 * e, hT[:, :, :n], 'hT')
        win = self.w_in[e]
        gT, qbT, mixT = self.gT, self.qbT, self.mixT
        if kind == 'p':
            KTv = self.KT[:, :, ti * n:(ti + 1) * n]
            ktk = 'KT'
            row0 = ti * n
        else:
            KTv = self.KTs[:, :, 0:n]
            ktk = 'KTs'
            row0 = 0
        fk = self.fk if kind == 'p' else self.fks
        fv = self.fv if kind == 'p' else self.fvs
        fl = self.fl if kind == 'p' else self.fls
        wv, wk = self.load_wcols(win, C_QB, 512, ('in', e, C_QB))
        self.proj_feat(wv, wk, n, lambda j, pt, pk: S.op('act', lambda e_: e_.activation(qbT[:, j, :n], pt[:, :n], AF.Copy), r=[pk], w=['qbT']))
        if cfg.kstop <= 1:
            return
        if kind == 's' and 'past' in cfg.flags and 'fox' in cfg.flags:
            self.fence()
            self.fox_sample_past(e)
            self.fence()
        wv, wk = self.load_wcols(win, C_GA, 512, ('in', e, C_GA))
        self.proj_feat(wv, wk, n, lambda j, pt, pk: S.op('act', lambda e_: e_.activation(gT[:, j, :n], pt[:, :n], AF.Silu), r=[pk], w=['gT']))
        if cfg.kstop <= 2:
            return
        S.op('dve', lambda e_: e_.memset(self.raug[0:32, :], 1.0), w=['raug'])
        pt, pk = self.ps()
        for kc in range(KC):
            S.op('pe', lambda e_, kc=kc, pt=pt: e_.matmul(pt[0:16, :n], self.wsmall[:, kc, 0:16], hT[:, kc, :n], start=(kc == 0), stop=(kc == KC - 1)),
                 r=['wsmall', 'hT'], w=[pk])
        S.op('act', lambda e_, pt=pt: e_.activation(self.raug[0:16, :n], pt[0:16, :n], AF.Copy), r=[pk], w=['raug'])
        wv, wk = self.load_wcols(win, C_KB, 512, ('in', e, C_KB))
        self.proj_feat(wv, wk, n, lambda j, pt, pk: S.op('act', lambda e_: e_.activation(KTv[:, j, :], pt[:, :n], AF.Copy), r=[pk], w=[ktk]))
        for t4 in range(NTK):
            pt, pk = self.proj_tok(wv, wk, P, t4, 512)
            S.op('act', lambda e_, pt=pt: e_.activation(self.stg[0:P, :], pt[0:P, :], AF.Copy), r=[pk], w=['g_oa'])
            S.dma('sp', fk[e, row0 + t4 * P:row0 + (t4 + 1) * P, :], self.stg[0:P, :], r=['g_oa'], out=True)
        if cfg.kstop <= 3:
            return
        wv, wk = self.load_wcols(win, C_VB, 512, ('in', e, C_VB))
        for t4 in range(NTK):
            pt, pk = self.proj_tok(wv, wk, P, t4, 512)
            S.op('act', lambda e_, pt=pt: e_.activation(self.stg[0:P, :], pt[0:P, :], AF.Copy), r=[pk], w=['g_oa'])
            S.dma('sp', fv[e, row0 + t4 * P:row0 + (t4 + 1) * P, :], self.stg[0:P, :], r=['g_oa'], out=True)
            if kind == 'p':
                vdst, vk = self.Vst[:, (row0 // 128) + t4, :], 'Vst'
            else:
                vdst, vk = self.Vs[0:P, :], 'Vs'
            S.op('dve', lambda e_, pt=pt, vdst=vdst: e_.tensor_copy(vdst, pt[0:P, :]), r=[pk], w=[vk])
        if cfg.kstop <= 4:
            return
        wv, wk = self.load_wcols(win, C_QKA, 512, ('in', e, C_QKA))
        for t4 in range(NTK):
            pt, pk = self.proj_tok(wv, wk, P, t4, 512)
            S.op('act', lambda e_, pt=pt, t4=t4: e_.activation(self.qk_tok[0:P, t4, :], pt[0:P, :], AF.Copy), r=[pk], w=['qk_tok'])
        wv, wk = self.load_wcols(win, C_VA, 512, ('in', e, C_VA))
        for t4 in range(NTK):
            pt, pk = self.proj_tok(wv, wk, P, t4, 512)
            S.op('act', lambda e_, pt=pt, t4=t4: e_.activation(self.va_tok[0:P, t4, :], pt[0:P, :], AF.Copy), r=[pk], w=['va_tok'])
        if cfg.kstop <= 5:
            return
        if kind == 'p':
            S.op('dve', lambda e_: e_.tensor_copy(self.Fref[:], self.carry[:]), r=['carry'], w=['Fref'])
        for t4 in range(NTK):
            pf, pfk = self.ps()
            for kc in range(KC):
                S.op('pe', lambda e_, kc=kc, pf=pf, t4=t4: e_.matmul(pf[0:P, 0:4], hT[:, kc, t4 * P:(t4 + 1) * P], self.wsmall[:, kc, 16:20],
                     start=(kc == 0), stop=(kc == KC - 1)), r=['wsmall', 'hT'], w=[pfk])
            lf = self.lf[0:P, t4, :]
            S.op('dve', lambda e_, pf=pf, lf=lf: e_.tensor_tensor(lf, pf[0:P, 0:4], self.bfg[0:P, :], ALU.add), r=[pfk, 'bfg'], w=['lf'])
            S.op('act', lambda e_, lf=lf: e_.activation(lf, lf, AF.Exp, scale=-1.0), r=['lf'], w=['lf'])
            S.op('act', lambda e_, lf=lf: e_.activation(lf, lf, AF.Ln, bias=1.0), r=['lf'], w=['lf'])
            S.op('dve', lambda e_, lf=lf: e_.tensor_scalar(lf, lf, -1.0, None, ALU.mult), r=['lf'], w=['lf'])
            S.dma('sp', fl[e, row0 + t4 * P:row0 + (t4 + 1) * P, :], lf, r=['lf'], out=True)
            pF, pFk = self.ps()
            if kind == 'p':
                tile_idx = row0 // 128 + t4
                S.op('pe', lambda e_, pF=pF, lf=lf: e_.matmul(pF[:, 0:4], self.cs('triU'), lf, start=True, stop=True), r=['lf', 'cst'], w=[pFk])
                S.op('pe', lambda e_, pF=pF, lf=lf: e_.matmul(pF[:, 4:8], self.ones_f[:], lf, start=True, stop=True), r=['lf', 'ones'], w=[pFk])
                S.op('dve', lambda e_, pF=pF, tile_idx=tile_idx: e_.tensor_tensor(self.Fcol[:, tile_idx, :], pF[:, 0:4], self.carry[:], ALU.add),
                     r=[pFk, 'carry'], w=['Fcol'])
                S.op('dve', lambda e_, pF=pF: e_.tensor_tensor(self.carry[:], self.carry[:], pF[:, 4:8], ALU.add), r=[pFk, 'carry'], w=['carry'])
            else:
                S.op('pe', lambda e_, pF=pF, lf=lf: e_.matmul(pF[0:P, 0:4], self.cs('maskCs', P), lf, start=True, stop=True), r=['lf', 'cst'], w=[pFk])
                S.op('dve', lambda e_, pF=pF: e_.tensor_scalar(self.Fn[0:P, :], pF[0:P, 0:4], -1.0, -SHIFT_C, ALU.mult, ALU.add), r=[pFk], w=['Fn'])
        if cfg.kstop <= 6:
            return
        fl_ = cfg.flags
        if 'gla' not in fl_ or 'fox' not in fl_:
            S.op('dve', lambda e_: e_.memset(mixT[:, :, :n], 0.0), w=['mixT'])
        if 'gla' in fl_:
            for t4 in range(NTK):
                self.gla_tile(e, kind, t4, n, ti)
        if 'fox' in fl_:
            if kind == 'p':
                self.fox_prompt(e, ti, n)
            elif 'past' in fl_:
                self.fox_sample(e)
        for blk in range(2):
            wv, wk = self.load_wcols(self.w_out[e], blk * 512, 512, ('out', e, blk))
            for j in range(4):
                dc = blk * 4 + j
                pt, pk = self.ps()
                for mc in range(KC):
                    S.op('pe', lambda e_, mc=mc, j=j, pt=pt, wv=wv: e_.matmul(pt[:, :n], wv[:, mc, j * 128:(j + 1) * 128], mixT[:, mc, :n],
                         start=(mc == 0), stop=(mc == KC - 1)), r=[wk, 'mixT'], w=[pk])
                S.op('dve', lambda e_, dc=dc, pt=pt: e_.tensor_tensor(x[:, dc, :], x[:, dc, :], pt[:, :n], ALU.add), r=[pk, xk], w=[xk])

    def gla_tile(self, e, kind, t4, n, ti):
        S, cfg = self.S, self.cfg
        if kind == 'p':
            P, Cs, nch = 128, 64, 2
            triC, onesC, maskC, csel, sel01 = (self.cs(k) for k in ('triC', 'onesC', 'maskC', 'csel', 'sel01'))
        else:
            P, Cs, nch = 16, 4, 4
            triC, onesC, maskC, csel, sel01 = (self.cs(k, 16) for k in ('triCs', 'onesCs', 'maskCs', 'csels', 'sel01s'))
        cols = slice(t4 * P, (t4 + 1) * P)
        qk = self.qk_tok[0:P, t4, :]
        va = self.va_tok[0:P, t4, :]
        sp, bsb, eb, enb, edb, qe, ke, kd, kdm = (t[0:P, :] for t in (self.g_sp, self.g_b, self.g_eb, self.g_enb, self.g_edb, self.g_qe, self.g_ke, self.g_kd, self.g_kdm))
        Sst, Sbf, ebl = self.Sst, self.Sbf, self.ebl
        pt, pk = self.ps()
        S.op('pe', lambda e_, pt=pt: e_.matmul(pt[0:P, 0:256], self.raug[0:17, cols], self.wg_aug[0:17, :], start=True, stop=True), r=['raug', 'wg'], w=[pk])
        S.op('act', lambda e_, pt=pt: e_.activation(sp, pt[0:P, 0:256], AF.Exp, scale=-1.0), r=[pk], w=['g_sp'])
        S.op('act', lambda e_: e_.activation(sp, sp, AF.Ln, bias=1.0), r=['g_sp'], w=['g_sp'])
        p2, p2k = self.ps()
        S.op('pe', lambda e_, p2=p2: e_.matmul(p2[0:P, 0:256], triC, sp, start=True, stop=True), r=['g_sp', 'cst'], w=[p2k])
        S.op('pe', lambda e_, p2=p2: e_.matmul(p2[0:P, 256:512], onesC, sp, start=True, stop=True), r=['g_sp', 'cst'], w=[p2k])
        S.op('act', lambda e_, p2=p2: e_.activation(bsb, p2[0:P, 0:256], AF.Copy), r=[p2k], w=['g_b'])
        S.op('act', lambda e_: e_.activation(eb, bsb, AF.Exp), r=['g_b'], w=['g_eb'])
        S.op('act', lambda e_: e_.activation(enb, bsb, AF.Exp, scale=-1.0), r=['g_b'], w=['g_enb'])
        S.op('dve', lambda e_, p2=p2: e_.tensor_tensor(edb, p2[0:P, 256:512], bsb, ALU.subtract), r=[p2k, 'g_b'], w=['g_edb'])
        S.op('act', lambda e_: e_.activation(edb, edb, AF.Exp), r=['g_edb'], w=['g_edb'])
        S.op('dve', lambda e_: e_.scalar_tensor_tensor(qe, qk[:, 0:256], 0.125, eb, ALU.mult, ALU.mult), r=['qk_tok', 'g_eb'], w=['g_qe'])
        S.op('dve', lambda e_: e_.tensor_tensor(ke, qk[:, 256:512], enb, ALU.mult), r=['qk_tok', 'g_enb'], w=['g_ke'])
        S.op('dve', lambda e_: e_.tensor_tensor(kd, qk[:, 256:512], edb, ALU.mult), r=['qk_tok', 'g_edb'], w=['g_kd'])
        idn = self.cs('ident', P, P)
        for src, srck, dst, dstk in ((qe, 'g_qe', self.qeT, 'qeT'), (ke, 'g_ke', self.keT, 'keT')):
            p3, p3k = self.ps()
            for h in range(4):
                S.op('pe', lambda e_, h=h, p3=p3, src=src: e_.transpose(p3[0:64, h * P:(h + 1) * P], src[:, h * 64:(h + 1) * 64], idn), r=[srck, 'cst'], w=[p3k])
            S.op('act', lambda e_, p3=p3, dst=dst: e_.activation(dst[:, :, 0:P], p3[0:64, 0:4 * P].rearrange("p (h t) -> p h t", h=4), AF.Copy), r=[p3k], w=[dstk])
        qeT, keT = self.qeT, self.keT
        p5, p5k = self.ps()
        for h in range(4):
            S.op('pe', lambda e_, h=h, p5=p5: e_.matmul(p5[0:P, h * P:(h + 1) * P], keT[:, h, 0:P], qeT[:, h, 0:P], start=True, stop=True), r=['qeT', 'keT'], w=[p5k])
        attnT = self.attnT
        S.op('dve', lambda e_, p5=p5: e_.tensor_tensor(attnT[0:P, :, 0:P], p5[0:P, 0:4 * P].rearrange("p (h t) -> p h t", h=4),
             maskC.unsqueeze(1).to_broadcast([P, 4, P]), ALU.mult), r=[p5k, 'cst'], w=['attnT'])
        p6, p6k = self.ps()
        for h in range(4):
            S.op('pe', lambda e_, h=h, p6=p6: e_.matmul(p6[0:64, h * nch:(h + 1) * nch], sp[:, h * 64:(h + 1) * 64], csel[:, 0:nch], start=True, stop=True),
                 r=['g_sp', 'cst'], w=[p6k])
        S.op('act', lambda e_, p6=p6: e_.activation(ebl[:, 0:4 * nch], p6[0:64, 0:4 * nch], AF.Exp), r=[p6k], w=['ebl'])
        p7, p7k = self.ps()
        for h in range(4):
            S.op('pe', lambda e_, h=h, p7=p7: e_.matmul(p7[:, h * P:(h + 1) * P], va[:, h * 128:(h + 1) * 128], attnT[0:P, h, 0:P], start=True, stop=True),
                 r=['va_tok', 'attnT'], w=[p7k])
        p8, p8k = self.ps()
        for c in range(nch):
            if kind == 's':
                S.dma('sp', Sst[:], self.sgla[e, c].rearrange("h k v -> k h v"), w=['Sst'])
                S.op('act', lambda e_: e_.activation(Sbf[:], Sst[:], AF.Copy), r=['Sst'], w=['Sbf'])
            for h in range(4):
                S.op('pe', lambda e_, h=h, c=c, p8=p8: e_.matmul(p8[:, h * P + c * Cs:h * P + (c + 1) * Cs], Sbf[:, h, :], qeT[:, h, c * Cs:(c + 1) * Cs],
                     start=True, stop=True), r=['Sbf', 'qeT'], w=[p8k])
            S.op('dve', lambda e_, c=c: e_.tensor_scalar(kdm, kd, sel01[:, c:c + 1], None, ALU.mult), r=['g_kd', 'cst'], w=['g_kdm'])
            p9, p9k = self.ps()
            for h in range(4):
                S.op('pe', lambda e_, h=h, p9=p9: e_.matmul(p9[0:64, h * 128:(h + 1) * 128], kdm[:, h * 64:(h + 1) * 64], va[:, h * 128:(h + 1) * 128],
                     start=True, stop=True), r=['g_kdm', 'va_tok'], w=[p9k])
            for h in range(4):
                S.op('dve', lambda e_, h=h, c=c, p9=p9: e_.scalar_tensor_tensor(Sst[:, h, :], Sst[:, h, :], ebl[:, h * nch + c:h * nch + c + 1],
                     p9[0:64, h * 128:(h + 1) * 128], ALU.mult, ALU.add), r=[p9k, 'Sst', 'ebl'], w=['Sst'])
            S.op('act', lambda e_: e_.activation(Sbf[:], Sst[:], AF.Copy), r=['Sst'], w=['Sbf'])
            if kind == 's':
                S.dma('sp', self.gs[e, c].rearrange("h k v -> k h v"), Sst[:], r=['Sst'], out=True)
        if kind == 'p' and ti == cfg.ntt - 1 and t4 == n // P - 1:
            S.dma('sp', self.gp[e].rearrange("h k v -> k h v"), Sst[:], r=['Sst'], out=True)
        inter, oa, sq2, rs2 = self.g_inter, self.g_oa, self.g_sq2, self.g_rs2
        W4 = 4 * P
        S.op('act', lambda e_, p8=p8: e_.activation(inter[:, 0:W4], p8[:, 0:W4], AF.Copy), r=[p8k], w=['g_inter'])
        S.op('dve', lambda e_, p7=p7: e_.tensor_tensor(oa[:, 0:W4], p7[:, 0:W4], inter[:, 0:W4], ALU.add), r=[p7k, 'g_inter'], w=['g_oa'])
        S.op('act', lambda e_: e_.activation(sq2[:, 0:W4], oa[:, 0:W4], AF.Square), r=['g_oa'], w=['g_sq2'])
        p10, p10k = self.ps()
        S.op('pe', lambda e_, p10=p10: e_.matmul(p10[:, 0:W4], self.ones_bf[:], sq2[:, 0:W4], start=True, stop=True), r=['g_sq2', 'ones'], w=[p10k])
        S.op('act', lambda e_, p10=p10: e_.activation(rs2[:, 0:W4], p10[:, 0:W4], AF.Ln, scale=1.0 / 128, bias=EPS), r=[p10k], w=['g_rs2'])
        S.op('act', lambda e_: e_.activation(rs2[:, 0:W4], rs2[:, 0:W4], AF.Exp, scale=-0.5), r=['g_rs2'], w=['g_rs2'])
        S.op('dve', lambda e_: e_.tensor_tensor(oa[:, 0:W4], oa[:, 0:W4], rs2[:, 0:W4], ALU.mult), r=['g_oa', 'g_rs2'], w=['g_oa'])
        gn = self.vecs_sb[:, self.NV - 2 + e:self.NV - 1 + e]
        S.op('dve', lambda e_: e_.scalar_tensor_tensor(self.mixT[:, 0:4, cols], oa[:, 0:W4].rearrange("p (h t) -> p h t", h=4), gn,
             self.gT[:, :, cols], ALU.mult, ALU.mult), r=['g_oa', 'gT', 'vecs'], w=['mixT'])

    def fox_prompt(self, e, ti, n):
        S = self.S
        scale = 128.0 ** -0.5
        njt = (ti + 1) * n // 128
        bT = self.biasT
        S.op('dve', lambda e_: e_.tensor_tensor(bT[:, 0:njt, :], self.Fref[:].unsqueeze(1).to_broadcast([128, njt, 4]), self.Fcol[:, 0:njt, :], ALU.subtract),
             r=['Fref', 'Fcol'], w=['biasT'])
        S.op('dve', lambda e_: e_.tensor_scalar(bT[:, 0:njt, :], bT[:, 0:njt, :], -SHIFT_C, None, ALU.add), r=['biasT'], w=['biasT'])
        for h in range(4):
            po, pok = self.pst[6], 'ps6'
            pl, plk = self.pst[7], 'ps7'
            for j in range(njt):
                c0 = max(0, j * 128 - ti * n)
                diag = j * 128 >= ti * n
                pt, pk = self.ps()
                PT, PTk = (self.PTa, 'PTa') if j % 2 == 0 else (self.PTb, 'PTb')
                S.op('pe', lambda e_, h=h, j=j, c0=c0, pt=pt: e_.matmul(pt[:, c0:n], self.KT[:, h, j * 128:(j + 1) * 128], self.qbT[:, h, c0:n], start=True, stop=True),
                     r=['KT', 'qbT'], w=[pk])
                S.op('act', lambda e_, h=h, j=j, c0=c0, pt=pt, PT=PT: e_.activation(PT[:, c0:n], pt[:, c0:n], AF.Exp, bias=bT[:, j, h:h + 1], scale=scale),
                     r=[pk, 'biasT'], w=[PTk])
                if diag:
                    S.op('dve', lambda e_, c0=c0, PT=PT: e_.tensor_tensor(PT[:, c0:c0 + 128], PT[:, c0:c0 + 128], self.triU_bf[:], ALU.mult), r=[PTk, 'triUbf'], w=[PTk])
                last = (j == njt - 1)
                S.op('pe', lambda e_, j=j, c0=c0, pl=pl, PT=PT, last=last: e_.matmul(pl[:, c0:n], self.ones_bf[:], PT[:, c0:n], start=(j == 0), stop=last),
                     r=[PTk, 'ones'], w=[plk])
                S.op('pe', lambda e_, h=h, j=j, c0=c0, po=po, PT=PT, last=last: e_.matmul(po[:, c0:n], self.Vst[:, j, h * 128:(h + 1) * 128], PT[:, c0:n], start=(j == 0), stop=last),
                     r=[PTk, 'Vst'], w=[pok])
            S.op('dve', lambda e_, pl=pl: e_.reciprocal(self.rl[:, :n], pl[:, :n]), r=[plk], w=['rl'])
            S.op('dve', lambda e_, h=h, po=po: e_.tensor_tensor(self.mixT[:, 4 + h, :n], po[:, :n], self.rl[:, :n], ALU.mult), r=[pok, 'rl'], w=['mixT'])

    def fox_sample_past(self, e):
        S, cfg = self.S, self.cfg
        NPG, NB = cfg.npg, cfg.nbs
        scale = 128.0 ** -0.5
        G = min(self.GS, NPG)
        psO, psOk = self.pst[6], 'ps6'
        psL, psLk = self.pst[7], 'ps7'
        lfpg, sa, sb_, RHs, RHc, s_sb, PTs = self.s_lfpg, self.s_sa, self.s_sb, self.s_RHs, self.s_RHc, self.s_ssb, self.s_PTs
        for b in range(NB):
            S.op('pool', lambda e_, b=b: e_.indirect_dma_start(out=lfpg[0:NPG, :], out_offset=None, in_=self.lfp[e],
                 in_offset=bass.IndirectOffsetOnAxis(ap=self.idx_pg[0:NPG, b:b + 1], axis=0)), r=['idx_pg'], w=['s_lfpg'], dma=True)
            X, Xk = lfpg, 's_lfpg'
            bufs = [(sa, 's_sa'), (sb_, 's_sb')]
            for st in range(7):
                sh = 1 << st
                Y, Yk = bufs[st % 2]
                xv = X[0:NPG, :].rearrange("p (r h) -> p r h", h=4)
                yv = Y[0:NPG, :].rearrange("p (r h) -> p r h", h=4)
                S.op('dve', lambda e_, xv=xv, yv=yv, sh=sh: e_.tensor_tensor(yv[:, 0:128 - sh, :], xv[:, 0:128 - sh, :], xv[:, sh:128, :], ALU.add), r=[Xk], w=[Yk])
                S.op('dve', lambda e_, xv=xv, yv=yv, sh=sh: e_.tensor_copy(yv[:, 128 - sh:128, :], xv[:, 128 - sh:128, :]), r=[Xk], w=[Yk])
                X, Xk = Y, Yk
            pl, plk = self.ps()
            S.op('pe', lambda e_, pl=pl, X=X: e_.matmul(pl[0:NPG, 0:4], self.cs('sufU', NPG, NPG), X[0:NPG, 0:4], start=True, stop=True), r=[Xk, 'cst'], w=[plk])
            S.op('act', lambda e_, pl=pl: e_.activation(self.s_later[0:NPG, :], pl[0:NPG, 0:4], AF.Copy), r=[plk], w=['s_later'])
            S.op('dve', lambda e_, X=X: e_.tensor_tensor(RHs[0:NPG, :], X[0:NPG, :], lfpg[0:NPG, :], ALU.subtract), r=[Xk, 's_lfpg'], w=['s_RHs'])
            rv = RHs[0:NPG, :].rearrange("p (r h) -> p r h", h=4)
            S.op('dve', lambda e_, rv=rv: e_.tensor_tensor(rv, rv, self.s_later[0:NPG, :].unsqueeze(1).to_broadcast([NPG, 128, 4]), ALU.add),
                 r=['s_RHs', 's_later'], w=['s_RHs'])
            pr, prk = self.ps()
            for h in range(4):
                S.op('pe', lambda e_, h=h, pr=pr, rv=rv: e_.transpose(pr[:, h * NPG:(h + 1) * NPG], rv[:, :, h], self.cs('ident', NPG, NPG)), r=['s_RHs', 'cst'], w=[prk])
            S.op('act', lambda e_, pr=pr: e_.activation(RHc[:, 0:4 * NPG], pr[:, 0:4 * NPG], AF.Copy), r=[prk], w=['s_RHc'])
            psa, psak = self.ps()
            psb, psbk = self.ps()
            for s0 in range(0, NPG, G):
                for g in range(G):
                    slot = s0 + g
                    S.op('pool', lambda e_, g=g, slot=slot, b=b: e_.indirect_dma_start(out=self.s_Kst[:, g, :], out_offset=None, in_=self.ktp[e],
                         in_offset=bass.IndirectOffsetOnAxis(ap=self.idx[:, b * NPG + slot:b * NPG + slot + 1], axis=0)), r=['idx'], w=['s_Kst%d' % g], dma=True)
                    eng = 'act' if g % 2 == 0 else 'dve'
                    if eng == 'act':
                        S.op('act', lambda e_, g=g: e_.activation(self.s_Kbf[:, g, :], self.s_Kst[:, g, :], AF.Copy), r=['s_Kst%d' % g], w=['s_Kbf%d' % g])
                    else:
                        S.op('dve', lambda e_, g=g: e_.tensor_copy(self.s_Kbf[:, g, :], self.s_Kst[:, g, :]), r=['s_Kst%d' % g], w=['s_Kbf%d' % g])
                    for h in range(4):
                        pp, ppk = (psa, psak) if h < 2 else (psb, psbk)
                        c0 = ((h % 2) * NPG + slot) * 4
                        S.op('pe', lambda e_, g=g, h=h, b=b, pp=pp, c0=c0: e_.matmul(pp[:, c0:c0 + 4], self.s_Kbf[:, g, h * 128:(h + 1) * 128], self.qbT[:, h, b * 4:(b + 1) * 4],
                             start=True, stop=True), r=['s_Kbf%d' % g, 'qbT'], w=[ppk])
            for h in range(4):
                pp, ppk = (psa, psak) if h < 2 else (psb, psbk)
                c0 = (h % 2) * NPG * 4
                S.op('dve', lambda e_, h=h, pp=pp, c0=c0: e_.scalar_tensor_tensor(s_sb[:, 0:NPG * 4].rearrange("p (s q) -> p s q", q=4),
                     pp[:, c0:c0 + NPG * 4].rearrange("p (s q) -> p s q", q=4), scale,
                     RHc[:, h * NPG:(h + 1) * NPG].unsqueeze(2).to_broadcast([128, NPG, 4]), ALU.mult, ALU.add), r=[ppk, 's_RHc'], w=['s_ssb'])
                S.op('act', lambda e_, h=h: e_.activation(PTs[:, h, 0:NPG * 4], s_sb[:, 0:NPG * 4], AF.Exp, bias=-SHIFT_C), r=['s_ssb'], w=['s_PTs'])
            for s0 in range(0, NPG, G):
                for g in range(G):
                    slot = s0 + g
                    S.op('pool', lambda e_, g=g, slot=slot, b=b: e_.indirect_dma_start(out=self.s_Vst[:, g, :], out_offset=None, in_=self.vp[e],
                         in_offset=bass.IndirectOffsetOnAxis(ap=self.idx[:, b * NPG + slot:b * NPG + slot + 1], axis=0)), r=['idx'], w=['s_Vst%d' % g], dma=True)
                    if g % 2 == 0:
                        S.op('act', lambda e_, g=g: e_.activation(self.s_Vbf[:, g, :], self.s_Vst[:, g, :], AF.Copy), r=['s_Vst%d' % g], w=['s_Vbf%d' % g])
                    else:
                        S.op('dve', lambda e_, g=g: e_.tensor_copy(self.s_Vbf[:, g, :], self.s_Vst[:, g, :]), r=['s_Vst%d' % g], w=['s_Vbf%d' % g])
                    for h in range(4):
                        c0 = (h * 4 + b) * 4
                        S.op('pe', lambda e_, g=g, h=h, slot=slot, c0=c0, b=b: e_.matmul(psO[:, c0:c0 + 4], self.s_Vbf[:, g, h * 128:(h + 1) * 128], PTs[:, h, slot * 4:(slot + 1) * 4],
                             start=(slot == 0 and h == 0 and b == 0), stop=(slot == NPG - 1 and h == 3 and b == NB - 1)), r=['s_Vbf%d' % g, 's_PTs'], w=[psOk])
                        S.op('pe', lambda e_, h=h, slot=slot, c0=c0, b=b: e_.matmul(psL[:, c0:c0 + 4], self.ones_bf[:], PTs[:, h, slot * 4:(slot + 1) * 4],
                             start=(slot == 0 and h == 0 and b == 0), stop=(slot == NPG - 1 and h == 3 and b == NB - 1)), r=['s_PTs', 'ones'], w=[psLk])
        S.op('act', lambda e_: e_.activation(self.Opast[:], psO[:, 0:64], AF.Copy), r=[psOk], w=['Opast'])
        S.op('act', lambda e_: e_.activation(self.Lpast[:], psL[:, 0:64], AF.Copy), r=[psLk], w=['Lpast'])

    def fox_sample(self, e):
        S, cfg = self.S, self.cfg
        scale = 128.0 ** -0.5
        NSK = cfg.ns
        ptn, ptnk = self.ps()
        for h in range(4):
            S.op('pe', lambda e_, h=h: e_.matmul(ptn[0:NSK, h * NSK:(h + 1) * NSK], self.KTs[:, h, 0:NSK], self.qbT[:, h, 0:NSK], start=True, stop=True),
                 r=['KTs', 'qbT'], w=[ptnk])
        Pnf, Pn = self.Pnf, self.Pn
        for h in range(4):
            S.op('act', lambda e_, h=h: e_.activation(Pnf[0:NSK, h, :], ptn[0:NSK, h * NSK:(h + 1) * NSK], AF.Exp, bias=self.Fn[0:NSK, h:h + 1], scale=scale),
                 r=[ptnk, 'Fn'], w=['Pnf'])
        S.op('dve', lambda e_: e_.tensor_tensor(Pn[0:NSK, :, :], Pnf[0:NSK, :, :], self.cs('maskCs', NSK).unsqueeze(1).to_broadcast([NSK, 4, NSK]), ALU.mult),
             r=['Pnf', 'cst'], w=['Pn'])
        pon, ponk = self.ps()
        pln, plnk = self.ps()
        for h in range(4):
            S.op('pe', lambda e_, h=h: e_.matmul(pon[:, h * NSK:(h + 1) * NSK], self.Vs[0:NSK, h * 128:(h + 1) * 128], Pn[0:NSK, h, :], start=True, stop=True),
                 r=['Vs', 'Pn'], w=[ponk])
            S.op('pe', lambda e_, h=h: e_.matmul(pln[:, h * NSK:(h + 1) * NSK], self.ones_bf[0:NSK, :], Pn[0:NSK, h, :], start=True, stop=True),
                 r=['ones', 'Pn'], w=[plnk])
        Ot, Lt = self.Ot, self.Lt
        S.op('dve', lambda e_: e_.tensor_tensor(Ot[:], pon[:, 0:64], self.Opast[:], ALU.add), r=[ponk, 'Opast'], w=['Ot'])
        S.op('dve', lambda e_: e_.tensor_tensor(Lt[:], pln[:, 0:64], self.Lpast[:], ALU.add), r=[plnk, 'Lpast'], w=['Lt'])
        S.op('dve', lambda e_: e_.reciprocal(Lt[:], Lt[:]), r=['Lt'], w=['Lt'])
        S.op('dve', lambda e_: e_.tensor_tensor(self.mixT[:, 4:8, 0:NSK], Ot[:].rearrange("p (h t) -> p h t", h=4), Lt[:].rearrange("p (h t) -> p h t", h=4), ALU.mult),
             r=['Ot', 'Lt'], w=['mixT'])

    def build(self):
        import contextlib
        nc, S, cfg = self.nc, self.S, self.cfg
        T, L, TT, NS, NB, NPG = cfg.seq, cfg.depth, cfg.tt, cfg.ns, cfg.nbs, cfg.npg
        di = lambda n, s, d=F32: nc.dram_tensor(n, s, d, kind="ExternalInput").ap()
        self.lfp = [di("lfp%d" % i, [cfg.npool, 512]) for i in range(2)]
        self.vp = [di("vp%d" % i, [cfg.npool * 128, 512]) for i in range(2)]
        self.ptabT = di("ptabT", [NPG, NB], I32)
        st = contextlib.ExitStack()
        sb = lambda name, shape, dt: st.enter_context(nc.sbuf_tensor(name, shape, dt))
        self.xT = sb("xT", [128, KC, T], F32); self.xsT = sb("xsT", [128, KC, NS], F32)
        self.hT = sb("hT", [128, KC, TT], BF16); self.sq = sb("sq", [128, KC, TT], BF16); self.rstd = sb("rstd", [128, TT], F32)
        self.wbufs = [sb("wbuf%d" % i, [128, 4096], BF16) for i in range(2)]
        self.cst_sb = sb("cst_sb", [128, CST_W], F32); self.vecs_sb = sb("vecs_sb", [128, self.NV], F32)
        self.ones_bf = sb("ones_bf", [128, 128], BF16); self.ones_f = sb("ones_f", [128, 128], F32); self.triU_bf = sb("triU_bf", [128, 128], BF16)
        self.KT = sb("KT", [128, 4, T], BF16); self.Vst = sb("Vst", [128, T // 128, 512], BF16)
        self.Fcol = sb("Fcol", [128, T // 128, 4], F32); self.carry = sb("carry", [128, 4], F32); self.Fref = sb("Fref", [128, 4], F32)
        self.bfg = sb("bfg", [128, 4], F32); self.wg_aug = sb("wg_aug", [32, 256], BF16); self.wsmall = sb("wsmall", [128, KC, 20], BF16)
        self.Sst = sb("Sst", [64, 4, 128], F32); self.Sbf = sb("Sbf", [64, 4, 128], BF16)
        self.KTs = sb("KTs", [128, 4, NS], BF16); self.Vs = sb("Vs", [NS, 512], BF16); self.Fn = sb("Fn", [NS, 4], F32)
        self.Opast = sb("Opast", [128, 64], F32); self.Lpast = sb("Lpast", [128, 64], F32); self.Ot = sb("Ot", [128, 64], F32); self.Lt = sb("Lt", [128, 64], F32)
        self.Pnf = sb("Pnf", [NS, 4, NS], F32); self.Pn = sb("Pn", [NS, 4, NS], BF16)
        self.idx = sb("idx", [128, NB * NPG], I32); self.idx_f = sb("idx_f", [128, NB * NPG], F32); self.idx_pg = sb("idx_pg", [NPG, NB], I32)
        self.ebl = sb("ebl", [64, 16], F32); self.lf = sb("lf", [128, max(1, TT // 128), 4], F32); self.biasT = sb("biasT", [128, T // 128, 4], F32)
        self.dummy = sb("dummy_t", [128, 1], F32)
        UFW = max(15 + TT, NB * 19)
        self.UW = 9984
        GS = self.GS = 4
        self.U = sb("U", [128, self.UW], F32)
        self.pst = [st.enter_context(nc.psum_tensor("ps%d" % i, [128, 512], F32)) for i in range(8)]
        self.uoff = 0
        self.aT = self.carve('aT', 16 * TT, BF16, "p (f t) -> p f t", f=FC)
        self.uF = self.carve('uF', 8 * UFW, F32, "p (k l) -> p k l", k=KC)
        self.wsA = self.carve('wsA', 2 * UFW, F32, "p (k l) -> p k l", k=2)
        self.wsB = self.carve('wsB', 2 * UFW, F32, "p (k l) -> p k l", k=2)
        self.tio = self.carve('tio', 1024, F32)
        self.relu = self.carve('relu', TT // 2, BF16)
        self.uoff = 0
        self.qbT = self.carve('qbT', 2 * TT, BF16, "p (h t) -> p h t", h=4)
        e_start = self.uoff
        NTK = max(1, TT // 128)
        self.qk_tok = self.carve('qk_tok', NTK * 512, F32, "p (t c) -> p t c", c=512)
        self.va_tok = self.carve('va_tok', NTK * 256, BF16, "p (t c) -> p t c", c=512)
        self.gT = self.carve('gT', 2 * TT, BF16, "p (h t) -> p h t", h=4)
        self.mixT = self.carve('mixT', 4 * TT, BF16, "p (h t) -> p h t", h=8)
        self.raug = self.carve('raug', TT // 2, BF16)
        self.g_sp, self.g_b, self.g_eb, self.g_enb, self.g_edb, self.g_qe, self.g_ke = (self.carve(k, 256, F32) for k in ('g_sp', 'g_b', 'g_eb', 'g_enb', 'g_edb', 'g_qe', 'g_ke'))
        self.g_kd = self.carve('g_kd', 128, BF16); self.g_kdm = self.carve('g_kdm', 128, BF16)
        self.attnT = self.carve('attnT', 256, BF16, "p (h t) -> p h t", h=4)
        self.qeT = self.carve('qeT', 256, BF16, "p (h t) -> p h t", h=4)[0:64]
        self.keT = self.carve('keT', 256, BF16, "p (h t) -> p h t", h=4)[0:64]
        self.g_inter = self.carve('g_inter', 512, F32); self.g_oa = self.carve('g_oa', 512, F32); self.g_rs2 = self.carve('g_rs2', 512, F32)
        self.stg = self.g_oa
        self.g_sq2 = self.carve('g_sq2', 256, BF16)
        self.PTa = self.carve('PTa', TT // 2, BF16); self.PTb = self.carve('PTb', TT // 2, BF16); self.rl = self.carve('rl', TT, F32)
        self.uoff = e_start
        self.s_lfpg = self.carve('s_lfpg', 512, F32); self.s_sa = self.carve('s_sa', 512, F32); self.s_sb = self.carve('s_sb', 512, F32)
        self.s_RHs = self.carve('s_RHs', 512, F32); self.s_RHc = self.carve('s_RHc', 4 * NPG, F32); self.s_ssb = self.carve('s_ssb', 4 * NPG, F32)
        self.s_PTs = self.carve('s_PTs', 8 * NPG, BF16, "p (h c) -> p h c", h=4)
        self.s_later = self.carve('s_later', 4, F32)
        self.s_Kst = self.carve('s_Kst', 512 * GS, F32, "p (g c) -> p g c", g=GS); self.s_Kbf = self.carve('s_Kbf', 256 * GS, BF16, "p (g c) -> p g c", g=GS)
        self.s_Vst = self.carve('s_Vst', 512 * GS, F32, "p (g c) -> p g c", g=GS); self.s_Vbf = self.carve('s_Vbf', 256 * GS, BF16, "p (g c) -> p g c", g=GS)
        for g in range(GS):
            self.ukeys += ['s_Kst%d' % g, 's_Kbf%d' % g, 's_Vst%d' % g, 's_Vbf%d' % g]
        xT, xsT = self.xT, self.xsT
        S.dma('sp', self.cst_sb[:], self.cst, w=['cst'])
        S.dma('sp', self.vecs_sb[:], self.vecs, w=['vecs'])
        S.op('dve', lambda e: e.memset(self.ones_bf[:], 1.0), w=['ones'])
        S.op('dve', lambda e: e.memset(self.ones_f[:], 1.0), w=['ones'])
        S.op('dve', lambda e: e.tensor_copy(self.triU_bf[:], self.cs('triU')), r=['cst'], w=['triUbf'])
        S.dma('sp', self.idx_pg[:], self.ptabT, w=['idx_pg'])
        S.dma('sp', self.idx[:], self.ptab.partition_broadcast(128), w=['idx'])
        S.op('dve', lambda e: e.tensor_copy(self.idx_f[:], self.idx[:]), r=['idx'], w=['idx_f'])
        S.op('dve', lambda e: e.scalar_tensor_tensor(self.idx_f[:], self.idx_f[:], 128.0, self.cs('iota').to_broadcast([128, NB * NPG]), ALU.mult, ALU.add),
             r=['idx_f', 'cst'], w=['idx_f'])
        S.op('dve', lambda e: e.tensor_copy(self.idx[:], self.idx_f[:]), r=['idx_f'], w=['idx'])
        for tt in range(T // 128):
            self.load_tokens(self.xp[tt * 128:(tt + 1) * 128, :], 128, xT[:, :, tt * 128:(tt + 1) * 128], 'x%d' % (tt * 128 // TT))
        self.load_tokens(self.xs, NS, xsT[:, :, :], 'xs')
        xt = lambda ti: (xT[:, :, ti * TT:(ti + 1) * TT], 'x%d' % ti)
        for l in range(L):
            if l % 2 == 0 and 'noeven' in cfg.flags:
                for ti in range(cfg.ntt):
                    x, xk = xt(ti)
                    self.mlp(l, x, xk, TT)
                continue
            if l % 2 == 0:
                e = l // 2
                S.dma('pool', self.wsmall[:, :, 0:16], self.w_in[e].rearrange("(kc p) f -> p kc f", p=128)[:, :, C_RA:C_RA + 16], w=['wsmall'])
                S.dma('pool', self.wsmall[:, :, 16:20], self.w_in[e].rearrange("(kc p) f -> p kc f", p=128)[:, :, C_FB:C_FB + 4], w=['wsmall'])
                S.dma('pool', self.wg_aug[0:16, :], self.w_gate[e], w=['wg'])
                S.dma('pool', self.wg_aug[16:17, :], self.b_gate[e], w=['wg'])
                S.dma('sp', self.bfg[:], self.b_forget[e].partition_broadcast(128), w=['bfg'])
                S.op('dve', lambda e_: e_.memset(self.carry[:], 0.0), w=['carry'])
                S.op('dve', lambda e_: e_.memset(self.Sst[:], 0.0), w=['Sst'])
                S.op('dve', lambda e_: e_.memset(self.Sbf[:], 0.0), w=['Sbf'])
                for ti in range(cfg.ntt):
                    x, xk = xt(ti)
                    self.fence()
                    self.even_tile(e, 'p', x, xk, TT, ti)
                    self.fence()
                    self.mlp(l, x, xk, TT)
                if 'sample' in cfg.flags:
                    self.fence()
                    self.even_tile(e, 's', xsT[:, :, :], 'xs', NS, 0)
                    self.fence()
                    self.mlp(l, xsT[:, :, :], 'xs', NS)
            else:
                o = l // 2
                self.fence()
                S.op('dve', lambda e_: e_.memset(self.uF[:, :, 0:15], 0.0), w=['uF'])
                for ti in range(cfg.ntt):
                    x, xk = xt(ti)
                    self.pool_mix(o, x, xk, TT, first=(ti == 0))
                    if ti == cfg.ntt - 1:
                        self.transpose_out(self.uF[:, :, TT:TT + 15], 'uF', 15, self.npool[o], 15)
                    else:
                        S.op('dve', lambda e_: e_.tensor_copy(self.uF[:, :, 0:15], self.uF[:, :, TT:TT + 15]), r=['uF'], w=['uF'])
                    self.mlp(l, x, xk, TT)
                if 'sample' not in cfg.flags:
                    continue
                uvs = self.uF[:, :, 0:NB * 19].rearrange("p k (b l) -> p k b l", b=NB)
                S.dma('sp', self.tio[0:NB * 15, :], self.spool[o].rearrange("b r d -> (b r) d"), w=['tio'])
                for kc in range(KC):
                    pt, pk = self.ps()
                    S.op('pe', lambda e_, kc=kc, pt=pt: e_.transpose(pt[:, 0:NB * 15], self.tio[0:NB * 15, kc * 128:(kc + 1) * 128], self.cs('ident', NB * 15, NB * 15)),
                         r=['tio', 'cst'], w=[pk])
                    S.op('act', lambda e_, kc=kc, pt=pt: e_.activation(uvs[:, kc, :, 0:15], pt[:, 0:NB * 15].rearrange("p (b r) -> p b r", b=NB), AF.Copy), r=[pk], w=['uF'])
                S.dma('sp', self.nps[o, :, 0:11, :], self.spool[o, :, 4:15, :], out=True)
                self.pool_mix(o, xsT[:, :, :], 'xs', NS, first=False, nb=NB)
                tmp = self.wsA[:, :, :].rearrange("p k l -> p (k l)")[:, 0:KC * NS].rearrange("p (k t) -> p k t", k=KC)
                for kc in range(KC):
                    S.op('dve', lambda e_, kc=kc: e_.tensor_copy(tmp[:, kc, :].rearrange("p (b t) -> p b t", b=NB), uvs[:, kc, :, 15:19]), r=['uF'], w=['wsA'])
                for kc in range(KC):
                    pt, pk = self.ps()
                    S.op('pe', lambda e_, kc=kc, pt=pt: e_.transpose(pt[0:NS, 0:128], tmp[:, kc, :], self.cs('ident')), r=['wsA', 'cst'], w=[pk])
                    S.op('act', lambda e_, kc=kc, pt=pt: e_.activation(self.tio[0:NS, kc * 128:(kc + 1) * 128], pt[0:NS, 0:128], AF.Copy), r=[pk], w=['tio'])
                for b in range(NB):
                    S.dma('sp', self.nps[o, b, 11:15, :], self.tio[b * 4:(b + 1) * 4, :], r=['tio'], out=True)
                self.mlp(l, xsT[:, :, :], 'xs', NS)
        self.fence()
        for ti in range(cfg.ntt):
            x, xk = xt(ti)
            self.rmsnorm(x, xk, TT, 2 * L, self.uF[:, :, 0:TT], 'uF')
            for t4 in range(TT // 128):
                r0 = ti * TT + t4 * 128
                self.transpose_out(self.uF[:, :, t4 * 128:(t4 + 1) * 128], 'uF', 128, self.y[r0:r0 + 128, :], 128)
        self.rmsnorm(xsT[:, :, :], 'xs', NS, 2 * L, self.uF[:, :, 0:NS], 'uF')
        self.transpose_out(self.uF[:, :, 0:NS], 'uF', NS, self.ys, NS)
        S.emit()
        print("sbuf bytes remaining:", nc.sbuf_bytes_remaining, "ops:", len(S.ops))
        st.close()
        return nc


def shared_inputs(cfg, inp):
    L = cfg.depth
    f = lambda a: np.ascontiguousarray(np.asarray(a), dtype=np.float32)
    cols = [inp["norm_mix"][l] for l in range(L)] + [inp["norm_mlp"][l] for l in range(L)] + [inp["norm_final"]] + \
           [inp["pool_scale"][o] for o in range(2)]
    vecs = np.concatenate([np.asarray(c, np.float32).reshape(KC, 128).T for c in cols] + [np.asarray(inp["gla_norm"], np.float32).T], axis=1)
    ck = np.asarray(inp["cache_fox_k"], np.float32)
    npool = ck.shape[1]
    ktp = np.ascontiguousarray(ck.transpose(0, 1, 4, 3, 2)).reshape(2, npool * 128, 512)
    vp = f(inp["cache_fox_v"]).reshape(2, npool * 128, 512)
    lfp = f(inp["cache_fox_logf"]).reshape(2, npool, 512)
    return dict(cst=make_consts(), vecs=np.ascontiguousarray(vecs), w_in=f(inp["w_in_even"]), w_out=f(inp["w_out_even"]),
                w_gate=f(inp["w_gate_up"]), b_gate=f(inp["b_gate"]).reshape(2, 1, 256), b_forget=f(inp["b_forget"]).reshape(2, 1, 4),
                w_up=f(inp["w_mlp_up"]), w_down=f(inp["w_mlp_down"]), w_pool=f(inp["w_pool"]),
                ktp0=ktp[0], ktp1=ktp[1], vp0=vp[0], vp1=vp[1], lfp0=lfp[0], lfp1=lfp[1])


def core_inputs(cfg, c, inp, shared):
    NB = cfg.nbs
    f = lambda a: np.ascontiguousarray(np.asarray(a), dtype=np.float32)
    pt = np.asarray(inp["page_table"])[NB * c:NB * (c + 1)].astype(np.int32)
    m = dict(shared)
    m.update(xp=f(inp["x_prompt"][c]), xs=f(np.asarray(inp["x_sample"])[NB * c:NB * (c + 1)]).reshape(cfg.ns, D),
             ptab=np.ascontiguousarray(pt.reshape(1, -1)), ptabT=np.ascontiguousarray(pt.T),
             sgla=f(np.asarray(inp["state_gla"])[:, NB * c:NB * (c + 1)]), spool=f(np.asarray(inp["state_pool"])[:, NB * c:NB * (c + 1)]))
    return m


def assemble(cfg, results, B, Bs):
    NB = cfg.nbs
    T = cfg.seq
    r = results
    cat = lambda k, ax: np.concatenate([r[c][k] for c in range(len(r))], axis=ax)
    y_prompt = np.stack([r[c]["y"] for c in range(B)], 0)
    y_sample = np.concatenate([r[c]["ys"].reshape(NB, 4, D) for c in range(B)], 0)
    fk = np.stack([r[c]["fk"].reshape(2, T, 4, 128) for c in range(B)], 1)
    fv = np.stack([r[c]["fv"].reshape(2, T, 4, 128) for c in range(B)], 1)
    fl = np.stack([r[c]["fl"] for c in range(B)], 1)
    fks = np.concatenate([r[c]["fks"].reshape(2, NB, 4, 4, 128) for c in range(B)], 1)
    fvs = np.concatenate([r[c]["fvs"].reshape(2, NB, 4, 4, 128) for c in range(B)], 1)
    fls = np.concatenate([r[c]["fls"].reshape(2, NB, 4, 4) for c in range(B)], 1)
    gp = np.stack([r[c]["gp"] for c in range(B)], 1)
    gs = np.concatenate([r[c]["gs"] for c in range(B)], 1)
    npool = np.stack([r[c]["npool"] for c in range(B)], 1)
    nps = np.concatenate([r[c]["nps"] for c in range(B)], 1)
    return tuple(np.ascontiguousarray(a, dtype=np.float32) for a in (y_prompt, y_sample, fk, fv, fl, fks, fvs, fls, gp, gs, npool, nps))


_NC_CACHE = {}


def kernel(**inp):
    B, T, _ = np.asarray(inp["x_prompt"]).shape
    Bs = np.asarray(inp["x_sample"]).shape[0]
    npg = np.asarray(inp["page_table"]).shape[1]
    npool = np.asarray(inp["cache_fox_k"]).shape[1]
    cfg = Cfg(seq=T, depth=4, tt=256, npg=npg, npool=npool)
    key = (T, npg, npool)
    if key not in _NC_CACHE:
        _NC_CACHE[key] = Builder(cfg).build()
    nc = _NC_CACHE[key]
    shared = shared_inputs(cfg, inp)
    maps = [core_inputs(cfg, c, inp, shared) for c in range(B)]
    res = run_bass_kernel_spmd(nc, maps, core_ids=list(range(B)))
    return assemble(cfg, res.results, B, Bs)
```

```python
import numpy as np
import concourse.bass as bass
import concourse.mybir as mybir
from concourse.bass_utils import run_bass_kernel_spmd

F32 = mybir.dt.float32
BF16 = mybir.dt.bfloat16
I32 = mybir.dt.int32
AF = mybir.ActivationFunctionType
ALU = mybir.AluOpType


class Sched:
    ENGS = ('pe', 'act', 'dve', 'pool', 'sp')
    NDMA = 20

    def __init__(self, nc, same_engine_sync=True):
        self.nc = nc
        self.ops = []
        self.last_w = {}
        self.readers = {}
        self.same = same_engine_sync
        self.out_dmas = []
        self.dma_rr = {'sp': 0, 'pool': 0, 'act': 0}
        self.dma_last = {}

    def op(self, eng, fn, r=(), w=(), dma=False, out=False):
        idx = len(self.ops)
        w = list(w) + [k for k in r if isinstance(k, str) and k.startswith('ps') and k not in w]
        deps = set()
        for k in r:
            if k in self.last_w:
                deps.add(self.last_w[k])
        for k in w:
            if k in self.last_w:
                deps.add(self.last_w[k])
            deps |= self.readers.get(k, set())
        semslot = None
        if dma:
            slot = self.dma_rr[eng]
            self.dma_rr[eng] = (slot + 1) % self.NDMA
            semslot = (eng, slot)
            if semslot in self.dma_last:
                deps.add(self.dma_last[semslot])
            self.dma_last[semslot] = idx
        for k in r:
            self.readers.setdefault(k, set()).add(idx)
        for k in w:
            self.last_w[k] = idx
            self.readers[k] = set()
        self.ops.append(dict(eng=eng, fn=fn, deps=deps, dma=dma, semslot=semslot))
        if out:
            self.out_dmas.append(idx)
        return idx

    def dma(self, eng, out_ap, in_ap, r=(), w=(), out=False, **kw):
        return self.op(eng, lambda e: e.dma_start(out=out_ap, in_=in_ap, **kw), r=r, w=w, dma=True, out=out)

    def emit(self):
        nc = self.nc
        ops = self.ops
        self.ops.append(dict(eng='sp', fn=None, deps=set(self.out_dmas) | set(self.dma_last.values()), dma=False, semslot=None))
        import contextlib
        stack = contextlib.ExitStack()
        esem = {e: stack.enter_context(nc.semaphore("se_" + e)) for e in ('pe', 'act', 'dve', 'pool')}
        dsem = {}
        for e in ('sp', 'pool'):
            for s in range(self.NDMA):
                dsem[(e, s)] = stack.enter_context(nc.semaphore("sd_%s_%d" % (e, s)))
        ecount = {e: 0 for e in esem}
        dcount = {k: 0 for k in dsem}
        for o in ops:
            if o['dma']:
                dcount[o['semslot']] += 16
                o['sig'] = (dsem[o['semslot']], dcount[o['semslot']], o['semslot'])
            elif o['eng'] in esem and o['fn'] is not None:
                ecount[o['eng']] += 1
                o['sig'] = (esem[o['eng']], ecount[o['eng']], o['eng'])
            else:
                o['sig'] = None
        streams = {e: [] for e in self.ENGS}
        for i, o in enumerate(ops):
            streams[o['eng']].append(i)
        eobj = {'pe': 'tensor', 'act': 'scalar', 'dve': 'vector', 'pool': 'gpsimd', 'sp': 'sync'}

        def make(engname):
            def body(e):
                waited = {}
                for i in streams[engname]:
                    o = ops[i]
                    for d in sorted(o['deps']):
                        dd = ops[d]
                        if dd['sig'] is None:
                            continue
                        sem, val, key = dd['sig']
                        if (not dd['dma']) and dd['eng'] == engname and (engname == 'pe' or not self.same):
                            continue
                        if waited.get(key, 0) >= val:
                            continue
                        e.wait_ge(sem, val)
                        waited[key] = val
                    if o['fn'] is None:
                        continue
                    ins = o['fn'](e)
                    if o['sig'] is not None:
                        ins.then_inc(o['sig'][0], 16 if o['dma'] else 1)
            return body

        with nc.Block() as block:
            block.tensor(make('pe'))
            block.scalar(make('act'))
            block.vector(make('dve'))
            block.gpsimd(make('pool'))
            block.sync(make('sp'))
        stack.close()


D = 1024
DFF = 4096
KC = 8
FC = 32
EPS = 1e-6
NINC = 3092
C_QKA, C_VA, C_RA, C_GA, C_QB, C_KB, C_VB, C_FB = 0, 512, 1024, 1040, 1552, 2064, 2576, 3088
SHIFT_C = 8.0


class Cfg:
    def __init__(self, seq=2048, depth=4, tt=256, npg=64, npool=2560):
        self.seq = seq
        self.depth = depth
        self.tt = min(tt, seq)
        self.ntt = seq // self.tt
        self.npg = npg
        self.npool = npool
        self.nbs = 4
        import os
        self.flags = os.environ.get('KFLAGS', 'sample,gla,fox,past,pool,mlp')
        self.kstop = int(os.environ.get('KSTOP', '99'))
        self.ns = 16


def _cst_layout():
    names = [('ident', 128), ('triU', 128), ('sufU', 128), ('triC', 128), ('onesC', 128), ('maskC', 128),
             ('csel', 2), ('sel01', 2), ('triCs', 16), ('onesCs', 16), ('maskCs', 16), ('csels', 4), ('sel01s', 4),
             ('iota', 1), ('invc', 64)]
    off, o = {}, 0
    for n, w in names:
        off[n] = (o, w)
        o += w
    return off, o


CST_OFF, CST_W = _cst_layout()


def make_consts():
    c = np.zeros((128, CST_W), np.float32)
    def put(name, a):
        o, w = CST_OFF[name]
        c[:a.shape[0], o:o + a.shape[1]] = a
    i = np.arange(128)
    put('ident', np.eye(128, dtype=np.float32))
    put('triU', (i[:, None] <= i[None, :]).astype(np.float32))
    put('sufU', (i[:, None] > i[None, :]).astype(np.float32))
    same = (i[:, None] // 64) == (i[None, :] // 64)
    put('triC', np.where(same & (i[:, None] <= i[None, :]), -1.0 / 16, 0.0).astype(np.float32))
    put('onesC', np.where(same, -1.0 / 16, 0.0).astype(np.float32))
    put('maskC', (same & (i[:, None] <= i[None, :])).astype(np.float32))
    put('csel', np.where((i[:, None] // 64) == np.arange(2)[None, :], -1.0 / 16, 0.0).astype(np.float32))
    put('sel01', ((i[:, None] // 64) == np.arange(2)[None, :]).astype(np.float32))
    j = np.arange(16)
    sames = (j[:, None] // 4) == (j[None, :] // 4)
    put('triCs', np.where(sames & (j[:, None] <= j[None, :]), -1.0 / 16, 0.0).astype(np.float32))
    put('onesCs', np.where(sames, -1.0 / 16, 0.0).astype(np.float32))
    put('maskCs', (sames & (j[:, None] <= j[None, :])).astype(np.float32))
    put('csels', np.where((j[:, None] // 4) == np.arange(4)[None, :], -1.0 / 16, 0.0).astype(np.float32))
    put('sel01s', ((j[:, None] // 4) == np.arange(4)[None, :]).astype(np.float32))
    put('iota', i[:, None].astype(np.float32))
    pos = np.arange(16, dtype=np.float32) + 1.0
    inv = np.concatenate([1.0 / np.minimum(float(2 << g), pos) for g in range(4)])[None, :]
    put('invc', np.broadcast_to(inv, (128, 64)).astype(np.float32))
    return c


class Builder:
    def __init__(self, cfg):
        self.cfg = cfg
        nc = self.nc = bass.Bass("TRN2", target_bir_lowering=False)
        self.S = Sched(nc)
        T, L = cfg.seq, cfg.depth
        NS, NB, NPG = cfg.ns, cfg.nbs, cfg.npg
        di = lambda n, s, d=F32: nc.dram_tensor(n, s, d, kind="ExternalInput").ap()
        do = lambda n, s: nc.dram_tensor(n, s, F32, kind="ExternalOutput").ap()
        self.xp = di("xp", [T, D]); self.xs = di("xs", [NS, D])
        self.cst = di("cst", [128, CST_W])
        self.NV = (2 * L + 3) * KC + 2
        self.vecs = di("vecs", [128, self.NV])
        self.w_in = di("w_in", [2, D, NINC]); self.w_out = di("w_out", [2, D, D])
        self.w_gate = di("w_gate", [2, 16, 256]); self.b_gate = di("b_gate", [2, 1, 256]); self.b_forget = di("b_forget", [2, 1, 4])
        self.w_up = di("w_up", [L, D, DFF]); self.w_down = di("w_down", [L, DFF, D]); self.w_pool = di("w_pool", [2, 4, 256, 256])
        self.ktp = [di("ktp%d" % i, [cfg.npool * 128, 512]) for i in range(2)]
        self.ptab = di("ptab", [1, NB * NPG], I32)
        self.sgla = di("sgla", [2, NB, 4, 64, 128]); self.spool = di("spool", [2, NB, 15, D])
        self.y = do("y", [T, D]); self.ys = do("ys", [NS, D])
        self.fk = do("fk", [2, T, 512]); self.fv = do("fv", [2, T, 512]); self.fl = do("fl", [2, T, 4])
        self.fks = do("fks", [2, NS, 512]); self.fvs = do("fvs", [2, NS, 512]); self.fls = do("fls", [2, NS, 4])
        self.gp = do("gp", [2, 4, 64, 128]); self.gs = do("gs", [2, NB, 4, 64, 128])
        self.npool = do("npool", [2, 15, D]); self.nps = do("nps", [2, NB, 15, D])
        self.psn = 0
        self.wslot = 0
        self.ukeys = []
        self.scr = {}
        self.wq = 0
        self.scr_t = nc.dram_tensor("wscr", [16 * L + 16, 128, 4096], BF16, kind="Internal").ap()

    def ps(self):
        i = self.psn
        self.psn = (self.psn + 1) % 6
        return self.pst[i], 'ps%d' % i

    def wb(self):
        i = self.wslot
        self.wslot = (self.wslot + 1) % len(self.wbufs)
        return self.wbufs[i], 'wb%d' % i

    def vec(self, idx):
        return self.vecs_sb[:, idx * KC:(idx + 1) * KC]

    def cs(self, name, rows=128, cols=None):
        o, w = CST_OFF[name]
        return self.cst_sb[0:rows, o:o + (cols if cols is not None else w)]

    def fence(self):
        ks = list(self.ukeys)
        self.S.op('dve', lambda e: e.memset(self.dummy[:], 0.0), r=ks, w=ks + ['dummy'])

    def carve(self, key, words, dt, pattern=None, **kw):
        off = self.uoff
        self.uoff += words
        assert self.uoff <= self.UW, (key, self.uoff)
        ap = self.U[:, off:off + words]
        if dt != F32:
            ap = ap.bitcast(dt)
        if pattern:
            ap = ap.rearrange(pattern, **kw)
        if key not in self.ukeys:
            self.ukeys.append(key)
        return ap

    def rmsnorm(self, x, xk, n, gidx, out, outk):
        S = self.S
        sq, rs = self.sq, self.rstd
        g = self.vec(gidx)
        pt, pk = self.ps()
        for kc in range(KC):
            S.op('act', lambda e, kc=kc: e.activation(sq[:, kc, :n], x[:, kc, :], AF.Square), r=[xk], w=['sq'])
        for kc in range(KC):
            S.op('pe', lambda e, kc=kc: e.matmul(pt[:, :n], self.ones_bf[:], sq[:, kc, :n], start=(kc == 0), stop=(kc == KC - 1)),
                 r=['sq', 'ones'], w=[pk])
        S.op('act', lambda e: e.activation(rs[:, :n], pt[:, :n], AF.Ln, scale=1.0 / D, bias=EPS), r=[pk], w=['rstd'])
        S.op('act', lambda e: e.activation(rs[:, :n], rs[:, :n], AF.Exp, scale=-0.5), r=['rstd'], w=['rstd'])
        for kc in range(KC):
            S.op('dve', lambda e, kc=kc: e.scalar_tensor_tensor(out[:, kc, :], x[:, kc, :], g[:, kc:kc + 1], rs[:, :n], ALU.mult, ALU.mult),
                 r=[xk, 'rstd', 'vecs'], w=[outk])

    def wcache(self, ckey):
        first = ckey not in self.scr
        if first:
            self.scr[ckey] = len(self.scr)
        return self.scr_t[self.scr[ckey]], 'scr_%d' % self.scr[ckey], first

    def load_wblock(self, wt, wk, dst_view, src_view, ckey):
        S = self.S
        scr, sk, first = self.wcache(ckey)
        if first:
            S.dma('pool', dst_view, src_view, w=[wk])
            S.dma('sp', scr, wt[:, 0:4096], r=[wk], w=[sk])
        else:
            q = ('pool', 'sp')[self.wq % 2]
            self.wq += 1
            S.dma(q, wt[:, 0:4096], scr, r=[sk], w=[wk])

    def load_wcols(self, src2d, c0, ncols, ckey):
        assert ncols == 512
        wt, wk = self.wb()
        wv = wt[:, 0:KC * ncols].rearrange("p (kc f) -> p kc f", kc=KC)
        self.load_wblock(wt, wk, wv, src2d.rearrange("(kc p) f -> p kc f", p=128)[:, :, c0:c0 + ncols], ckey)
        return wv, wk

    def mlp(self, layer, x, xk, n):
        S = self.S
        hT, aT = self.hT, self.aT
        self.rmsnorm(x, xk, n, self.cfg.depth + layer, hT[:, :, :n], 'hT')
        wd = self.w_down[layer].rearrange("(fc p) d -> p fc d", p=128)
        for fb in range(8):
            wv, wk = self.load_wcols(self.w_up[layer], fb * 512, 512, ('up', layer, fb))
            for j in range(4):
                fc = fb * 4 + j
                pt, pk = self.ps()
                for kc in range(KC):
                    S.op('pe', lambda e, kc=kc, j=j, wv=wv, pt=pt: e.matmul(pt[:, :n], wv[:, kc, j * 128:(j + 1) * 128], hT[:, kc, :n],
                                                                   start=(kc == 0), stop=(kc == KC - 1)), r=[wk, 'hT'], w=[pk])
                S.op('act', lambda e, pt=pt: e.activation(self.relu[:, :n], pt[:, :n], AF.Relu), r=[pk], w=['relu'])
                S.op('dve', lambda e, fc=fc: e.tensor_tensor(aT[:, fc, :n], self.relu[:, :n], self.relu[:, :n], ALU.mult), r=['relu'], w=['aT'])
        for dc in range(KC):
            wt, wk = self.wb()
            wv = wt[:, 0:FC * 128].rearrange("p (fc d) -> p fc d", fc=FC)
            self.load_wblock(wt, wk, wv, wd[:, :, dc * 128:(dc + 1) * 128], ('dn', layer, dc))
            pt, pk = self.ps()
            for fc in range(FC):
                S.op('pe', lambda e, fc=fc, wv=wv, pt=pt: e.matmul(pt[:, :n], wv[:, fc, :], aT[:, fc, :n], start=(fc == 0), stop=(fc == FC - 1)),
                     r=[wk, 'aT'], w=[pk])
            S.op('dve', lambda e, dc=dc, pt=pt: e.tensor_tensor(x[:, dc, :], x[:, dc, :], pt[:, :n], ALU.add), r=[pk, xk], w=[xk])

    def pool_mix(self, o, x, xk, n, first, nb=1, hb=15):
        S = self.S
        uF, A, B, pb = self.uF, self.wsA, self.wsB, self.hT
        npt = n // nb
        Lx = hb + npt
        uv = uF[:, :, 0:nb * Lx].rearrange("p k (b l) -> p k b l", b=nb)
        av = A[:, :, 0:nb * Lx].rearrange("p k (b l) -> p k b l", b=nb)
        bv = B[:, :, 0:nb * Lx].rearrange("p k (b l) -> p k b l", b=nb)
        pbv = pb[:, :, 0:n].rearrange("p k (b l) -> p k b l", b=nb)
        xv = x.rearrange("p k (b l) -> p k b l", b=nb)
        sq, rs = self.sq, self.rstd
        g = self.vec(2 * o + 1)
        pt, pk = self.ps()
        for kc in range(KC):
            S.op('act', lambda e, kc=kc: e.activation(sq[:, kc, :n], x[:, kc, :], AF.Square), r=[xk], w=['sq'])
        for kc in range(KC):
            S.op('pe', lambda e, kc=kc, pt=pt: e.matmul(pt[:, :n], self.ones_bf[:], sq[:, kc, :n], start=(kc == 0), stop=(kc == KC - 1)), r=['sq', 'ones'], w=[pk])
        S.op('act', lambda e, pt=pt: e.activation(rs[:, :n], pt[:, :n], AF.Ln, scale=1.0 / D, bias=EPS), r=[pk], w=['rstd'])
        S.op('act', lambda e: e.activation(rs[:, :n], rs[:, :n], AF.Exp, scale=-0.5), r=['rstd'], w=['rstd'])
        rsv = rs[:, :n].rearrange("p (b l) -> p b l", b=nb)
        for kc in range(KC):
            S.op('dve', lambda e, kc=kc: e.scalar_tensor_tensor(uv[:, kc, :, hb:Lx], xv[:, kc, :, :], g[:, kc:kc + 1], rsv, ALU.mult, ALU.mult),
                 r=[xk, 'rstd', 'vecs'], w=['uF'])
        wt, wk = self.wb()
        wv = wt[:, 0:2048].rearrange("p (g kc d) -> p g kc d", g=4, kc=2)
        S.dma('pool', wv, self.w_pool[o].rearrange("g (kc p) d -> p g kc d", p=128), w=[wk])
        for g_ in range(4):
            w = 2 << g_
            kparts = [(slice(2 * g_, 2 * g_ + 2), slice(0, 2))] if nb == 1 else [(2 * g_ + kk, kk) for kk in range(2)]
            for ku, kw in kparts:
                src, srck = None, 'uF'
                bufs = [(av, 'wsA'), (bv, 'wsB')]
                for st in range(g_ + 1):
                    sh = 1 << st
                    dst, dstk = bufs[st % 2]
                    lo = 2 * sh - 1
                    if src is None:
                        S.op('dve', lambda e, dst=dst, ku=ku, kw=kw, sh=sh, lo=lo: e.tensor_tensor(dst[:, kw, :, lo:Lx], uv[:, ku, :, lo:Lx], uv[:, ku, :, lo - sh:Lx - sh], ALU.add),
                             r=['uF'], w=[dstk])
                    else:
                        S.op('dve', lambda e, dst=dst, src=src, kw=kw, sh=sh, lo=lo: e.tensor_tensor(dst[:, kw, :, lo:Lx], src[:, kw, :, lo:Lx], src[:, kw, :, lo - sh:Lx - sh], ALU.add),
                             r=[srck], w=[dstk])
                    src, srck = dst, dstk
                S.op('dve', lambda e, src=src, ku=ku, kw=kw, w=w: e.scalar_tensor_tensor(pbv[:, ku, :, :], src[:, kw, :, hb:Lx], 1.0 / w, uv[:, ku, :, hb:Lx],
                     ALU.mult, ALU.subtract), r=[srck, 'uF'], w=['hT'])
            ksl = slice(2 * g_, 2 * g_ + 2)
            if first:
                ic = self.cs('invc')[:, g_ * 16:g_ * 16 + 15]
                S.op('dve', lambda e, src=src, ic=ic: e.tensor_tensor(src[:, :, 0, hb:hb + 15], src[:, :, 0, hb:hb + 15],
                     ic.unsqueeze(1).to_broadcast([128, 2, 15]), ALU.mult), r=[srck, 'cst'], w=[srck])
                S.op('dve', lambda e, src=src, ksl=ksl: e.tensor_tensor(pbv[:, ksl, 0, 0:15], src[:, :, 0, hb:hb + 15], uv[:, ksl, 0, hb:hb + 15], ALU.subtract),
                     r=[srck, 'uF'], w=['hT'])
            for oc in range(2):
                pt, pk = self.ps()
                for k2 in range(2):
                    S.op('pe', lambda e, g_=g_, oc=oc, k2=k2, pt=pt: e.matmul(pt[:, :n], wv[:, g_, k2, oc * 128:(oc + 1) * 128], pb[:, 2 * g_ + k2, :n],
                         start=(k2 == 0), stop=(k2 == 1)), r=[wk, 'hT'], w=[pk])
                dc = 2 * g_ + oc
                psc = self.vec(2 * self.cfg.depth + 1 + o)
                S.op('dve', lambda e, dc=dc, pt=pt, psc=psc: e.scalar_tensor_tensor(x[:, dc, :], pt[:, :n], psc[:, dc:dc + 1], x[:, dc, :], ALU.mult, ALU.add),
                     r=[pk, xk, 'vecs'], w=[xk])

    def transpose_out(self, src3, srck, ncol, dst_dram, rows):
        S = self.S
        for kc in range(KC):
            pt, pk = self.ps()
            S.op('pe', lambda e, kc=kc, pt=pt: e.transpose(pt[0:ncol, 0:128], src3[:, kc, :], self.cs('ident')), r=[srck, 'cst'], w=[pk])
            S.op('act', lambda e, kc=kc, pt=pt: e.activation(self.tio[0:ncol, kc * 128:(kc + 1) * 128], pt[0:ncol, 0:128], AF.Copy), r=[pk], w=['tio'])
        S.dma('sp', dst_dram, self.tio[0:ncol, :], r=['tio'], out=True)

    def load_tokens(self, src_dram, nrows, dst3, dstk):
        S = self.S
        S.dma('sp', self.tio[0:nrows, :], src_dram, w=['tio'])
        for kc in range(KC):
            pt, pk = self.ps()
            S.op('pe', lambda e, kc=kc, pt=pt: e.transpose(pt[:, 0:nrows], self.tio[0:nrows, kc * 128:(kc + 1) * 128], self.cs('ident', nrows, nrows)),
                 r=['tio', 'cst'], w=[pk])
            S.op('act', lambda e, kc=kc, pt=pt: e.activation(dst3[:, kc, :], pt[:, 0:nrows], AF.Copy), r=[pk], w=[dstk])

    def proj_feat(self, wv, wk, n, evac):
        S = self.S
        for j in range(4):
            pt, pk = self.ps()
            for kc in range(KC):
                S.op('pe', lambda e, kc=kc, j=j, pt=pt: e.matmul(pt[:, :n], wv[:, kc, j * 128:(j + 1) * 128], self.hT[:, kc, :n],
                     start=(kc == 0), stop=(kc == KC - 1)), r=[wk, 'hT'], w=[pk])
            evac(j, pt, pk)

    def proj_tok(self, wv, wk, P, t4, ncols, c0=0):
        S = self.S
        pt, pk = self.ps()
        for kc in range(KC):
            S.op('pe', lambda e, kc=kc, pt=pt: e.matmul(pt[0:P, 0:ncols], self.hT[:, kc, t4 * P:(t4 + 1) * P], wv[:, kc, c0:c0 + ncols],
                 start=(kc == 0), stop=(kc == KC - 1)), r=[wk, 'hT'], w=[pk])
        return pt, pk

    def even_tile(self, e, kind, x, xk, n, ti):
        S, cfg = self.S, self.cfg
        P = 128 if kind == 'p' else cfg.ns
        NTK = n // P
        hT = self.hT
        self.rmsnorm(x, xk, n, 2 * e, hT[:, :, :n], 'hT')
        win = self.w_in[e]
        gT, qbT, mixT = self.gT, self.qbT, self.mixT
        if kind == 'p':
            KTv = self.KT[:, :, ti * n:(ti + 1) * n]
            ktk = 'KT'
            row0 = ti * n
        else:
            KTv = self.KTs[:, :, 0:n]
            ktk = 'KTs'
            row0 = 0
        fk = self.fk if kind == 'p' else self.fks
        fv = self.fv if kind == 'p' else self.fvs
        fl = self.fl if kind == 'p' else self.fls
        wv, wk = self.load_wcols(win, C_QB, 512, ('in', e, C_QB))
        self.proj_feat(wv, wk, n, lambda j, pt, pk: S.op('act', lambda e_: e_.activation(qbT[:, j, :n], pt[:, :n], AF.Copy), r=[pk], w=['qbT']))
        if cfg.kstop <= 1:
            return
        if kind == 's' and 'past' in cfg.flags and 'fox' in cfg.flags:
            self.fence()
            self.fox_sample_past(e)
            self.fence()
        wv, wk = self.load_wcols(win, C_GA, 512, ('in', e, C_GA))
        self.proj_feat(wv, wk, n, lambda j, pt, pk: S.op('act', lambda e_: e_.activation(gT[:, j, :n], pt[:, :n], AF.Silu), r=[pk], w=['gT']))
        if cfg.kstop <= 2:
            return
        S.op('dve', lambda e_: e_.memset(self.raug[0:32, :], 1.0), w=['raug'])
        pt, pk = self.ps()
        for kc in range(KC):
            S.op('pe', lambda e_, kc=kc, pt=pt: e_.matmul(pt[0:16, :n], self.wsmall[:, kc, 0:16], hT[:, kc, :n], start=(kc == 0), stop=(kc == KC - 1)),
                 r=['wsmall', 'hT'], w=[pk])
        S.op('act', lambda e_, pt=pt: e_.activation(self.raug[0:16, :n], pt[0:16, :n], AF.Copy), r=[pk], w=['raug'])
        wv, wk = self.load_wcols(win, C_KB, 512, ('in', e, C_KB))
        self.proj_feat(wv, wk, n, lambda j, pt, pk: S.op('act', lambda e_: e_.activation(KTv[:, j, :], pt[:, :n], AF.Copy), r=[pk], w=[ktk]))
        for t4 in range(NTK):
            pt, pk = self.proj_tok(wv, wk, P, t4, 512)
            S.op('act', lambda e_, pt=pt: e_.activation(self.stg[0:P, :], pt[0:P, :], AF.Copy), r=[pk], w=['g_oa'])
            S.dma('sp', fk[e, row0 + t4 * P:row0 + (t4 + 1) * P, :], self.stg[0:P, :], r=['g_oa'], out=True)
        if cfg.kstop <= 3:
            return
        wv, wk = self.load_wcols(win, C_VB, 512, ('in', e, C_VB))
        for t4 in range(NTK):
            pt, pk = self.proj_tok(wv, wk, P, t4, 512)
            S.op('act', lambda e_, pt=pt: e_.activation(self.stg[0:P, :], pt[0:P, :], AF.Copy), r=[pk], w=['g_oa'])
            S.dma('sp', fv[e, row0 + t4 * P:row0 + (t4 + 1) * P, :], self.stg[0:P, :], r=['g_oa'], out=True)
            if kind == 'p':
                vdst, vk = self.Vst[:, (row0 // 128) + t4, :], 'Vst'
            else:
                vdst, vk = self.Vs[0:P, :], 'Vs'
            S.op('dve', lambda e_, pt=pt, vdst=vdst: e_.tensor_copy(vdst, pt[0:P, :]), r=[pk], w=[vk])
        if cfg.kstop <= 4:
            return
        wv, wk = self.load_wcols(win, C_QKA, 512, ('in', e, C_QKA))
        for t4 in range(NTK):
            pt, pk = self.proj_tok(wv, wk, P, t4, 512)
            S.op('act', lambda e_, pt=pt, t4=t4: e_.activation(self.qk_tok[0:P, t4, :], pt[0:P, :], AF.Copy), r=[pk], w=['qk_tok'])
        wv, wk = self.load_wcols(win, C_VA, 512, ('in', e, C_VA))
        for t4 in range(NTK):
            pt, pk = self.proj_tok(wv, wk, P, t4, 512)
            S.op('act', lambda e_, pt=pt, t4=t4: e_.activation(self.va_tok[0:P, t4, :], pt[0:P, :], AF.Copy), r=[pk], w=['va_tok'])
        if cfg.kstop <= 5:
            return
        if kind == 'p':
            S.op('dve', lambda e_: e_.tensor_copy(self.Fref[:], self.carry[:]), r=['carry'], w=['Fref'])
        for t4 in range(NTK):
            pf, pfk = self.ps()
            for kc in range(KC):
                S.op('pe', lambda e_, kc=kc, pf=pf, t4=t4: e_.matmul(pf[0:P, 0:4], hT[:, kc, t4 * P:(t4 + 1) * P], self.wsmall[:, kc, 16:20],
                     start=(kc == 0), stop=(kc == KC - 1)), r=['wsmall', 'hT'], w=[pfk])
            lf = self.lf[0:P, t4, :]
            S.op('dve', lambda e_, pf=pf, lf=lf: e_.tensor_tensor(lf, pf[0:P, 0:4], self.bfg[0:P, :], ALU.add), r=[pfk, 'bfg'], w=['lf'])
            S.op('act', lambda e_, lf=lf: e_.activation(lf, lf, AF.Exp, scale=-1.0), r=['lf'], w=['lf'])
            S.op('act', lambda e_, lf=lf: e_.activation(lf, lf, AF.Ln, bias=1.0), r=['lf'], w=['lf'])
            S.op('dve', lambda e_, lf=lf: e_.tensor_scalar(lf, lf, -1.0, None, ALU.mult), r=['lf'], w=['lf'])
            S.dma('sp', fl[e, row0 + t4 * P:row0 + (t4 + 1) * P, :], lf, r=['lf'], out=True)
            pF, pFk = self.ps()
            if kind == 'p':
                tile_idx = row0 // 128 + t4
                S.op('pe', lambda e_, pF=pF, lf=lf: e_.matmul(pF[:, 0:4], self.cs('triU'), lf, start=True, stop=True), r=['lf', 'cst'], w=[pFk])
                S.op('pe', lambda e_, pF=pF, lf=lf: e_.matmul(pF[:, 4:8], self.ones_f[:], lf, start=True, stop=True), r=['lf', 'ones'], w=[pFk])
                S.op('dve', lambda e_, pF=pF, tile_idx=tile_idx: e_.tensor_tensor(self.Fcol[:, tile_idx, :], pF[:, 0:4], self.carry[:], ALU.add),
                     r=[pFk, 'carry'], w=['Fcol'])
                S.op('dve', lambda e_, pF=pF: e_.tensor_tensor(self.carry[:], self.carry[:], pF[:, 4:8], ALU.add), r=[pFk, 'carry'], w=['carry'])
            else:
                S.op('pe', lambda e_, pF=pF, lf=lf: e_.matmul(pF[0:P, 0:4], self.cs('maskCs', P), lf, start=True, stop=True), r=['lf', 'cst'], w=[pFk])
                S.op('dve', lambda e_, pF=pF: e_.tensor_scalar(self.Fn[0:P, :], pF[0:P, 0:4], -1.0, -SHIFT_C, ALU.mult, ALU.add), r=[pFk], w=['Fn'])
        if cfg.kstop <= 6:
            return
        fl_ = cfg.flags
        if 'gla' not in fl_ or 'fox' not in fl_:
            S.op('dve', lambda e_: e_.memset(mixT[:, :, :n], 0.0), w=['mixT'])
        if 'gla' in fl_:
            for t4 in range(NTK):
                self.gla_tile(e, kind, t4, n, ti)
        if 'fox' in fl_:
            if kind == 'p':
                self.fox_prompt(e, ti, n)
            elif 'past' in fl_:
                self.fox_sample(e)
        for blk in range(2):
            wv, wk = self.load_wcols(self.w_out[e], blk * 512, 512, ('out', e, blk))
            for j in range(4):
                dc = blk * 4 + j
                pt, pk = self.ps()
                for mc in range(KC):
                    S.op('pe', lambda e_, mc=mc, j=j, pt=pt, wv=wv: e_.matmul(pt[:, :n], wv[:, mc, j * 128:(j + 1) * 128], mixT[:, mc, :n],
                         start=(mc == 0), stop=(mc == KC - 1)), r=[wk, 'mixT'], w=[pk])
                S.op('dve', lambda e_, dc=dc, pt=pt: e_.tensor_tensor(x[:, dc, :], x[:, dc, :], pt[:, :n], ALU.add), r=[pk, xk], w=[xk])

    def gla_tile(self, e, kind, t4, n, ti):
        S, cfg = self.S, self.cfg
        if kind == 'p':
            P, Cs, nch = 128, 64, 2
            triC, onesC, maskC, csel, sel01 = (self.cs(k) for k in ('triC', 'onesC', 'maskC', 'csel', 'sel01'))
        else:
            P, Cs, nch = 16, 4, 4
            triC, onesC, maskC, csel, sel01 = (self.cs(k, 16) for k in ('triCs', 'onesCs', 'maskCs', 'csels', 'sel01s'))
        cols = slice(t4 * P, (t4 + 1) * P)
        qk = self.qk_tok[0:P, t4, :]
        va = self.va_tok[0:P, t4, :]
        sp, bsb, eb, enb, edb, qe, ke, kd, kdm = (t[0:P, :] for t in (self.g_sp, self.g_b, self.g_eb, self.g_enb, self.g_edb, self.g_qe, self.g_ke, self.g_kd, self.g_kdm))
        Sst, Sbf, ebl = self.Sst, self.Sbf, self.ebl
        pt, pk = self.ps()
        S.op('pe', lambda e_, pt=pt: e_.matmul(pt[0:P, 0:256], self.raug[0:17, cols], self.wg_aug[0:17, :], start=True, stop=True), r=['raug', 'wg'], w=[pk])
        S.op('act', lambda e_, pt=pt: e_.activation(sp, pt[0:P, 0:256], AF.Exp, scale=-1.0), r=[pk], w=['g_sp'])
        S.op('act', lambda e_: e_.activation(sp, sp, AF.Ln, bias=1.0), r=['g_sp'], w=['g_sp'])
        p2, p2k = self.ps()
        S.op('pe', lambda e_, p2=p2: e_.matmul(p2[0:P, 0:256], triC, sp, start=True, stop=True), r=['g_sp', 'cst'], w=[p2k])
        S.op('pe', lambda e_, p2=p2: e_.matmul(p2[0:P, 256:512], onesC, sp, start=True, stop=True), r=['g_sp', 'cst'], w=[p2k])
        S.op('act', lambda e_, p2=p2: e_.activation(bsb, p2[0:P, 0:256], AF.Copy), r=[p2k], w=['g_b'])
        S.op('act', lambda e_: e_.activation(eb, bsb, AF.Exp), r=['g_b'], w=['g_eb'])
        S.op('act', lambda e_: e_.activation(enb, bsb, AF.Exp, scale=-1.0), r=['g_b'], w=['g_enb'])
        S.op('dve', lambda e_, p2=p2: e_.tensor_tensor(edb, p2[0:P, 256:512], bsb, ALU.subtract), r=[p2k, 'g_b'], w=['g_edb'])
        S.op('act', lambda e_: e_.activation(edb, edb, AF.Exp), r=['g_edb'], w=['g_edb'])
        S.op('dve', lambda e_: e_.scalar_tensor_tensor(qe, qk[:, 0:256], 0.125, eb, ALU.mult, ALU.mult), r=['qk_tok', 'g_eb'], w=['g_qe'])
        S.op('dve', lambda e_: e_.tensor_tensor(ke, qk[:, 256:512], enb, ALU.mult), r=['qk_tok', 'g_enb'], w=['g_ke'])
        S.op('dve', lambda e_: e_.tensor_tensor(kd, qk[:, 256:512], edb, ALU.mult), r=['qk_tok', 'g_edb'], w=['g_kd'])
        idn = self.cs('ident', P, P)
        for src, srck, dst, dstk in ((qe, 'g_qe', self.qeT, 'qeT'), (ke, 'g_ke', self.keT, 'keT')):
            p3, p3k = self.ps()
            for h in range(4):
                S.op('pe', lambda e_, h=h, p3=p3, src=src: e_.transpose(p3[0:64, h * P:(h + 1) * P], src[:, h * 64:(h + 1) * 64], idn), r=[srck, 'cst'], w=[p3k])
            S.op('act', lambda e_, p3=p3, dst=dst: e_.activation(dst[:, :, 0:P], p3[0:64, 0:4 * P].rearrange("p (h t) -> p h t", h=4), AF.Copy), r=[p3k], w=[dstk])
        qeT, keT = self.qeT, self.keT
        p5, p5k = self.ps()
        for h in range(4):
            S.op('pe', lambda e_, h=h, p5=p5: e_.matmul(p5[0:P, h * P:(h + 1) * P], keT[:, h, 0:P], qeT[:, h, 0:P], start=True, stop=True), r=['qeT', 'keT'], w=[p5k])
        attnT = self.attnT
        S.op('dve', lambda e_, p5=p5: e_.tensor_tensor(attnT[0:P, :, 0:P], p5[0:P, 0:4 * P].rearrange("p (h t) -> p h t", h=4),
             maskC.unsqueeze(1).to_broadcast([P, 4, P]), ALU.mult), r=[p5k, 'cst'], w=['attnT'])
        p6, p6k = self.ps()
        for h in range(4):
            S.op('pe', lambda e_, h=h, p6=p6: e_.matmul(p6[0:64, h * nch:(h + 1) * nch], sp[:, h * 64:(h + 1) * 64], csel[:, 0:nch], start=True, stop=True),
                 r=['g_sp', 'cst'], w=[p6k])
        S.op('act', lambda e_, p6=p6: e_.activation(ebl[:, 0:4 * nch], p6[0:64, 0:4 * nch], AF.Exp), r=[p6k], w=['ebl'])
        p7, p7k = self.ps()
        for h in range(4):
            S.op('pe', lambda e_, h=h, p7=p7: e_.matmul(p7[:, h * P:(h + 1) * P], va[:, h * 128:(h + 1) * 128], attnT[0:P, h, 0:P], start=True, stop=True),
                 r=['va_tok', 'attnT'], w=[p7k])
        p8, p8k = self.ps()
        for c in range(nch):
            if kind == 's':
                S.dma('sp', Sst[:], self.sgla[e, c].rearrange("h k v -> k h v"), w=['Sst'])
                S.op('act', lambda e_: e_.activation(Sbf[:], Sst[:], AF.Copy), r=['Sst'], w=['Sbf'])
            for h in range(4):
                S.op('pe', lambda e_, h=h, c=c, p8=p8: e_.matmul(p8[:, h * P + c * Cs:h * P + (c + 1) * Cs], Sbf[:, h, :], qeT[:, h, c * Cs:(c + 1) * Cs],
                     start=True, stop=True), r=['Sbf', 'qeT'], w=[p8k])
            S.op('dve', lambda e_, c=c: e_.tensor_scalar(kdm, kd, sel01[:, c:c + 1], None, ALU.mult), r=['g_kd', 'cst'], w=['g_kdm'])
            p9, p9k = self.ps()
            for h in range(4):
                S.op('pe', lambda e_, h=h, p9=p9: e_.matmul(p9[0:64, h * 128:(h + 1) * 128], kdm[:, h * 64:(h + 1) * 64], va[:, h * 128:(h + 1) * 128],
                     start=True, stop=True), r=['g_kdm', 'va_tok'], w=[p9k])
            for h in range(4):
                S.op('dve', lambda e_, h=h, c=c, p9=p9: e_.scalar_tensor_tensor(Sst[:, h, :], Sst[:, h, :], ebl[:, h * nch + c:h * nch + c + 1],
                     p9[0:64, h * 128:(h + 1) * 128], ALU.mult, ALU.add), r=[p9k, 'Sst', 'ebl'], w=['Sst'])
            S.op('act', lambda e_: e_.activation(Sbf[:], Sst[:], AF.Copy), r=['Sst'], w=['Sbf'])
            if kind == 's':
                S.dma('sp', self.gs[e, c].rearrange("h k v -> k h v"), Sst[:], r=['Sst'], out=True)
        if kind == 'p' and ti == cfg.ntt - 1 and t4 == n // P - 1:
            S.dma('sp', self.gp[e].rearrange("h k v -> k h v"), Sst[:], r=['Sst'], out=True)
        inter, oa, sq2, rs2 = self.g_inter, self.g_oa, self.g_sq2, self.g_rs2
        W4 = 4 * P
        S.op('act', lambda e_, p8=p8: e_.activation(inter[:, 0:W4], p8[:, 0:W4], AF.Copy), r=[p8k], w=['g_inter'])
        S.op('dve', lambda e_, p7=p7: e_.tensor_tensor(oa[:, 0:W4], p7[:, 0:W4], inter[:, 0:W4], ALU.add), r=[p7k, 'g_inter'], w=['g_oa'])
        S.op('act', lambda e_: e_.activation(sq2[:, 0:W4], oa[:, 0:W4], AF.Square), r=['g_oa'], w=['g_sq2'])
        p10, p10k = self.ps()
        S.op('pe', lambda e_, p10=p10: e_.matmul(p10[:, 0:W4], self.ones_bf[:], sq2[:, 0:W4], start=True, stop=True), r=['g_sq2', 'ones'], w=[p10k])
        S.op('act', lambda e_, p10=p10: e_.activation(rs2[:, 0:W4], p10[:, 0:W4], AF.Ln, scale=1.0 / 128, bias=EPS), r=[p10k], w=['g_rs2'])
        S.op('act', lambda e_: e_.activation(rs2[:, 0:W4], rs2[:, 0:W4], AF.Exp, scale=-0.5), r=['g_rs2'], w=['g_rs2'])
        S.op('dve', lambda e_: e_.tensor_tensor(oa[:, 0:W4], oa[:, 0:W4], rs2[:, 0:W4], ALU.mult), r=['g_oa', 'g_rs2'], w=['g_oa'])
        gn = self.vecs_sb[:, self.NV - 2 + e:self.NV - 1 + e]
        S.op('dve', lambda e_: e_.scalar_tensor_tensor(self.mixT[:, 0:4, cols], oa[:, 0:W4].rearrange("p (h t) -> p h t", h=4), gn,
             self.gT[:, :, cols], ALU.mult, ALU.mult), r=['g_oa', 'gT', 'vecs'], w=['mixT'])

    def fox_prompt(self, e, ti, n):
        S = self.S
        scale = 128.0 ** -0.5
        njt = (ti + 1) * n // 128
        bT = self.biasT
        S.op('dve', lambda e_: e_.tensor_tensor(bT[:, 0:njt, :], self.Fref[:].unsqueeze(1).to_broadcast([128, njt, 4]), self.Fcol[:, 0:njt, :], ALU.subtract),
             r=['Fref', 'Fcol'], w=['biasT'])
        S.op('dve', lambda e_: e_.tensor_scalar(bT[:, 0:njt, :], bT[:, 0:njt, :], -SHIFT_C, None, ALU.add), r=['biasT'], w=['biasT'])
        for h in range(4):
            po, pok = self.pst[6], 'ps6'
            pl, plk = self.pst[7], 'ps7'
            for j in range(njt):
                c0 = max(0, j * 128 - ti * n)
                diag = j * 128 >= ti * n
                pt, pk = self.ps()
                PT, PTk = (self.PTa, 'PTa') if j % 2 == 0 else (self.PTb, 'PTb')
                S.op('pe', lambda e_, h=h, j=j, c0=c0, pt=pt: e_.matmul(pt[:, c0:n], self.KT[:, h, j * 128:(j + 1) * 128], self.qbT[:, h, c0:n], start=True, stop=True),
                     r=['KT', 'qbT'], w=[pk])
                S.op('act', lambda e_, h=h, j=j, c0=c0, pt=pt, PT=PT: e_.activation(PT[:, c0:n], pt[:, c0:n], AF.Exp, bias=bT[:, j, h:h + 1], scale=scale),
                     r=[pk, 'biasT'], w=[PTk])
                if diag:
                    S.op('dve', lambda e_, c0=c0, PT=PT: e_.tensor_tensor(PT[:, c0:c0 + 128], PT[:, c0:c0 + 128], self.triU_bf[:], ALU.mult), r=[PTk, 'triUbf'], w=[PTk])
                last = (j == njt - 1)
                S.op('pe', lambda e_, j=j, c0=c0, pl=pl, PT=PT, last=last: e_.matmul(pl[:, c0:n], self.ones_bf[:], PT[:, c0:n], start=(j == 0), stop=last),
                     r=[PTk, 'ones'], w=[plk])
                S.op('pe', lambda e_, h=h, j=j, c0=c0, po=po, PT=PT, last=last: e_.matmul(po[:, c0:n], self.Vst[:, j, h * 128:(h + 1) * 128], PT[:, c0:n], start=(j == 0), stop=last),
                     r=[PTk, 'Vst'], w=[pok])
            S.op('dve', lambda e_, pl=pl: e_.reciprocal(self.rl[:, :n], pl[:, :n]), r=[plk], w=['rl'])
            S.op('dve', lambda e_, h=h, po=po: e_.tensor_tensor(self.mixT[:, 4 + h, :n], po[:, :n], self.rl[:, :n], ALU.mult), r=[pok, 'rl'], w=['mixT'])

    def fox_sample_past(self, e):
        S, cfg = self.S, self.cfg
        NPG, NB = cfg.npg, cfg.nbs
        scale = 128.0 ** -0.5
        G = min(self.GS, NPG)
        psO, psOk = self.pst[6], 'ps6'
        psL, psLk = self.pst[7], 'ps7'
        lfpg, sa, sb_, RHs, RHc, s_sb, PTs = self.s_lfpg, self.s_sa, self.s_sb, self.s_RHs, self.s_RHc, self.s_ssb, self.s_PTs
        for b in range(NB):
            S.op('pool', lambda e_, b=b: e_.indirect_dma_start(out=lfpg[0:NPG, :], out_offset=None, in_=self.lfp[e],
                 in_offset=bass.IndirectOffsetOnAxis(ap=self.idx_pg[0:NPG, b:b + 1], axis=0)), r=['idx_pg'], w=['s_lfpg'], dma=True)
            X, Xk = lfpg, 's_lfpg'
            bufs = [(sa, 's_sa'), (sb_, 's_sb')]
            for st in range(7):
                sh = 1 << st
                Y, Yk = bufs[st % 2]
                xv = X[0:NPG, :].rearrange("p (r h) -> p r h", h=4)
                yv = Y[0:NPG, :].rearrange("p (r h) -> p r h", h=4)
                S.op('dve', lambda e_, xv=xv, yv=yv, sh=sh: e_.tensor_tensor(yv[:, 0:128 - sh, :], xv[:, 0:128 - sh, :], xv[:, sh:128, :], ALU.add), r=[Xk], w=[Yk])
                S.op('dve', lambda e_, xv=xv, yv=yv, sh=sh: e_.tensor_copy(yv[:, 128 - sh:128, :], xv[:, 128 - sh:128, :]), r=[Xk], w=[Yk])
                X, Xk = Y, Yk
            pl, plk = self.ps()
            S.op('pe', lambda e_, pl=pl, X=X: e_.matmul(pl[0:NPG, 0:4], self.cs('sufU', NPG, NPG), X[0:NPG, 0:4], start=True, stop=True), r=[Xk, 'cst'], w=[plk])
            S.op('act', lambda e_, pl=pl: e_.activation(self.s_later[0:NPG, :], pl[0:NPG, 0:4], AF.Copy), r=[plk], w=['s_later'])
            S.op('dve', lambda e_, X=X: e_.tensor_tensor(RHs[0:NPG, :], X[0:NPG, :], lfpg[0:NPG, :], ALU.subtract), r=[Xk, 's_lfpg'], w=['s_RHs'])
            rv = RHs[0:NPG, :].rearrange("p (r h) -> p r h", h=4)
            S.op('dve', lambda e_, rv=rv: e_.tensor_tensor(rv, rv, self.s_later[0:NPG, :].unsqueeze(1).to_broadcast([NPG, 128, 4]), ALU.add),
                 r=['s_RHs', 's_later'], w=['s_RHs'])
            pr, prk = self.ps()
            for h in range(4):
                S.op('pe', lambda e_, h=h, pr=pr, rv=rv: e_.transpose(pr[:, h * NPG:(h + 1) * NPG], rv[:, :, h], self.cs('ident', NPG, NPG)), r=['s_RHs', 'cst'], w=[prk])
            S.op('act', lambda e_, pr=pr: e_.activation(RHc[:, 0:4 * NPG], pr[:, 0:4 * NPG], AF.Copy), r=[prk], w=['s_RHc'])
            psa, psak = self.ps()
            psb, psbk = self.ps()
            for s0 in range(0, NPG, G):
                for g in range(G):
                    slot = s0 + g
                    S.op('pool', lambda e_, g=g, slot=slot, b=b: e_.indirect_dma_start(out=self.s_Kst[:, g, :], out_offset=None, in_=self.ktp[e],
                         in_offset=bass.IndirectOffsetOnAxis(ap=self.idx[:, b * NPG + slot:b * NPG + slot + 1], axis=0)), r=['idx'], w=['s_Kst%d' % g], dma=True)
                    eng = 'act' if g % 2 == 0 else 'dve'
                    if eng == 'act':
                        S.op('act', lambda e_, g=g: e_.activation(self.s_Kbf[:, g, :], self.s_Kst[:, g, :], AF.Copy), r=['s_Kst%d' % g], w=['s_Kbf%d' % g])
                    else:
                        S.op('dve', lambda e_, g=g: e_.tensor_copy(self.s_Kbf[:, g, :], self.s_Kst[:, g, :]), r=['s_Kst%d' % g], w=['s_Kbf%d' % g])
                    for h in range(4):
                        pp, ppk = (psa, psak) if h < 2 else (psb, psbk)
                        c0 = ((h % 2) * NPG + slot) * 4
                        S.op('pe', lambda e_, g=g, h=h, b=b, pp=pp, c0=c0: e_.matmul(pp[:, c0:c0 + 4], self.s_Kbf[:, g, h * 128:(h + 1) * 128], self.qbT[:, h, b * 4:(b + 1) * 4],
                             start=True, stop=True), r=['s_Kbf%d' % g, 'qbT'], w=[ppk])
            for h in range(4):
                pp, ppk = (psa, psak) if h < 2 else (psb, psbk)
                c0 = (h % 2) * NPG * 4
                S.op('dve', lambda e_, h=h, pp=pp, c0=c0: e_.scalar_tensor_tensor(s_sb[:, 0:NPG * 4].rearrange("p (s q) -> p s q", q=4),
                     pp[:, c0:c0 + NPG * 4].rearrange("p (s q) -> p s q", q=4), scale,
                     RHc[:, h * NPG:(h + 1) * NPG].unsqueeze(2).to_broadcast([128, NPG, 4]), ALU.mult, ALU.add), r=[ppk, 's_RHc'], w=['s_ssb'])
                S.op('act', lambda e_, h=h: e_.activation(PTs[:, h, 0:NPG * 4], s_sb[:, 0:NPG * 4], AF.Exp, bias=-SHIFT_C), r=['s_ssb'], w=['s_PTs'])
            for s0 in range(0, NPG, G):
                for g in range(G):
                    slot = s0 + g
                    S.op('pool', lambda e_, g=g, slot=slot, b=b: e_.indirect_dma_start(out=self.s_Vst[:, g, :], out_offset=None, in_=self.vp[e],
                         in_offset=bass.IndirectOffsetOnAxis(ap=self.idx[:, b * NPG + slot:b * NPG + slot + 1], axis=0)), r=['idx'], w=['s_Vst%d' % g], dma=True)
                    if g % 2 == 0:
                        S.op('act', lambda e_, g=g: e_.activation(self.s_Vbf[:, g, :], self.s_Vst[:, g, :], AF.Copy), r=['s_Vst%d' % g], w=['s_Vbf%d' % g])
                    else:
                        S.op('dve', lambda e_, g=g: e_.tensor_copy(self.s_Vbf[:, g, :], self.s_Vst[:, g, :]), r=['s_Vst%d' % g], w=['s_Vbf%d' % g])
                    for h in range(4):
                        c0 = (h * 4 + b) * 4
                        S.op('pe', lambda e_, g=g, h=h, slot=slot, c0=c0, b=b: e_.matmul(psO[:, c0:c0 + 4], self.s_Vbf[:, g, h * 128:(h + 1) * 128], PTs[:, h, slot * 4:(slot + 1) * 4],
                             start=(slot == 0 and h == 0 and b == 0), stop=(slot == NPG - 1 and h == 3 and b == NB - 1)), r=['s_Vbf%d' % g, 's_PTs'], w=[psOk])
                        S.op('pe', lambda e_, h=h, slot=slot, c0=c0, b=b: e_.matmul(psL[:, c0:c0 + 4], self.ones_bf[:], PTs[:, h, slot * 4:(slot + 1) * 4],
                             start=(slot == 0 and h == 0 and b == 0), stop=(slot == NPG - 1 and h == 3 and b == NB - 1)), r=['s_PTs', 'ones'], w=[psLk])
        S.op('act', lambda e_: e_.activation(self.Opast[:], psO[:, 0:64], AF.Copy), r=[psOk], w=['Opast'])
        S.op('act', lambda e_: e_.activation(self.Lpast[:], psL[:, 0:64], AF.Copy), r=[psLk], w=['Lpast'])

    def fox_sample(self, e):
        S, cfg = self.S, self.cfg
        scale = 128.0 ** -0.5
        NSK = cfg.ns
        ptn, ptnk = self.ps()
        for h in range(4):
            S.op('pe', lambda e_, h=h: e_.matmul(ptn[0:NSK, h * NSK:(h + 1) * NSK], self.KTs[:, h, 0:NSK], self.qbT[:, h, 0:NSK], start=True, stop=True),
                 r=['KTs', 'qbT'], w=[ptnk])
        Pnf, Pn = self.Pnf, self.Pn
        for h in range(4):
            S.op('act', lambda e_, h=h: e_.activation(Pnf[0:NSK, h, :], ptn[0:NSK, h * NSK:(h + 1) * NSK], AF.Exp, bias=self.Fn[0:NSK, h:h + 1], scale=scale),
                 r=[ptnk, 'Fn'], w=['Pnf'])
        S.op('dve', lambda e_: e_.tensor_tensor(Pn[0:NSK, :, :], Pnf[0:NSK, :, :], self.cs('maskCs', NSK).unsqueeze(1).to_broadcast([NSK, 4, NSK]), ALU.mult),
             r=['Pnf', 'cst'], w=['Pn'])
        pon, ponk = self.ps()
        pln, plnk = self.ps()
        for h in range(4):
            S.op('pe', lambda e_, h=h: e_.matmul(pon[:, h * NSK:(h + 1) * NSK], self.Vs[0:NSK, h * 128:(h + 1) * 128], Pn[0:NSK, h, :], start=True, stop=True),
                 r=['Vs', 'Pn'], w=[ponk])
            S.op('pe', lambda e_, h=h: e_.matmul(pln[:, h * NSK:(h + 1) * NSK], self.ones_bf[0:NSK, :], Pn[0:NSK, h, :], start=True, stop=True),
                 r=['ones', 'Pn'], w=[plnk])
        Ot, Lt = self.Ot, self.Lt
        S.op('dve', lambda e_: e_.tensor_tensor(Ot[:], pon[:, 0:64], self.Opast[:], ALU.add), r=[ponk, 'Opast'], w=['Ot'])
        S.op('dve', lambda e_: e_.tensor_tensor(Lt[:], pln[:, 0:64], self.Lpast[:], ALU.add), r=[plnk, 'Lpast'], w=['Lt'])
        S.op('dve', lambda e_: e_.reciprocal(Lt[:], Lt[:]), r=['Lt'], w=['Lt'])
        S.op('dve', lambda e_: e_.tensor_tensor(self.mixT[:, 4:8, 0:NSK], Ot[:].rearrange("p (h t) -> p h t", h=4), Lt[:].rearrange("p (h t) -> p h t", h=4), ALU.mult),
             r=['Ot', 'Lt'], w=['mixT'])

    def build(self):
        import contextlib
        nc, S, cfg = self.nc, self.S, self.cfg
        T, L, TT, NS, NB, NPG = cfg.seq, cfg.depth, cfg.tt, cfg.ns, cfg.nbs, cfg.npg
        di = lambda n, s, d=F32: nc.dram_tensor(n, s, d, kind="ExternalInput").ap()
        self.lfp = [di("lfp%d" % i, [cfg.npool, 512]) for i in range(2)]
        self.vp = [di("vp%d" % i, [cfg.npool * 128, 512]) for i in range(2)]
        self.ptabT = di("ptabT", [NPG, NB], I32)
        st = contextlib.ExitStack()
        sb = lambda name, shape, dt: st.enter_context(nc.sbuf_tensor(name, shape, dt))
        self.xT = sb("xT", [128, KC, T], F32); self.xsT = sb("xsT", [128, KC, NS], F32)
        self.hT = sb("hT", [128, KC, TT], BF16); self.sq = sb("sq", [128, KC, TT], BF16); self.rstd = sb("rstd", [128, TT], F32)
        self.wbufs = [sb("wbuf%d" % i, [128, 4096], BF16) for i in range(4)]
        self.cst_sb = sb("cst_sb", [128, CST_W], F32); self.vecs_sb = sb("vecs_sb", [128, self.NV], F32)
        self.ones_bf = sb("ones_bf", [128, 128], BF16); self.ones_f = sb("ones_f", [128, 128], F32); self.triU_bf = sb("triU_bf", [128, 128], BF16)
        self.KT = sb("KT", [128, 4, T], BF16); self.Vst = sb("Vst", [128, T // 128, 512], BF16)
        self.Fcol = sb("Fcol", [128, T // 128, 4], F32); self.carry = sb("carry", [128, 4], F32); self.Fref = sb("Fref", [128, 4], F32)
        self.bfg = sb("bfg", [128, 4], F32); self.wg_aug = sb("wg_aug", [32, 256], BF16); self.wsmall = sb("wsmall", [128, KC, 20], BF16)
        self.Sst = sb("Sst", [64, 4, 128], F32); self.Sbf = sb("Sbf", [64, 4, 128], BF16)
        self.KTs = sb("KTs", [128, 4, NS], BF16); self.Vs = sb("Vs", [NS, 512], BF16); self.Fn = sb("Fn", [NS, 4], F32)
        self.Opast = sb("Opast", [128, 64], F32); self.Lpast = sb("Lpast", [128, 64], F32); self.Ot = sb("Ot", [128, 64], F32); self.Lt = sb("Lt", [128, 64], F32)
        self.Pnf = sb("Pnf", [NS, 4, NS], F32); self.Pn = sb("Pn", [NS, 4, NS], BF16)
        self.idx = sb("idx", [128, NB * NPG], I32); self.idx_f = sb("idx_f", [128, NB * NPG], F32); self.idx_pg = sb("idx_pg", [NPG, NB], I32)
        self.ebl = sb("ebl", [64, 16], F32); self.lf = sb("lf", [128, max(1, TT // 128), 4], F32); self.biasT = sb("biasT", [128, T // 128, 4], F32)
        self.dummy = sb("dummy_t", [128, 1], F32)
        UFW = max(15 + TT, NB * 19)
        self.UW = 9984
        GS = self.GS = 4
        self.U = sb("U", [128, self.UW], F32)
        self.pst = [st.enter_context(nc.psum_tensor("ps%d" % i, [128, 512], F32)) for i in range(8)]
        self.uoff = 0
        self.aT = self.carve('aT', 16 * TT, BF16, "p (f t) -> p f t", f=FC)
        self.uF = self.carve('uF', 8 * UFW, F32, "p (k l) -> p k l", k=KC)
        self.wsA = self.carve('wsA', 2 * UFW, F32, "p (k l) -> p k l", k=2)
        self.wsB = self.carve('wsB', 2 * UFW, F32, "p (k l) -> p k l", k=2)
        self.tio = self.carve('tio', 1024, F32)
        self.relu = self.carve('relu', TT // 2, BF16)
        self.uoff = 0
        self.qbT = self.carve('qbT', 2 * TT, BF16, "p (h t) -> p h t", h=4)
        e_start = self.uoff
        NTK = max(1, TT // 128)
        self.qk_tok = self.carve('qk_tok', NTK * 512, F32, "p (t c) -> p t c", c=512)
        self.va_tok = self.carve('va_tok', NTK * 256, BF16, "p (t c) -> p t c", c=512)
        self.gT = self.carve('gT', 2 * TT, BF16, "p (h t) -> p h t", h=4)
        self.mixT = self.carve('mixT', 4 * TT, BF16, "p (h t) -> p h t", h=8)
        self.raug = self.carve('raug', TT // 2, BF16)
        self.g_sp, self.g_b, self.g_eb, self.g_enb, self.g_edb, self.g_qe, self.g_ke = (self.carve(k, 256, F32) for k in ('g_sp', 'g_b', 'g_eb', 'g_enb', 'g_edb', 'g_qe', 'g_ke'))
        self.g_kd = self.carve('g_kd', 128, BF16); self.g_kdm = self.carve('g_kdm', 128, BF16)
        self.attnT = self.carve('attnT', 256, BF16, "p (h t) -> p h t", h=4)
        self.qeT = self.carve('qeT', 256, BF16, "p (h t) -> p h t", h=4)[0:64]
        self.keT = self.carve('keT', 256, BF16, "p (h t) -> p h t", h=4)[0:64]
        self.g_inter = self.carve('g_inter', 512, F32); self.g_oa = self.carve('g_oa', 512, F32); self.g_rs2 = self.carve('g_rs2', 512, F32)
        self.stg = self.g_oa
        self.g_sq2 = self.carve('g_sq2', 256, BF16)
        self.PTa = self.carve('PTa', TT // 2, BF16); self.PTb = self.carve('PTb', TT // 2, BF16); self.rl = self.carve('rl', TT, F32)
        self.uoff = e_start
        self.s_lfpg = self.carve('s_lfpg', 512, F32); self.s_sa = self.carve('s_sa', 512, F32); self.s_sb = self.carve('s_sb', 512, F32)
        self.s_RHs = self.carve('s_RHs', 512, F32); self.s_RHc = self.carve('s_RHc', 4 * NPG, F32); self.s_ssb = self.carve('s_ssb', 4 * NPG, F32)
        self.s_PTs = self.carve('s_PTs', 8 * NPG, BF16, "p (h c) -> p h c", h=4)
        self.s_later = self.carve('s_later', 4, F32)
        self.s_Kst = self.carve('s_Kst', 512 * GS, F32, "p (g c) -> p g c", g=GS); self.s_Kbf = self.carve('s_Kbf', 256 * GS, BF16, "p (g c) -> p g c", g=GS)
        self.s_Vst = self.carve('s_Vst', 512 * GS, F32, "p (g c) -> p g c", g=GS); self.s_Vbf = self.carve('s_Vbf', 256 * GS, BF16, "p (g c) -> p g c", g=GS)
        for g in range(GS):
            self.ukeys += ['s_Kst%d' % g, 's_Kbf%d' % g, 's_Vst%d' % g, 's_Vbf%d' % g]
        xT, xsT = self.xT, self.xsT
        S.dma('sp', self.cst_sb[:], self.cst, w=['cst'])
        S.dma('sp', self.vecs_sb[:], self.vecs, w=['vecs'])
        S.op('dve', lambda e: e.memset(self.ones_bf[:], 1.0), w=['ones'])
        S.op('dve', lambda e: e.memset(self.ones_f[:], 1.0), w=['ones'])
        S.op('dve', lambda e: e.tensor_copy(self.triU_bf[:], self.cs('triU')), r=['cst'], w=['triUbf'])
        S.dma('sp', self.idx_pg[:], self.ptabT, w=['idx_pg'])
        S.dma('sp', self.idx[:], self.ptab.partition_broadcast(128), w=['idx'])
        S.op('dve', lambda e: e.tensor_copy(self.idx_f[:], self.idx[:]), r=['idx'], w=['idx_f'])
        S.op('dve', lambda e: e.scalar_tensor_tensor(self.idx_f[:], self.idx_f[:], 128.0, self.cs('iota').to_broadcast([128, NB * NPG]), ALU.mult, ALU.add),
             r=['idx_f', 'cst'], w=['idx_f'])
        S.op('dve', lambda e: e.tensor_copy(self.idx[:], self.idx_f[:]), r=['idx_f'], w=['idx'])
        for tt in range(T // 128):
            self.load_tokens(self.xp[tt * 128:(tt + 1) * 128, :], 128, xT[:, :, tt * 128:(tt + 1) * 128], 'x%d' % (tt * 128 // TT))
        self.load_tokens(self.xs, NS, xsT[:, :, :], 'xs')
        xt = lambda ti: (xT[:, :, ti * TT:(ti + 1) * TT], 'x%d' % ti)
        for l in range(L):
            if l % 2 == 0 and 'noeven' in cfg.flags:
                for ti in range(cfg.ntt):
                    x, xk = xt(ti)
                    self.mlp(l, x, xk, TT)
                continue
            if l % 2 == 0:
                e = l // 2
                S.dma('pool', self.wsmall[:, :, 0:16], self.w_in[e].rearrange("(kc p) f -> p kc f", p=128)[:, :, C_RA:C_RA + 16], w=['wsmall'])
                S.dma('pool', self.wsmall[:, :, 16:20], self.w_in[e].rearrange("(kc p) f -> p kc f", p=128)[:, :, C_FB:C_FB + 4], w=['wsmall'])
                S.dma('pool', self.wg_aug[0:16, :], self.w_gate[e], w=['wg'])
                S.dma('pool', self.wg_aug[16:17, :], self.b_gate[e], w=['wg'])
                S.dma('sp', self.bfg[:], self.b_forget[e].partition_broadcast(128), w=['bfg'])
                S.op('dve', lambda e_: e_.memset(self.carry[:], 0.0), w=['carry'])
                S.op('dve', lambda e_: e_.memset(self.Sst[:], 0.0), w=['Sst'])
                S.op('dve', lambda e_: e_.memset(self.Sbf[:], 0.0), w=['Sbf'])
                for ti in range(cfg.ntt):
                    x, xk = xt(ti)
                    self.fence()
                    self.even_tile(e, 'p', x, xk, TT, ti)
                    self.fence()
                    self.mlp(l, x, xk, TT)
                if 'sample' in cfg.flags:
                    self.fence()
                    self.even_tile(e, 's', xsT[:, :, :], 'xs', NS, 0)
                    self.fence()
                    self.mlp(l, xsT[:, :, :], 'xs', NS)
            else:
                o = l // 2
                self.fence()
                S.op('dve', lambda e_: e_.memset(self.uF[:, :, 0:15], 0.0), w=['uF'])
                for ti in range(cfg.ntt):
                    x, xk = xt(ti)
                    self.pool_mix(o, x, xk, TT, first=(ti == 0))
                    if ti == cfg.ntt - 1:
                        self.transpose_out(self.uF[:, :, TT:TT + 15], 'uF', 15, self.npool[o], 15)
                    else:
                        S.op('dve', lambda e_: e_.tensor_copy(self.uF[:, :, 0:15], self.uF[:, :, TT:TT + 15]), r=['uF'], w=['uF'])
                    self.mlp(l, x, xk, TT)
                if 'sample' not in cfg.flags:
                    continue
                uvs = self.uF[:, :, 0:NB * 19].rearrange("p k (b l) -> p k b l", b=NB)
                S.dma('sp', self.tio[0:NB * 15, :], self.spool[o].rearrange("b r d -> (b r) d"), w=['tio'])
                for kc in range(KC):
                    pt, pk = self.ps()
                    S.op('pe', lambda e_, kc=kc, pt=pt: e_.transpose(pt[:, 0:NB * 15], self.tio[0:NB * 15, kc * 128:(kc + 1) * 128], self.cs('ident', NB * 15, NB * 15)),
                         r=['tio', 'cst'], w=[pk])
                    S.op('act', lambda e_, kc=kc, pt=pt: e_.activation(uvs[:, kc, :, 0:15], pt[:, 0:NB * 15].rearrange("p (b r) -> p b r", b=NB), AF.Copy), r=[pk], w=['uF'])
                S.dma('sp', self.nps[o, :, 0:11, :], self.spool[o, :, 4:15, :], out=True)
                self.pool_mix(o, xsT[:, :, :], 'xs', NS, first=False, nb=NB)
                tmp = self.wsA[:, :, :].rearrange("p k l -> p (k l)")[:, 0:KC * NS].rearrange("p (k t) -> p k t", k=KC)
                for kc in range(KC):
                    S.op('dve', lambda e_, kc=kc: e_.tensor_copy(tmp[:, kc, :].rearrange("p (b t) -> p b t", b=NB), uvs[:, kc, :, 15:19]), r=['uF'], w=['wsA'])
                for kc in range(KC):
                    pt, pk = self.ps()
                    S.op('pe', lambda e_, kc=kc, pt=pt: e_.transpose(pt[0:NS, 0:128], tmp[:, kc, :], self.cs('ident')), r=['wsA', 'cst'], w=[pk])
                    S.op('act', lambda e_, kc=kc, pt=pt: e_.activation(self.tio[0:NS, kc * 128:(kc + 1) * 128], pt[0:NS, 0:128], AF.Copy), r=[pk], w=['tio'])
                for b in range(NB):
                    S.dma('sp', self.nps[o, b, 11:15, :], self.tio[b * 4:(b + 1) * 4, :], r=['tio'], out=True)
                self.mlp(l, xsT[:, :, :], 'xs', NS)
        self.fence()
        for ti in range(cfg.ntt):
            x, xk = xt(ti)
            self.rmsnorm(x, xk, TT, 2 * L, self.uF[:, :, 0:TT], 'uF')
            for t4 in range(TT // 128):
                r0 = ti * TT + t4 * 128
                self.transpose_out(self.uF[:, :, t4 * 128:(t4 + 1) * 128], 'uF', 128, self.y[r0:r0 + 128, :], 128)
        self.rmsnorm(xsT[:, :, :], 'xs', NS, 2 * L, self.uF[:, :, 0:NS], 'uF')
        self.transpose_out(self.uF[:, :, 0:NS], 'uF', NS, self.ys, NS)
        S.emit()
        print("sbuf bytes remaining:", nc.sbuf_bytes_remaining, "ops:", len(S.ops))
        st.close()
        return nc


def shared_inputs(cfg, inp):
    L = cfg.depth
    f = lambda a: np.ascontiguousarray(np.asarray(a), dtype=np.float32)
    cols = [inp["norm_mix"][l] for l in range(L)] + [inp["norm_mlp"][l] for l in range(L)] + [inp["norm_final"]] + \
           [inp["pool_scale"][o] for o in range(2)]
    vecs = np.concatenate([np.asarray(c, np.float32).reshape(KC, 128).T for c in cols] + [np.asarray(inp["gla_norm"], np.float32).T], axis=1)
    ck = np.asarray(inp["cache_fox_k"], np.float32)
    npool = ck.shape[1]
    ktp = np.ascontiguousarray(ck.transpose(0, 1, 4, 3, 2)).reshape(2, npool * 128, 512)
    vp = f(inp["cache_fox_v"]).reshape(2, npool * 128, 512)
    lfp = f(inp["cache_fox_logf"]).reshape(2, npool, 512)
    return dict(cst=make_consts(), vecs=np.ascontiguousarray(vecs), w_in=f(inp["w_in_even"]), w_out=f(inp["w_out_even"]),
                w_gate=f(inp["w_gate_up"]), b_gate=f(inp["b_gate"]).reshape(2, 1, 256), b_forget=f(inp["b_forget"]).reshape(2, 1, 4),
                w_up=f(inp["w_mlp_up"]), w_down=f(inp["w_mlp_down"]), w_pool=f(inp["w_pool"]),
                ktp0=ktp[0], ktp1=ktp[1], vp0=vp[0], vp1=vp[1], lfp0=lfp[0], lfp1=lfp[1])


def core_inputs(cfg, c, inp, shared):
    NB = cfg.nbs
    f = lambda a: np.ascontiguousarray(np.asarray(a), dtype=np.float32)
    pt = np.asarray(inp["page_table"])[NB * c:NB * (c + 1)].astype(np.int32)
    m = dict(shared)
    m.update(xp=f(inp["x_prompt"][c]), xs=f(np.asarray(inp["x_sample"])[NB * c:NB * (c + 1)]).reshape(cfg.ns, D),
             ptab=np.ascontiguousarray(pt.reshape(1, -1)), ptabT=np.ascontiguousarray(pt.T),
             sgla=f(np.asarray(inp["state_gla"])[:, NB * c:NB * (c + 1)]), spool=f(np.asarray(inp["state_pool"])[:, NB * c:NB * (c + 1)]))
    return m


def assemble(cfg, results, B, Bs):
    NB = cfg.nbs
    T = cfg.seq
    r = results
    cat = lambda k, ax: np.concatenate([r[c][k] for c in range(len(r))], axis=ax)
    y_prompt = np.stack([r[c]["y"] for c in range(B)], 0)
    y_sample = np.concatenate([r[c]["ys"].reshape(NB, 4, D) for c in range(B)], 0)
    fk = np.stack([r[c]["fk"].reshape(2, T, 4, 128) for c in range(B)], 1)
    fv = np.stack([r[c]["fv"].reshape(2, T, 4, 128) for c in range(B)], 1)
    fl = np.stack([r[c]["fl"] for c in range(B)], 1)
    fks = np.concatenate([r[c]["fks"].reshape(2, NB, 4, 4, 128) for c in range(B)], 1)
    fvs = np.concatenate([r[c]["fvs"].reshape(2, NB, 4, 4, 128) for c in range(B)], 1)
    fls = np.concatenate([r[c]["fls"].reshape(2, NB, 4, 4) for c in range(B)], 1)
    gp = np.stack([r[c]["gp"] for c in range(B)], 1)
    gs = np.concatenate([r[c]["gs"] for c in range(B)], 1)
    npool = np.stack([r[c]["npool"] for c in range(B)], 1)
    nps = np.concatenate([r[c]["nps"] for c in range(B)], 1)
    return tuple(np.ascontiguousarray(a, dtype=np.float32) for a in (y_prompt, y_sample, fk, fv, fl, fks, fvs, fls, gp, gs, npool, nps))


_NC_CACHE = {}


def kernel(**inp):
    B, T, _ = np.asarray(inp["x_prompt"]).shape
    Bs = np.asarray(inp["x_sample"]).shape[0]
    npg = np.asarray(inp["page_table"]).shape[1]
    npool = np.asarray(inp["cache_fox_k"]).shape[1]
    cfg = Cfg(seq=T, depth=4, tt=256, npg=npg, npool=npool)
    key = (T, npg, npool)
    if key not in _NC_CACHE:
        _NC_CACHE[key] = Builder(cfg).build()
    nc = _NC_CACHE[key]
    shared = shared_inputs(cfg, inp)
    maps = [core_inputs(cfg, c, inp, shared) for c in range(B)]
    res = run_bass_kernel_spmd(nc, maps, core_ids=list(range(B)))
    return assemble(cfg, res.results, B, Bs)
```

```python
import numpy as np
import concourse.bass as bass
import concourse.mybir as mybir
from concourse.bass_utils import run_bass_kernel_spmd

F32 = mybir.dt.float32
BF16 = mybir.dt.bfloat16
I32 = mybir.dt.int32
AF = mybir.ActivationFunctionType
ALU = mybir.AluOpType


class Sched:
    ENGS = ('pe', 'act', 'dve', 'pool', 'sp')
    NDMA = 20

    def __init__(self, nc, same_engine_sync=True):
        self.nc = nc
        self.ops = []
        self.last_w = {}
        self.readers = {}
        self.same = same_engine_sync
        self.out_dmas = []
        self.dma_rr = {'sp': 0, 'pool': 0, 'act': 0}
        self.dma_last = {}

    def op(self, eng, fn, r=(), w=(), dma=False, out=False):
        idx = len(self.ops)
        w = list(w) + [k for k in r if isinstance(k, str) and k.startswith('ps') and k not in w]
        deps = set()
        for k in r:
            if k in self.last_w:
                deps.add(self.last_w[k])
        for k in w:
            if k in self.last_w:
                deps.add(self.last_w[k])
            deps |= self.readers.get(k, set())
        semslot = None
        if dma:
            slot = self.dma_rr[eng]
            self.dma_rr[eng] = (slot + 1) % self.NDMA
            semslot = (eng, slot)
            if semslot in self.dma_last:
                deps.add(self.dma_last[semslot])
            self.dma_last[semslot] = idx
        for k in r:
            self.readers.setdefault(k, set()).add(idx)
        for k in w:
            self.last_w[k] = idx
            self.readers[k] = set()
        self.ops.append(dict(eng=eng, fn=fn, deps=deps, dma=dma, semslot=semslot))
        if out:
            self.out_dmas.append(idx)
        return idx

    def dma(self, eng, out_ap, in_ap, r=(), w=(), out=False, **kw):
        return self.op(eng, lambda e: e.dma_start(out=out_ap, in_=in_ap, **kw), r=r, w=w, dma=True, out=out)

    def emit(self):
        nc = self.nc
        ops = self.ops
        self.ops.append(dict(eng='sp', fn=None, deps=set(self.out_dmas) | set(self.dma_last.values()), dma=False, semslot=None))
        import contextlib
        stack = contextlib.ExitStack()
        esem = {e: stack.enter_context(nc.semaphore("se_" + e)) for e in ('pe', 'act', 'dve', 'pool')}
        dsem = {}
        for e in ('sp', 'pool'):
            for s in range(self.NDMA):
                dsem[(e, s)] = stack.enter_context(nc.semaphore("sd_%s_%d" % (e, s)))
        ecount = {e: 0 for e in esem}
        dcount = {k: 0 for k in dsem}
        for o in ops:
            if o['dma']:
                dcount[o['semslot']] += 16
                o['sig'] = (dsem[o['semslot']], dcount[o['semslot']], o['semslot'])
            elif o['eng'] in esem and o['fn'] is not None:
                ecount[o['eng']] += 1
                o['sig'] = (esem[o['eng']], ecount[o['eng']], o['eng'])
            else:
                o['sig'] = None
        streams = {e: [] for e in self.ENGS}
        for i, o in enumerate(ops):
            streams[o['eng']].append(i)
        eobj = {'pe': 'tensor', 'act': 'scalar', 'dve': 'vector', 'pool': 'gpsimd', 'sp': 'sync'}

        def make(engname):
            def body(e):
                waited = {}
                for i in streams[engname]:
                    o = ops[i]
                    for d in sorted(o['deps']):
                        dd = ops[d]
                        if dd['sig'] is None:
                            continue
                        sem, val, key = dd['sig']
                        if (not dd['dma']) and dd['eng'] == engname and (engname == 'pe' or not self.same):
                            continue
                        if waited.get(key, 0) >= val:
                            continue
                        e.wait_ge(sem, val)
                        waited[key] = val
                    if o['fn'] is None:
                        continue
                    ins = o['fn'](e)
                    if o['sig'] is not None:
                        ins.then_inc(o['sig'][0], 16 if o['dma'] else 1)
            return body

        with nc.Block() as block:
            block.tensor(make('pe'))
            block.scalar(make('act'))
            block.vector(make('dve'))
            block.gpsimd(make('pool'))
            block.sync(make('sp'))
        stack.close()


D = 1024
DFF = 4096
KC = 8
FC = 32
EPS = 1e-6
NINC = 3092
C_QKA, C_VA, C_RA, C_GA, C_QB, C_KB, C_VB, C_FB = 0, 512, 1024, 1040, 1552, 2064, 2576, 3088
SHIFT_C = 8.0


class Cfg:
    def __init__(self, seq=2048, depth=4, tt=256, npg=64, npool=2560):
        self.seq = seq
        self.depth = depth
        self.tt = min(tt, seq)
        self.ntt = seq // self.tt
        self.npg = npg
        self.npool = npool
        self.nbs = 4
        import os
        self.flags = os.environ.get('KFLAGS', 'sample,gla,fox,past,pool,mlp')
        self.kstop = int(os.environ.get('KSTOP', '99'))
        self.ns = 16


def _cst_layout():
    names = [('ident', 128), ('triU', 128), ('sufU', 128), ('triC', 128), ('onesC', 128), ('maskC', 128),
             ('csel', 2), ('sel01', 2), ('triCs', 16), ('onesCs', 16), ('maskCs', 16), ('csels', 4), ('sel01s', 4),
             ('iota', 1), ('invc', 64)]
    off, o = {}, 0
    for n, w in names:
        off[n] = (o, w)
        o += w
    return off, o


CST_OFF, CST_W = _cst_layout()


def make_consts():
    c = np.zeros((128, CST_W), np.float32)
    def put(name, a):
        o, w = CST_OFF[name]
        c[:a.shape[0], o:o + a.shape[1]] = a
    i = np.arange(128)
    put('ident', np.eye(128, dtype=np.float32))
    put('triU', (i[:, None] <= i[None, :]).astype(np.float32))
    put('sufU', (i[:, None] > i[None, :]).astype(np.float32))
    same = (i[:, None] // 64) == (i[None, :] // 64)
    put('triC', np.where(same & (i[:, None] <= i[None, :]), -1.0 / 16, 0.0).astype(np.float32))
    put('onesC', np.where(same, -1.0 / 16, 0.0).astype(np.float32))
    put('maskC', (same & (i[:, None] <= i[None, :])).astype(np.float32))
    put('csel', np.where((i[:, None] // 64) == np.arange(2)[None, :], -1.0 / 16, 0.0).astype(np.float32))
    put('sel01', ((i[:, None] // 64) == np.arange(2)[None, :]).astype(np.float32))
    j = np.arange(16)
    sames = (j[:, None] // 4) == (j[None, :] // 4)
    put('triCs', np.where(sames & (j[:, None] <= j[None, :]), -1.0 / 16, 0.0).astype(np.float32))
    put('onesCs', np.where(sames, -1.0 / 16, 0.0).astype(np.float32))
    put('maskCs', (sames & (j[:, None] <= j[None, :])).astype(np.float32))
    put('csels', np.where((j[:, None] // 4) == np.arange(4)[None, :], -1.0 / 16, 0.0).astype(np.float32))
    put('sel01s', ((j[:, None] // 4) == np.arange(4)[None, :]).astype(np.float32))
    put('iota', i[:, None].astype(np.float32))
    pos = np.arange(16, dtype=np.float32) + 1.0
    inv = np.concatenate([1.0 / np.minimum(float(2 << g), pos) for g in range(4)])[None, :]
    put('invc', np.broadcast_to(inv, (128, 64)).astype(np.float32))
    return c


class Builder:
    def __init__(self, cfg):
        self.cfg = cfg
        nc = self.nc = bass.Bass("TRN2", target_bir_lowering=False)
        self.S = Sched(nc)
        T, L = cfg.seq, cfg.depth
        NS, NB, NPG = cfg.ns, cfg.nbs, cfg.npg
        di = lambda n, s, d=F32: nc.dram_tensor(n, s, d, kind="ExternalInput").ap()
        do = lambda n, s: nc.dram_tensor(n, s, F32, kind="ExternalOutput").ap()
        self.xp = di("xp", [T, D]); self.xs = di("xs", [NS, D])
        self.cst = di("cst", [128, CST_W])
        self.NV = (2 * L + 3) * KC + 2
        self.vecs = di("vecs", [128, self.NV])
        self.w_in = di("w_in", [2, D, NINC]); self.w_out = di("w_out", [2, D, D])
        self.w_gate = di("w_gate", [2, 16, 256]); self.b_gate = di("b_gate", [2, 1, 256]); self.b_forget = di("b_forget", [2, 1, 4])
        self.w_up = di("w_up", [L, D, DFF]); self.w_down = di("w_down", [L, DFF, D]); self.w_pool = di("w_pool", [2, 4, 256, 256])
        self.ktp = [di("ktp%d" % i, [cfg.npool * 128, 512]) for i in range(2)]
        self.ptab = di("ptab", [1, NB * NPG], I32)
        self.sgla = di("sgla", [2, NB, 4, 64, 128]); self.spool = di("spool", [2, NB, 15, D])
        self.y = do("y", [T, D]); self.ys = do("ys", [NS, D])
        self.fk = do("fk", [2, T, 512]); self.fv = do("fv", [2, T, 512]); self.fl = do("fl", [2, T, 4])
        self.fks = do("fks", [2, NS, 512]); self.fvs = do("fvs", [2, NS, 512]); self.fls = do("fls", [2, NS, 4])
        self.gp = do("gp", [2, 4, 64, 128]); self.gs = do("gs", [2, NB, 4, 64, 128])
        self.npool = do("npool", [2, 15, D]); self.nps = do("nps", [2, NB, 15, D])
        self.psn = 0
        self.wslot = 0
        self.ukeys = []
        self.scr = {}
        self.wq = 0
        self.scr_t = nc.dram_tensor("wscr", [16 * L + 16, 128, 4096], BF16, kind="Internal").ap()

    def ps(self):
        i = self.psn
        self.psn = (self.psn + 1) % 6
        return self.pst[i], 'ps%d' % i

    def wb(self):
        i = self.wslot
        self.wslot = (self.wslot + 1) % len(self.wbufs)
        return self.wbufs[i], 'wb%d' % i

    def vec(self, idx):
        return self.vecs_sb[:, idx * KC:(idx + 1) * KC]

    def cs(self, name, rows=128, cols=None):
        o, w = CST_OFF[name]
        return self.cst_sb[0:rows, o:o + (cols if cols is not None else w)]

    def fence(self):
        ks = list(self.ukeys)
        self.S.op('dve', lambda e: e.memset(self.dummy[:], 0.0), r=ks, w=ks + ['dummy'])

    def carve(self, key, words, dt, pattern=None, **kw):
        off = self.uoff
        self.uoff += words
        assert self.uoff <= self.UW, (key, self.uoff)
        ap = self.U[:, off:off + words]
        if dt != F32:
            ap = ap.bitcast(dt)
        if pattern:
            ap = ap.rearrange(pattern, **kw)
        if key not in self.ukeys:
            self.ukeys.append(key)
        return ap

    def rmsnorm(self, x, xk, n, gidx, out, outk):
        S = self.S
        sq, rs = self.sq, self.rstd
        g = self.vec(gidx)
        pt, pk = self.ps()
        for kc in range(KC):
            S.op('act', lambda e, kc=kc: e.activation(sq[:, kc, :n], x[:, kc, :], AF.Square), r=[xk], w=['sq'])
        for kc in range(KC):
            S.op('pe', lambda e, kc=kc: e.matmul(pt[:, :n], self.ones_bf[:], sq[:, kc, :n], start=(kc == 0), stop=(kc == KC - 1)),
                 r=['sq', 'ones'], w=[pk])
        S.op('act', lambda e: e.activation(rs[:, :n], pt[:, :n], AF.Ln, scale=1.0 / D, bias=EPS), r=[pk], w=['rstd'])
        S.op('act', lambda e: e.activation(rs[:, :n], rs[:, :n], AF.Exp, scale=-0.5), r=['rstd'], w=['rstd'])
        for kc in range(KC):
            S.op('dve', lambda e, kc=kc: e.scalar_tensor_tensor(out[:, kc, :], x[:, kc, :], g[:, kc:kc + 1], rs[:, :n], ALU.mult, ALU.mult),
                 r=[xk, 'rstd', 'vecs'], w=[outk])

    def wcache(self, ckey):
        first = ckey not in self.scr
        if first:
            self.scr[ckey] = len(self.scr)
        return self.scr_t[self.scr[ckey]], 'scr_%d' % self.scr[ckey], first

    def load_wblock(self, wt, wk, dst_view, src_view, ckey):
        S = self.S
        scr, sk, first = self.wcache(ckey)
        if first:
            S.dma('pool', dst_view, src_view, w=[wk])
            S.dma('sp', scr, wt[:, 0:4096], r=[wk], w=[sk])
        else:
            q = ('pool', 'sp')[self.wq % 2]
            self.wq += 1
            S.dma(q, wt[:, 0:4096], scr, r=[sk], w=[wk])

    def load_wcols(self, src2d, c0, ncols, ckey):
        assert ncols == 512
        wt, wk = self.wb()
        wv = wt[:, 0:KC * ncols].rearrange("p (kc f) -> p kc f", kc=KC)
        self.load_wblock(wt, wk, wv, src2d.rearrange("(kc p) f -> p kc f", p=128)[:, :, c0:c0 + ncols], ckey)
        return wv, wk

    def mlp(self, layer, x, xk, n):
        S = self.S
        hT, aT = self.hT, self.aT
        self.rmsnorm(x, xk, n, self.cfg.depth + layer, hT[:, :, :n], 'hT')
        wd = self.w_down[layer].rearrange("(fc p) d -> p fc d", p=128)
        for fb in range(8):
            wv, wk = self.load_wcols(self.w_up[layer], fb * 512, 512, ('up', layer, fb))
            for j in range(4):
                fc = fb * 4 + j
                pt, pk = self.ps()
                for kc in range(KC):
                    S.op('pe', lambda e, kc=kc, j=j, wv=wv, pt=pt: e.matmul(pt[:, :n], wv[:, kc, j * 128:(j + 1) * 128], hT[:, kc, :n],
                                                                   start=(kc == 0), stop=(kc == KC - 1)), r=[wk, 'hT'], w=[pk])
                S.op('act', lambda e, pt=pt: e.activation(self.relu[:, :n], pt[:, :n], AF.Relu), r=[pk], w=['relu'])
                S.op('dve', lambda e, fc=fc: e.tensor_tensor(aT[:, fc, :n], self.relu[:, :n], self.relu[:, :n], ALU.mult), r=['relu'], w=['aT'])
        for dc in range(KC):
            wt, wk = self.wb()
            wv = wt[:, 0:FC * 128].rearrange("p (fc d) -> p fc d", fc=FC)
            self.load_wblock(wt, wk, wv, wd[:, :, dc * 128:(dc + 1) * 128], ('dn', layer, dc))
            pt, pk = self.ps()
            for fc in range(FC):
                S.op('pe', lambda e, fc=fc, wv=wv, pt=pt: e.matmul(pt[:, :n], wv[:, fc, :], aT[:, fc, :n], start=(fc == 0), stop=(fc == FC - 1)),
                     r=[wk, 'aT'], w=[pk])
            S.op('dve', lambda e, dc=dc, pt=pt: e.tensor_tensor(x[:, dc, :], x[:, dc, :], pt[:, :n], ALU.add), r=[pk, xk], w=[xk])

    def pool_mix(self, o, x, xk, n, first, nb=1, hb=15):
        S = self.S
        uF, A, B, pb = self.uF, self.wsA, self.wsB, self.hT
        npt = n // nb
        Lx = hb + npt
        uv = uF[:, :, 0:nb * Lx].rearrange("p k (b l) -> p k b l", b=nb)
        av = A[:, :, 0:nb * Lx].rearrange("p k (b l) -> p k b l", b=nb)
        bv = B[:, :, 0:nb * Lx].rearrange("p k (b l) -> p k b l", b=nb)
        pbv = pb[:, :, 0:n].rearrange("p k (b l) -> p k b l", b=nb)
        xv = x.rearrange("p k (b l) -> p k b l", b=nb)
        sq, rs = self.sq, self.rstd
        g = self.vec(2 * o + 1)
        pt, pk = self.ps()
        for kc in range(KC):
            S.op('act', lambda e, kc=kc: e.activation(sq[:, kc, :n], x[:, kc, :], AF.Square), r=[xk], w=['sq'])
        for kc in range(KC):
            S.op('pe', lambda e, kc=kc, pt=pt: e.matmul(pt[:, :n], self.ones_bf[:], sq[:, kc, :n], start=(kc == 0), stop=(kc == KC - 1)), r=['sq', 'ones'], w=[pk])
        S.op('act', lambda e, pt=pt: e.activation(rs[:, :n], pt[:, :n], AF.Ln, scale=1.0 / D, bias=EPS), r=[pk], w=['rstd'])
        S.op('act', lambda e: e.activation(rs[:, :n], rs[:, :n], AF.Exp, scale=-0.5), r=['rstd'], w=['rstd'])
        rsv = rs[:, :n].rearrange("p (b l) -> p b l", b=nb)
        for kc in range(KC):
            S.op('dve', lambda e, kc=kc: e.scalar_tensor_tensor(uv[:, kc, :, hb:Lx], xv[:, kc, :, :], g[:, kc:kc + 1], rsv, ALU.mult, ALU.mult),
                 r=[xk, 'rstd', 'vecs'], w=['uF'])
        wt, wk = self.wb()
        wv = wt[:, 0:2048].rearrange("p (g kc d) -> p g kc d", g=4, kc=2)
        S.dma('pool', wv, self.w_pool[o].rearrange("g (kc p) d -> p g kc d", p=128), w=[wk])
        for g_ in range(4):
            w = 2 << g_
            kparts = [(slice(2 * g_, 2 * g_ + 2), slice(0, 2))] if nb == 1 else [(2 * g_ + kk, kk) for kk in range(2)]
            for ku, kw in kparts:
                src, srck = None, 'uF'
                bufs = [(av, 'wsA'), (bv, 'wsB')]
                for st in range(g_ + 1):
                    sh = 1 << st
                    dst, dstk = bufs[st % 2]
                    lo = 2 * sh - 1
                    if src is None:
                        S.op('dve', lambda e, dst=dst, ku=ku, kw=kw, sh=sh, lo=lo: e.tensor_tensor(dst[:, kw, :, lo:Lx], uv[:, ku, :, lo:Lx], uv[:, ku, :, lo - sh:Lx - sh], ALU.add),
                             r=['uF'], w=[dstk])
                    else:
                        S.op('dve', lambda e, dst=dst, src=src, kw=kw, sh=sh, lo=lo: e.tensor_tensor(dst[:, kw, :, lo:Lx], src[:, kw, :, lo:Lx], src[:, kw, :, lo - sh:Lx - sh], ALU.add),
                             r=[srck], w=[dstk])
                    src, srck = dst, dstk
                S.op('dve', lambda e, src=src, ku=ku, kw=kw, w=w: e.scalar_tensor_tensor(pbv[:, ku, :, :], src[:, kw, :, hb:Lx], 1.0 / w, uv[:, ku, :, hb:Lx],
                     ALU.mult, ALU.subtract), r=[srck, 'uF'], w=['hT'])
            ksl = slice(2 * g_, 2 * g_ + 2)
            if first:
                ic = self.cs('invc')[:, g_ * 16:g_ * 16 + 15]
                S.op('dve', lambda e, src=src, ic=ic: e.tensor_tensor(src[:, :, 0, hb:hb + 15], src[:, :, 0, hb:hb + 15],
                     ic.unsqueeze(1).to_broadcast([128, 2, 15]), ALU.mult), r=[srck, 'cst'], w=[srck])
                S.op('dve', lambda e, src=src, ksl=ksl: e.tensor_tensor(pbv[:, ksl, 0, 0:15], src[:, :, 0, hb:hb + 15], uv[:, ksl, 0, hb:hb + 15], ALU.subtract),
                     r=[srck, 'uF'], w=['hT'])
            for oc in range(2):
                pt, pk = self.ps()
                for k2 in range(2):
                    S.op('pe', lambda e, g_=g_, oc=oc, k2=k2, pt=pt: e.matmul(pt[:, :n], wv[:, g_, k2, oc * 128:(oc + 1) * 128], pb[:, 2 * g_ + k2, :n],
                         start=(k2 == 0), stop=(k2 == 1)), r=[wk, 'hT'], w=[pk])
                dc = 2 * g_ + oc
                psc = self.vec(2 * self.cfg.depth + 1 + o)
                S.op('dve', lambda e, dc=dc, pt=pt, psc=psc: e.scalar_tensor_tensor(x[:, dc, :], pt[:, :n], psc[:, dc:dc + 1], x[:, dc, :], ALU.mult, ALU.add),
                     r=[pk, xk, 'vecs'], w=[xk])

    def transpose_out(self, src3, srck, ncol, dst_dram, rows):
        S = self.S
        for kc in range(KC):
            pt, pk = self.ps()
            S.op('pe', lambda e, kc=kc, pt=pt: e.transpose(pt[0:ncol, 0:128], src3[:, kc, :], self.cs('ident')), r=[srck, 'cst'], w=[pk])
            S.op('act', lambda e, kc=kc, pt=pt: e.activation(self.tio[0:ncol, kc * 128:(kc + 1) * 128], pt[0:ncol, 0:128], AF.Copy), r=[pk], w=['tio'])
        S.dma('sp', dst_dram, self.tio[0:ncol, :], r=['tio'], out=True)

    def load_tokens(self, src_dram, nrows, dst3, dstk):
        S = self.S
        S.dma('sp', self.tio[0:nrows, :], src_dram, w=['tio'])
        for kc in range(KC):
            pt, pk = self.ps()
            S.op('pe', lambda e, kc=kc, pt=pt: e.transpose(pt[:, 0:nrows], self.tio[0:nrows, kc * 128:(kc + 1) * 128], self.cs('ident', nrows, nrows)),
                 r=['tio', 'cst'], w=[pk])
            S.op('act', lambda e, kc=kc, pt=pt: e.activation(dst3[:, kc, :], pt[:, 0:nrows], AF.Copy), r=[pk], w=[dstk])

    def proj_feat(self, wv, wk, n, evac):
        S = self.S
        for j in range(4):
            pt, pk = self.ps()
            for kc in range(KC):
                S.op('pe', lambda e, kc=kc, j=j, pt=pt: e.matmul(pt[:, :n], wv[:, kc, j * 128:(j + 1) * 128], self.hT[:, kc, :n],
                     start=(kc == 0), stop=(kc == KC - 1)), r=[wk, 'hT'], w=[pk])
            evac(j, pt, pk)

    def proj_tok(self, wv, wk, P, t4, ncols, c0=0):
        S = self.S
        pt, pk = self.ps()
        for kc in range(KC):
            S.op('pe', lambda e, kc=kc, pt=pt: e.matmul(pt[0:P, 0:ncols], self.hT[:, kc, t4 * P:(t4 + 1) * P], wv[:, kc, c0:c0 + ncols],
                 start=(kc == 0), stop=(kc == KC - 1)), r=[wk, 'hT'], w=[pk])
        return pt, pk

    def even_tile(self, e, kind, x, xk, n, ti):
        S, cfg = self.S, self.cfg
        P = 128 if kind == 'p' else cfg.ns
        NTK = n // P
        hT = self.hT
        self.rmsnorm(x, xk, n, 2 * e, hT[:, :, :n], 'hT')
        win = self.w_in[e]
        gT, qbT, mixT = self.gT, self.qbT, self.mixT
        if kind == 'p':
            KTv = self.KT[:, :, ti * n:(ti + 1) * n]
            ktk = 'KT'
            row0 = ti * n
        else:
            KTv = self.KTs[:, :, 0:n]
            ktk = 'KTs'
            row0 = 0
        fk = self.fk if kind == 'p' else self.fks
        fv = self.fv if kind == 'p' else self.fvs
        fl = self.fl if kind == 'p' else self.fls
        wv, wk = self.load_wcols(win, C_QB, 512, ('in', e, C_QB))
        self.proj_feat(wv, wk, n, lambda j, pt, pk: S.op('act', lambda e_: e_.activation(qbT[:, j, :n], pt[:, :n], AF.Copy), r=[pk], w=['qbT']))
        if cfg.kstop <= 1:
            return
        if kind == 's' and 'past' in cfg.flags and 'fox' in cfg.flags:
            self.fence()
            self.fox_sample_past(e)
            self.fence()
        wv, wk = self.load_wcols(win, C_GA, 512, ('in', e, C_GA))
        self.proj_feat(wv, wk, n, lambda j, pt, pk: S.op('act', lambda e_: e_.activation(gT[:, j, :n], pt[:, :n], AF.Silu), r=[pk], w=['gT']))
        if cfg.kstop <= 2:
            return
        S.op('dve', lambda e_: e_.memset(self.raug[0:32, :], 1.0), w=['raug'])
        pt, pk = self.ps()
        for kc in range(KC):
            S.op('pe', lambda e_, kc=kc, pt=pt: e_.matmul(pt[0:16, :n], self.wsmall[:, kc, 0:16], hT[:, kc, :n], start=(kc == 0), stop=(kc == KC - 1)),
                 r=['wsmall', 'hT'], w=[pk])
        S.op('act', lambda e_, pt=pt: e_.activation(self.raug[0:16, :n], pt[0:16, :n], AF.Copy), r=[pk], w=['raug'])
        wv, wk = self.load_wcols(win, C_KB, 512, ('in', e, C_KB))
        self.proj_feat(wv, wk, n, lambda j, pt, pk: S.op('act', lambda e_: e_.activation(KTv[:, j, :], pt[:, :n], AF.Copy), r=[pk], w=[ktk]))
        for t4 in range(NTK):
            pt, pk = self.proj_tok(wv, wk, P, t4, 512)
            S.op('act', lambda e_, pt=pt: e_.activation(self.stg[0:P, :], pt[0:P, :], AF.Copy), r=[pk], w=['g_oa'])
            S.dma('sp', fk[e, row0 + t4 * P:row0 + (t4 + 1) * P, :], self.stg[0:P, :], r=['g_oa'], out=True)
        if cfg.kstop <= 3:
            return
        wv, wk = self.load_wcols(win, C_VB, 512, ('in', e, C_VB))
        for t4 in range(NTK):
            pt, pk = self.proj_tok(wv, wk, P, t4, 512)
            S.op('act', lambda e_, pt=pt: e_.activation(self.stg[0:P, :], pt[0:P, :], AF.Copy), r=[pk], w=['g_oa'])
            S.dma('sp', fv[e, row0 + t4 * P:row0 + (t4 + 1) * P, :], self.stg[0:P, :], r=['g_oa'], out=True)
            if kind == 'p':
                vdst, vk = self.Vst[:, (row0 // 128) + t4, :], 'Vst'
            else:
                vdst, vk = self.Vs[0:P, :], 'Vs'
            S.op('dve', lambda e_, pt=pt, vdst=vdst: e_.tensor_copy(vdst, pt[0:P, :]), r=[pk], w=[vk])
        if cfg.kstop <= 4:
            return
        wv, wk = self.load_wcols(win, C_QKA, 512, ('in', e, C_QKA))
        for t4 in range(NTK):
            pt, pk = self.proj_tok(wv, wk, P, t4, 512)
            S.op('act', lambda e_, pt=pt, t4=t4: e_.activation(self.qk_tok[0:P, t4, :], pt[0:P, :], AF.Copy), r=[pk], w=['qk_tok'])
        wv, wk = self.load_wcols(win, C_VA, 512, ('in', e, C_VA))
        for t4 in range(NTK):
            pt, pk = self.proj_tok(wv, wk, P, t4, 512)
            S.op('act', lambda e_, pt=pt, t4=t4: e_.activation(self.va_tok[0:P, t4, :], pt[0:P, :], AF.Copy), r=[pk], w=['va_tok'])
        if cfg.kstop <= 5:
            return
        if kind == 'p':
            S.op('dve', lambda e_: e_.tensor_copy(self.Fref[:], self.carry[:]), r=['carry'], w=['Fref'])
        for t4 in range(NTK):
            pf, pfk = self.ps()
            for kc in range(KC):
                S.op('pe', lambda e_, kc=kc, pf=pf, t4=t4: e_.matmul(pf[0:P, 0:4], hT[:, kc, t4 * P:(t4 + 1) * P], self.wsmall[:, kc, 16:20],
                     start=(kc == 0), stop=(kc == KC - 1)), r=['wsmall', 'hT'], w=[pfk])
            lf = self.lf[0:P, t4, :]
            S.op('dve', lambda e_, pf=pf, lf=lf: e_.tensor_tensor(lf, pf[0:P, 0:4], self.bfg[0:P, :], ALU.add), r=[pfk, 'bfg'], w=['lf'])
            S.op('act', lambda e_, lf=lf: e_.activation(lf, lf, AF.Exp, scale=-1.0), r=['lf'], w=['lf'])
            S.op('act', lambda e_, lf=lf: e_.activation(lf, lf, AF.Ln, bias=1.0), r=['lf'], w=['lf'])
            S.op('dve', lambda e_, lf=lf: e_.tensor_scalar(lf, lf, -1.0, None, ALU.mult), r=['lf'], w=['lf'])
            S.dma('sp', fl[e, row0 + t4 * P:row0 + (t4 + 1) * P, :], lf, r=['lf'], out=True)
            pF, pFk = self.ps()
            if kind == 'p':
                tile_idx = row0 // 128 + t4
                S.op('pe', lambda e_, pF=pF, lf=lf: e_.matmul(pF[:, 0:4], self.cs('triU'), lf, start=True, stop=True), r=['lf', 'cst'], w=[pFk])
                S.op('pe', lambda e_, pF=pF, lf=lf: e_.matmul(pF[:, 4:8], self.ones_f[:], lf, start=True, stop=True), r=['lf', 'ones'], w=[pFk])
                S.op('dve', lambda e_, pF=pF, tile_idx=tile_idx: e_.tensor_tensor(self.Fcol[:, tile_idx, :], pF[:, 0:4], self.carry[:], ALU.add),
                     r=[pFk, 'carry'], w=['Fcol'])
                S.op('dve', lambda e_, pF=pF: e_.tensor_tensor(self.carry[:], self.carry[:], pF[:, 4:8], ALU.add), r=[pFk, 'carry'], w=['carry'])
            else:
                S.op('pe', lambda e_, pF=pF, lf=lf: e_.matmul(pF[0:P, 0:4], self.cs('maskCs', P), lf, start=True, stop=True), r=['lf', 'cst'], w=[pFk])
                S.op('dve', lambda e_, pF=pF: e_.tensor_scalar(self.Fn[0:P, :], pF[0:P, 0:4], -1.0, -SHIFT_C, ALU.mult, ALU.add), r=[pFk], w=['Fn'])
        if cfg.kstop <= 6:
            return
        fl_ = cfg.flags
        if 'gla' not in fl_ or 'fox' not in fl_:
            S.op('dve', lambda e_: e_.memset(mixT[:, :, :n], 0.0), w=['mixT'])
        if 'gla' in fl_:
            for t4 in range(NTK):
                self.gla_tile(e, kind, t4, n, ti)
        if 'fox' in fl_:
            if kind == 'p':
                self.fox_prompt(e, ti, n)
            elif 'past' in fl_:
                self.fox_sample(e)
        for blk in range(2):
            wv, wk = self.load_wcols(self.w_out[e], blk * 512, 512, ('out', e, blk))
            for j in range(4):
                dc = blk * 4 + j
                pt, pk = self.ps()
                for mc in range(KC):
                    S.op('pe', lambda e_, mc=mc, j=j, pt=pt, wv=wv: e_.matmul(pt[:, :n], wv[:, mc, j * 128:(j + 1) * 128], mixT[:, mc, :n],
                         start=(mc == 0), stop=(mc == KC - 1)), r=[wk, 'mixT'], w=[pk])
                S.op('dve', lambda e_, dc=dc, pt=pt: e_.tensor_tensor(x[:, dc, :], x[:, dc, :], pt[:, :n], ALU.add), r=[pk, xk], w=[xk])

    def gla_tile(self, e, kind, t4, n, ti):
        S, cfg = self.S, self.cfg
        if kind == 'p':
            P, Cs, nch = 128, 64, 2
            triC, onesC, maskC, csel, sel01 = (self.cs(k) for k in ('triC', 'onesC', 'maskC', 'csel', 'sel01'))
        else:
            P, Cs, nch = 16, 4, 4
            triC, onesC, maskC, csel, sel01 = (self.cs(k, 16) for k in ('triCs', 'onesCs', 'maskCs', 'csels', 'sel01s'))
        cols = slice(t4 * P, (t4 + 1) * P)
        qk = self.qk_tok[0:P, t4, :]
        va = self.va_tok[0:P, t4, :]
        sp, bsb, eb, enb, edb, qe, ke, kd, kdm = (t[0:P, :] for t in (self.g_sp, self.g_b, self.g_eb, self.g_enb, self.g_edb, self.g_qe, self.g_ke, self.g_kd, self.g_kdm))
        Sst, Sbf, ebl = self.Sst, self.Sbf, self.ebl
        pt, pk = self.ps()
        S.op('pe', lambda e_, pt=pt: e_.matmul(pt[0:P, 0:256], self.raug[0:17, cols], self.wg_aug[0:17, :], start=True, stop=True), r=['raug', 'wg'], w=[pk])
        S.op('act', lambda e_, pt=pt: e_.activation(sp, pt[0:P, 0:256], AF.Exp, scale=-1.0), r=[pk], w=['g_sp'])
        S.op('act', lambda e_: e_.activation(sp, sp, AF.Ln, bias=1.0), r=['g_sp'], w=['g_sp'])
        p2, p2k = self.ps()
        S.op('pe', lambda e_, p2=p2: e_.matmul(p2[0:P, 0:256], triC, sp, start=True, stop=True), r=['g_sp', 'cst'], w=[p2k])
        S.op('pe', lambda e_, p2=p2: e_.matmul(p2[0:P, 256:512], onesC, sp, start=True, stop=True), r=['g_sp', 'cst'], w=[p2k])
        S.op('act', lambda e_, p2=p2: e_.activation(bsb, p2[0:P, 0:256], AF.Copy), r=[p2k], w=['g_b'])
        S.op('act', lambda e_: e_.activation(eb, bsb, AF.Exp), r=['g_b'], w=['g_eb'])
        S.op('act', lambda e_: e_.activation(enb, bsb, AF.Exp, scale=-1.0), r=['g_b'], w=['g_enb'])
        S.op('dve', lambda e_, p2=p2: e_.tensor_tensor(edb, p2[0:P, 256:512], bsb, ALU.subtract), r=[p2k, 'g_b'], w=['g_edb'])
        S.op('act', lambda e_: e_.activation(edb, edb, AF.Exp), r=['g_edb'], w=['g_edb'])
        S.op('dve', lambda e_: e_.scalar_tensor_tensor(qe, qk[:, 0:256], 0.125, eb, ALU.mult, ALU.mult), r=['qk_tok', 'g_eb'], w=['g_qe'])
        S.op('dve', lambda e_: e_.tensor_tensor(ke, qk[:, 256:512], enb, ALU.mult), r=['qk_tok', 'g_enb'], w=['g_ke'])
        S.op('dve', lambda e_: e_.tensor_tensor(kd, qk[:, 256:512], edb, ALU.mult), r=['qk_tok', 'g_edb'], w=['g_kd'])
        idn = self.cs('ident', P, P)
        for src, srck, dst, dstk in ((qe, 'g_qe', self.qeT, 'qeT'), (ke, 'g_ke', self.keT, 'keT')):
            p3, p3k = self.ps()
            for h in range(4):
                S.op('pe', lambda e_, h=h, p3=p3, src=src: e_.transpose(p3[0:64, h * P:(h + 1) * P], src[:, h * 64:(h + 1) * 64], idn), r=[srck, 'cst'], w=[p3k])
            S.op('act', lambda e_, p3=p3, dst=dst: e_.activation(dst[:, :, 0:P], p3[0:64, 0:4 * P].rearrange("p (h t) -> p h t", h=4), AF.Copy), r=[p3k], w=[dstk])
        qeT, keT = self.qeT, self.keT
        p5, p5k = self.ps()
        for h in range(4):
            S.op('pe', lambda e_, h=h, p5=p5: e_.matmul(p5[0:P, h * P:(h + 1) * P], keT[:, h, 0:P], qeT[:, h, 0:P], start=True, stop=True), r=['qeT', 'keT'], w=[p5k])
        attnT = self.attnT
        S.op('dve', lambda e_, p5=p5: e_.tensor_tensor(attnT[0:P, :, 0:P], p5[0:P, 0:4 * P].rearrange("p (h t) -> p h t", h=4),
             maskC.unsqueeze(1).to_broadcast([P, 4, P]), ALU.mult), r=[p5k, 'cst'], w=['attnT'])
        p6, p6k = self.ps()
        for h in range(4):
            S.op('pe', lambda e_, h=h, p6=p6: e_.matmul(p6[0:64, h * nch:(h + 1) * nch], sp[:, h * 64:(h + 1) * 64], csel[:, 0:nch], start=True, stop=True),
                 r=['g_sp', 'cst'], w=[p6k])
        S.op('act', lambda e_, p6=p6: e_.activation(ebl[:, 0:4 * nch], p6[0:64, 0:4 * nch], AF.Exp), r=[p6k], w=['ebl'])
        p7, p7k = self.ps()
        for h in range(4):
            S.op('pe', lambda e_, h=h, p7=p7: e_.matmul(p7[:, h * P:(h + 1) * P], va[:, h * 128:(h + 1) * 128], attnT[0:P, h, 0:P], start=True, stop=True),
                 r=['va_tok', 'attnT'], w=[p7k])
        p8, p8k = self.ps()
        for c in range(nch):
            if kind == 's':
                S.dma('sp', Sst[:], self.sgla[e, c].rearrange("h k v -> k h v"), w=['Sst'])
                S.op('act', lambda e_: e_.activation(Sbf[:], Sst[:], AF.Copy), r=['Sst'], w=['Sbf'])
            for h in range(4):
                S.op('pe', lambda e_, h=h, c=c, p8=p8: e_.matmul(p8[:, h * P + c * Cs:h * P + (c + 1) * Cs], Sbf[:, h, :], qeT[:, h, c * Cs:(c + 1) * Cs],
                     start=True, stop=True), r=['Sbf', 'qeT'], w=[p8k])
            S.op('dve', lambda e_, c=c: e_.tensor_scalar(kdm, kd, sel01[:, c:c + 1], None, ALU.mult), r=['g_kd', 'cst'], w=['g_kdm'])
            p9, p9k = self.ps()
            for h in range(4):
                S.op('pe', lambda e_, h=h, p9=p9: e_.matmul(p9[0:64, h * 128:(h + 1) * 128], kdm[:, h * 64:(h + 1) * 64], va[:, h * 128:(h + 1) * 128],
                     start=True, stop=True), r=['g_kdm', 'va_tok'], w=[p9k])
            for h in range(4):
                S.op('dve', lambda e_, h=h, c=c, p9=p9: e_.scalar_tensor_tensor(Sst[:, h, :], Sst[:, h, :], ebl[:, h * nch + c:h * nch + c + 1],
                     p9[0:64, h * 128:(h + 1) * 128], ALU.mult, ALU.add), r=[p9k, 'Sst', 'ebl'], w=['Sst'])
            S.op('act', lambda e_: e_.activation(Sbf[:], Sst[:], AF.Copy), r=['Sst'], w=['Sbf'])
            if kind == 's':
                S.dma('sp', self.gs[e, c].rearrange("h k v -> k h v"), Sst[:], r=['Sst'], out=True)
        if kind == 'p' and ti == cfg.ntt - 1 and t4 == n // P - 1:
            S.dma('sp', self.gp[e].rearrange("h k v -> k h v"), Sst[:], r=['Sst'], out=True)
        inter, oa, sq2, rs2 = self.g_inter, self.g_oa, self.g_sq2, self.g_rs2
        W4 = 4 * P
        S.op('act', lambda e_, p8=p8: e_.activation(inter[:, 0:W4], p8[:, 0:W4], AF.Copy), r=[p8k], w=['g_inter'])
        S.op('dve', lambda e_, p7=p7: e_.tensor_tensor(oa[:, 0:W4], p7[:, 0:W4], inter[:, 0:W4], ALU.add), r=[p7k, 'g_inter'], w=['g_oa'])
        S.op('act', lambda e_: e_.activation(sq2[:, 0:W4], oa[:, 0:W4], AF.Square), r=['g_oa'], w=['g_sq2'])
        p10, p10k = self.ps()
        S.op('pe', lambda e_, p10=p10: e_.matmul(p10[:, 0:W4], self.ones_bf[:], sq2[:, 0:W4], start=True, stop=True), r=['g_sq2', 'ones'], w=[p10k])
        S.op('act', lambda e_, p10=p10: e_.activation(rs2[:, 0:W4], p10[:, 0:W4], AF.Ln, scale=1.0 / 128, bias=EPS), r=[p10k], w=['g_rs2'])
        S.op('act', lambda e_: e_.activation(rs2[:, 0:W4], rs2[:, 0:W4], AF.Exp, scale=-0.5), r=['g_rs2'], w=['g_rs2'])
        S.op('dve', lambda e_: e_.tensor_tensor(oa[:, 0:W4], oa[:, 0:W4], rs2[:, 0:W4], ALU.mult), r=['g_oa', 'g_rs2'], w=['g_oa'])
        gn = self.vecs_sb[:, self.NV - 2 + e:self.NV - 1 + e]
        S.op('dve', lambda e_: e_.scalar_tensor_tensor(self.mixT[:, 0:4, cols], oa[:, 0:W4].rearrange("p (h t) -> p h t", h=4), gn,
             self.gT[:, :, cols], ALU.mult, ALU.mult), r=['g_oa', 'gT', 'vecs'], w=['mixT'])

    def fox_prompt(self, e, ti, n):
        S = self.S
        scale = 128.0 ** -0.5
        njt = (ti + 1) * n // 128
        bT = self.biasT
        S.op('dve', lambda e_: e_.tensor_tensor(bT[:, 0:njt, :], self.Fref[:].unsqueeze(1).to_broadcast([128, njt, 4]), self.Fcol[:, 0:njt, :], ALU.subtract),
             r=['Fref', 'Fcol'], w=['biasT'])
        S.op('dve', lambda e_: e_.tensor_scalar(bT[:, 0:njt, :], bT[:, 0:njt, :], -SHIFT_C, None, ALU.add), r=['biasT'], w=['biasT'])
        for h in range(4):
            po, pok = self.pst[6], 'ps6'
            pl, plk = self.pst[7], 'ps7'
            def scores(j):
                c0 = max(0, j * 128 - ti * n)
                diag = j * 128 >= ti * n
                pt, pk = self.ps()
                PT, PTk = (self.PTa, 'PTa') if j % 2 == 0 else (self.PTb, 'PTb')
                S.op('pe', lambda e_, h=h, j=j, c0=c0, pt=pt: e_.matmul(pt[:, c0:n], self.KT[:, h, j * 128:(j + 1) * 128], self.qbT[:, h, c0:n], start=True, stop=True),
                     r=['KT', 'qbT'], w=[pk])
                S.op('act', lambda e_, h=h, j=j, c0=c0, pt=pt, PT=PT: e_.activation(PT[:, c0:n], pt[:, c0:n], AF.Exp, bias=bT[:, j, h:h + 1], scale=scale),
                     r=[pk, 'biasT'], w=[PTk])
                if diag:
                    S.op('dve', lambda e_, c0=c0, PT=PT: e_.tensor_tensor(PT[:, c0:c0 + 128], PT[:, c0:c0 + 128], self.triU_bf[:], ALU.mult), r=[PTk, 'triUbf'], w=[PTk])
            scores(0)
            for j in range(njt):
                if j + 1 < njt:
                    scores(j + 1)
                c0 = max(0, j * 128 - ti * n)
                PT, PTk = (self.PTa, 'PTa') if j % 2 == 0 else (self.PTb, 'PTb')
                last = (j == njt - 1)
                S.op('pe', lambda e_, j=j, c0=c0, pl=pl, PT=PT, last=last: e_.matmul(pl[:, c0:n], self.ones_bf[:], PT[:, c0:n], start=(j == 0), stop=last),
                     r=[PTk, 'ones'], w=[plk])
                S.op('pe', lambda e_, h=h, j=j, c0=c0, po=po, PT=PT, last=last: e_.matmul(po[:, c0:n], self.Vst[:, j, h * 128:(h + 1) * 128], PT[:, c0:n], start=(j == 0), stop=last),
                     r=[PTk, 'Vst'], w=[pok])
            S.op('dve', lambda e_, pl=pl: e_.reciprocal(self.rl[:, :n], pl[:, :n]), r=[plk], w=['rl'])
            S.op('dve', lambda e_, h=h, po=po: e_.tensor_tensor(self.mixT[:, 4 + h, :n], po[:, :n], self.rl[:, :n], ALU.mult), r=[pok, 'rl'], w=['mixT'])

    def fox_sample_past(self, e):
        S, cfg = self.S, self.cfg
        NPG, NB = cfg.npg, cfg.nbs
        scale = 128.0 ** -0.5
        G = min(2 * self.GS, NPG)
        psO, psOk = self.pst[6], 'ps6'
        psL, psLk = self.pst[7], 'ps7'
        lfpg, sa, sb_, RHs, RHc, s_sb, PTs = self.s_lfpg, self.s_sa, self.s_sb, self.s_RHs, self.s_RHc, self.s_ssb, self.s_PTs
        for b in range(NB):
            S.op('pool', lambda e_, b=b: e_.indirect_dma_start(out=lfpg[0:NPG, :], out_offset=None, in_=self.lfp[e],
                 in_offset=bass.IndirectOffsetOnAxis(ap=self.idx_pg[0:NPG, b:b + 1], axis=0)), r=['idx_pg'], w=['s_lfpg'], dma=True)
            X, Xk = lfpg, 's_lfpg'
            bufs = [(sa, 's_sa'), (sb_, 's_sb')]
            for st in range(7):
                sh = 1 << st
                Y, Yk = bufs[st % 2]
                xv = X[0:NPG, :].rearrange("p (r h) -> p r h", h=4)
                yv = Y[0:NPG, :].rearrange("p (r h) -> p r h", h=4)
                S.op('dve', lambda e_, xv=xv, yv=yv, sh=sh: e_.tensor_tensor(yv[:, 0:128 - sh, :], xv[:, 0:128 - sh, :], xv[:, sh:128, :], ALU.add), r=[Xk], w=[Yk])
                S.op('dve', lambda e_, xv=xv, yv=yv, sh=sh: e_.tensor_copy(yv[:, 128 - sh:128, :], xv[:, 128 - sh:128, :]), r=[Xk], w=[Yk])
                X, Xk = Y, Yk
            pl, plk = self.ps()
            S.op('pe', lambda e_, pl=pl, X=X: e_.matmul(pl[0:NPG, 0:4], self.cs('sufU', NPG, NPG), X[0:NPG, 0:4], start=True, stop=True), r=[Xk, 'cst'], w=[plk])
            S.op('act', lambda e_, pl=pl: e_.activation(self.s_later[0:NPG, :], pl[0:NPG, 0:4], AF.Copy), r=[plk], w=['s_later'])
            S.op('dve', lambda e_, X=X: e_.tensor_tensor(RHs[0:NPG, :], X[0:NPG, :], lfpg[0:NPG, :], ALU.subtract), r=[Xk, 's_lfpg'], w=['s_RHs'])
            rv = RHs[0:NPG, :].rearrange("p (r h) -> p r h", h=4)
            S.op('dve', lambda e_, rv=rv: e_.tensor_tensor(rv, rv, self.s_later[0:NPG, :].unsqueeze(1).to_broadcast([NPG, 128, 4]), ALU.add),
                 r=['s_RHs', 's_later'], w=['s_RHs'])
            pr, prk = self.ps()
            for h in range(4):
                S.op('pe', lambda e_, h=h, pr=pr, rv=rv: e_.transpose(pr[:, h * NPG:(h + 1) * NPG], rv[:, :, h], self.cs('ident', NPG, NPG)), r=['s_RHs', 'cst'], w=[prk])
            S.op('act', lambda e_, pr=pr: e_.activation(RHc[:, 0:4 * NPG], pr[:, 0:4 * NPG], AF.Copy), r=[prk], w=['s_RHc'])
            psa, psak = self.ps()
            psb, psbk = self.ps()
            for s0 in range(0, NPG, G):
                for g in range(G):
                    slot = s0 + g
                    S.op('pool', lambda e_, g=g, slot=slot, b=b: e_.indirect_dma_start(out=self.s_Kst[:, g, :], out_offset=None, in_=self.ktp[e],
                         in_offset=bass.IndirectOffsetOnAxis(ap=self.idx[:, b * NPG + slot:b * NPG + slot + 1], axis=0)), r=['idx'], w=['s_Kst%d' % g], dma=True)
                    eng = 'act' if g % 2 == 0 else 'dve'
                    if eng == 'act':
                        S.op('act', lambda e_, g=g: e_.activation(self.s_Kbf[:, g, :], self.s_Kst[:, g, :], AF.Copy), r=['s_Kst%d' % g], w=['s_Kbf%d' % g])
                    else:
                        S.op('dve', lambda e_, g=g: e_.tensor_copy(self.s_Kbf[:, g, :], self.s_Kst[:, g, :]), r=['s_Kst%d' % g], w=['s_Kbf%d' % g])
                    for h in range(4):
                        pp, ppk = (psa, psak) if h < 2 else (psb, psbk)
                        c0 = ((h % 2) * NPG + slot) * 4
                        S.op('pe', lambda e_, g=g, h=h, b=b, pp=pp, c0=c0: e_.matmul(pp[:, c0:c0 + 4], self.s_Kbf[:, g, h * 128:(h + 1) * 128], self.qbT[:, h, b * 4:(b + 1) * 4],
                             start=True, stop=True), r=['s_Kbf%d' % g, 'qbT'], w=[ppk])
            for h in range(4):
                pp, ppk = (psa, psak) if h < 2 else (psb, psbk)
                c0 = (h % 2) * NPG * 4
                S.op('dve', lambda e_, h=h, pp=pp, c0=c0: e_.scalar_tensor_tensor(s_sb[:, 0:NPG * 4].rearrange("p (s q) -> p s q", q=4),
                     pp[:, c0:c0 + NPG * 4].rearrange("p (s q) -> p s q", q=4), scale,
                     RHc[:, h * NPG:(h + 1) * NPG].unsqueeze(2).to_broadcast([128, NPG, 4]), ALU.mult, ALU.add), r=[ppk, 's_RHc'], w=['s_ssb'])
                S.op('act', lambda e_, h=h: e_.activation(PTs[:, h, 0:NPG * 4], s_sb[:, 0:NPG * 4], AF.Exp, bias=-SHIFT_C), r=['s_ssb'], w=['s_PTs'])
            for s0 in range(0, NPG, G):
                for g in range(G):
                    slot = s0 + g
                    S.op('pool', lambda e_, g=g, slot=slot, b=b: e_.indirect_dma_start(out=self.s_Vst[:, g, :], out_offset=None, in_=self.vp[e],
                         in_offset=bass.IndirectOffsetOnAxis(ap=self.idx[:, b * NPG + slot:b * NPG + slot + 1], axis=0)), r=['idx'], w=['s_Kst%d' % g], dma=True)
                    if g % 2 == 0:
                        S.op('act', lambda e_, g=g: e_.activation(self.s_Vbf[:, g, :], self.s_Vst[:, g, :], AF.Copy), r=['s_Kst%d' % g], w=['s_Kbf%d' % g])
                    else:
                        S.op('dve', lambda e_, g=g: e_.tensor_copy(self.s_Vbf[:, g, :], self.s_Vst[:, g, :]), r=['s_Kst%d' % g], w=['s_Kbf%d' % g])
                    for h in range(4):
                        c0 = (h * 4 + b) * 4
                        S.op('pe', lambda e_, g=g, h=h, slot=slot, c0=c0, b=b: e_.matmul(psO[:, c0:c0 + 4], self.s_Vbf[:, g, h * 128:(h + 1) * 128], PTs[:, h, slot * 4:(slot + 1) * 4],
                             start=(slot == 0 and h == 0 and b == 0), stop=(slot == NPG - 1 and h == 3 and b == NB - 1)), r=['s_Kbf%d' % g, 's_PTs'], w=[psOk])
                        S.op('pe', lambda e_, h=h, slot=slot, c0=c0, b=b: e_.matmul(psL[:, c0:c0 + 4], self.ones_bf[:], PTs[:, h, slot * 4:(slot + 1) * 4],
                             start=(slot == 0 and h == 0 and b == 0), stop=(slot == NPG - 1 and h == 3 and b == NB - 1)), r=['s_PTs', 'ones'], w=[psLk])
        S.op('act', lambda e_: e_.activation(self.Opast[:], psO[:, 0:64], AF.Copy), r=[psOk], w=['Opast'])
        S.op('act', lambda e_: e_.activation(self.Lpast[:], psL[:, 0:64], AF.Copy), r=[psLk], w=['Lpast'])

    def fox_sample(self, e):
        S, cfg = self.S, self.cfg
        scale = 128.0 ** -0.5
        NSK = cfg.ns
        ptn, ptnk = self.ps()
        for h in range(4):
            S.op('pe', lambda e_, h=h: e_.matmul(ptn[0:NSK, h * NSK:(h + 1) * NSK], self.KTs[:, h, 0:NSK], self.qbT[:, h, 0:NSK], start=True, stop=True),
                 r=['KTs', 'qbT'], w=[ptnk])
        Pnf, Pn = self.Pnf, self.Pn
        for h in range(4):
            S.op('act', lambda e_, h=h: e_.activation(Pnf[0:NSK, h, :], ptn[0:NSK, h * NSK:(h + 1) * NSK], AF.Exp, bias=self.Fn[0:NSK, h:h + 1], scale=scale),
                 r=[ptnk, 'Fn'], w=['Pnf'])
        S.op('dve', lambda e_: e_.tensor_tensor(Pn[0:NSK, :, :], Pnf[0:NSK, :, :], self.cs('maskCs', NSK).unsqueeze(1).to_broadcast([NSK, 4, NSK]), ALU.mult),
             r=['Pnf', 'cst'], w=['Pn'])
        pon, ponk = self.ps()
        pln, plnk = self.ps()
        for h in range(4):
            S.op('pe', lambda e_, h=h: e_.matmul(pon[:, h * NSK:(h + 1) * NSK], self.Vs[0:NSK, h * 128:(h + 1) * 128], Pn[0:NSK, h, :], start=True, stop=True),
                 r=['Vs', 'Pn'], w=[ponk])
            S.op('pe', lambda e_, h=h: e_.matmul(pln[:, h * NSK:(h + 1) * NSK], self.ones_bf[0:NSK, :], Pn[0:NSK, h, :], start=True, stop=True),
                 r=['ones', 'Pn'], w=[plnk])
        Ot, Lt = self.Ot, self.Lt
        S.op('dve', lambda e_: e_.tensor_tensor(Ot[:], pon[:, 0:64], self.Opast[:], ALU.add), r=[ponk, 'Opast'], w=['Ot'])
        S.op('dve', lambda e_: e_.tensor_tensor(Lt[:], pln[:, 0:64], self.Lpast[:], ALU.add), r=[plnk, 'Lpast'], w=['Lt'])
        S.op('dve', lambda e_: e_.reciprocal(Lt[:], Lt[:]), r=['Lt'], w=['Lt'])
        S.op('dve', lambda e_: e_.tensor_tensor(self.mixT[:, 4:8, 0:NSK], Ot[:].rearrange("p (h t) -> p h t", h=4), Lt[:].rearrange("p (h t) -> p h t", h=4), ALU.mult),
             r=['Ot', 'Lt'], w=['mixT'])

    def build(self):
        import contextlib
        nc, S, cfg = self.nc, self.S, self.cfg
        T, L, TT, NS, NB, NPG = cfg.seq, cfg.depth, cfg.tt, cfg.ns, cfg.nbs, cfg.npg
        di = lambda n, s, d=F32: nc.dram_tensor(n, s, d, kind="ExternalInput").ap()
        self.lfp = [di("lfp%d" % i, [cfg.npool, 512]) for i in range(2)]
        self.vp = [di("vp%d" % i, [cfg.npool * 128, 512]) for i in range(2)]
        self.ptabT = di("ptabT", [NPG, NB], I32)
        st = contextlib.ExitStack()
        sb = lambda name, shape, dt: st.enter_context(nc.sbuf_tensor(name, shape, dt))
        self.xT = sb("xT", [128, KC, T], F32); self.xsT = sb("xsT", [128, KC, NS], F32)
        self.hT = sb("hT", [128, KC, TT], BF16); self.sq = sb("sq", [128, KC, TT], BF16); self.rstd = sb("rstd", [128, TT], F32)
        self.wbufs = [sb("wbuf%d" % i, [128, 4096], BF16) for i in range(4)]
        self.cst_sb = sb("cst_sb", [128, CST_W], F32); self.vecs_sb = sb("vecs_sb", [128, self.NV], F32)
        self.ones_bf = sb("ones_bf", [128, 128], BF16); self.ones_f = sb("ones_f", [128, 128], F32); self.triU_bf = sb("triU_bf", [128, 128], BF16)
        self.KT = sb("KT", [128, 4, T], BF16); self.Vst = sb("Vst", [128, T // 128, 512], BF16)
        self.Fcol = sb("Fcol", [128, T // 128, 4], F32); self.carry = sb("carry", [128, 4], F32); self.Fref = sb("Fref", [128, 4], F32)
        self.bfg = sb("bfg", [128, 4], F32); self.wg_aug = sb("wg_aug", [32, 256], BF16); self.wsmall = sb("wsmall", [128, KC, 20], BF16)
        self.Sst = sb("Sst", [64, 4, 128], F32); self.Sbf = sb("Sbf", [64, 4, 128], BF16)
        self.KTs = sb("KTs", [128, 4, NS], BF16); self.Vs = sb("Vs", [NS, 512], BF16); self.Fn = sb("Fn", [NS, 4], F32)
        self.Opast = sb("Opast", [128, 64], F32); self.Lpast = sb("Lpast", [128, 64], F32); self.Ot = sb("Ot", [128, 64], F32); self.Lt = sb("Lt", [128, 64], F32)
        self.Pnf = sb("Pnf", [NS, 4, NS], F32); self.Pn = sb("Pn", [NS, 4, NS], BF16)
        self.idx = sb("idx", [128, NB * NPG], I32); self.idx_f = sb("idx_f", [128, NB * NPG], F32); self.idx_pg = sb("idx_pg", [NPG, NB], I32)
        self.ebl = sb("ebl", [64, 16], F32); self.lf = sb("lf", [128, max(1, TT // 128), 4], F32); self.biasT = sb("biasT", [128, T // 128, 4], F32)
        self.dummy = sb("dummy_t", [128, 1], F32)
        UFW = max(15 + TT, NB * 19)
        self.UW = 9984
        GS = self.GS = 4
        self.U = sb("U", [128, self.UW], F32)
        self.pst = [st.enter_context(nc.psum_tensor("ps%d" % i, [128, 512], F32)) for i in range(8)]
        self.uoff = 0
        self.aT = self.carve('aT', 16 * TT, BF16, "p (f t) -> p f t", f=FC)
        self.uF = self.carve('uF', 8 * UFW, F32, "p (k l) -> p k l", k=KC)
        self.wsA = self.carve('wsA', 2 * UFW, F32, "p (k l) -> p k l", k=2)
        self.wsB = self.carve('wsB', 2 * UFW, F32, "p (k l) -> p k l", k=2)
        self.tio = self.carve('tio', 1024, F32)
        self.relu = self.carve('relu', TT // 2, BF16)
        self.uoff = 0
        self.qbT = self.carve('qbT', 2 * TT, BF16, "p (h t) -> p h t", h=4)
        e_start = self.uoff
        NTK = max(1, TT // 128)
        self.qk_tok = self.carve('qk_tok', NTK * 512, F32, "p (t c) -> p t c", c=512)
        self.va_tok = self.carve('va_tok', NTK * 256, BF16, "p (t c) -> p t c", c=512)
        self.gT = self.carve('gT', 2 * TT, BF16, "p (h t) -> p h t", h=4)
        self.mixT = self.carve('mixT', 4 * TT, BF16, "p (h t) -> p h t", h=8)
        self.raug = self.carve('raug', TT // 2, BF16)
        self.g_sp, self.g_b, self.g_eb, self.g_enb, self.g_edb, self.g_qe, self.g_ke = (self.carve(k, 256, F32) for k in ('g_sp', 'g_b', 'g_eb', 'g_enb', 'g_edb', 'g_qe', 'g_ke'))
        self.g_kd = self.carve('g_kd', 128, BF16); self.g_kdm = self.carve('g_kdm', 128, BF16)
        self.attnT = self.carve('attnT', 256, BF16, "p (h t) -> p h t", h=4)
        self.qeT = self.carve('qeT', 256, BF16, "p (h t) -> p h t", h=4)[0:64]
        self.keT = self.carve('keT', 256, BF16, "p (h t) -> p h t", h=4)[0:64]
        self.g_inter = self.carve('g_inter', 512, F32); self.g_oa = self.carve('g_oa', 512, F32); self.g_rs2 = self.carve('g_rs2', 512, F32)
        self.stg = self.g_oa
        self.g_sq2 = self.carve('g_sq2', 256, BF16)
        self.PTa = self.carve('PTa', TT // 2, BF16); self.PTb = self.carve('PTb', TT // 2, BF16); self.rl = self.carve('rl', TT, F32)
        self.uoff = e_start
        self.s_lfpg = self.carve('s_lfpg', 512, F32); self.s_sa = self.carve('s_sa', 512, F32); self.s_sb = self.carve('s_sb', 512, F32)
        self.s_RHs = self.carve('s_RHs', 512, F32); self.s_RHc = self.carve('s_RHc', 4 * NPG, F32); self.s_ssb = self.carve('s_ssb', 4 * NPG, F32)
        self.s_PTs = self.carve('s_PTs', 8 * NPG, BF16, "p (h c) -> p h c", h=4)
        self.s_later = self.carve('s_later', 4, F32)
        GS2 = 2 * GS
        self.s_Kst = self.carve('s_Kst', 512 * GS2, F32, "p (g c) -> p g c", g=GS2); self.s_Kbf = self.carve('s_Kbf', 256 * GS2, BF16, "p (g c) -> p g c", g=GS2)
        self.s_Vst, self.s_Vbf = self.s_Kst, self.s_Kbf
        for g in range(GS2):
            self.ukeys += ['s_Kst%d' % g, 's_Kbf%d' % g]
        xT, xsT = self.xT, self.xsT
        S.dma('sp', self.cst_sb[:], self.cst, w=['cst'])
        S.dma('sp', self.vecs_sb[:], self.vecs, w=['vecs'])
        S.op('dve', lambda e: e.memset(self.ones_bf[:], 1.0), w=['ones'])
        S.op('dve', lambda e: e.memset(self.ones_f[:], 1.0), w=['ones'])
        S.op('dve', lambda e: e.tensor_copy(self.triU_bf[:], self.cs('triU')), r=['cst'], w=['triUbf'])
        S.dma('sp', self.idx_pg[:], self.ptabT, w=['idx_pg'])
        S.dma('sp', self.idx[:], self.ptab.partition_broadcast(128), w=['idx'])
        S.op('dve', lambda e: e.tensor_copy(self.idx_f[:], self.idx[:]), r=['idx'], w=['idx_f'])
        S.op('dve', lambda e: e.scalar_tensor_tensor(self.idx_f[:], self.idx_f[:], 128.0, self.cs('iota').to_broadcast([128, NB * NPG]), ALU.mult, ALU.add),
             r=['idx_f', 'cst'], w=['idx_f'])
        S.op('dve', lambda e: e.tensor_copy(self.idx[:], self.idx_f[:]), r=['idx_f'], w=['idx'])
        for tt in range(T // 128):
            self.load_tokens(self.xp[tt * 128:(tt + 1) * 128, :], 128, xT[:, :, tt * 128:(tt + 1) * 128], 'x%d' % (tt * 128 // TT))
        self.load_tokens(self.xs, NS, xsT[:, :, :], 'xs')
        xt = lambda ti: (xT[:, :, ti * TT:(ti + 1) * TT], 'x%d' % ti)
        for l in range(L):
            if l % 2 == 0 and 'noeven' in cfg.flags:
                for ti in range(cfg.ntt):
                    x, xk = xt(ti)
                    self.mlp(l, x, xk, TT)
                continue
            if l % 2 == 0:
                e = l // 2
                S.dma('pool', self.wsmall[:, :, 0:16], self.w_in[e].rearrange("(kc p) f -> p kc f", p=128)[:, :, C_RA:C_RA + 16], w=['wsmall'])
                S.dma('pool', self.wsmall[:, :, 16:20], self.w_in[e].rearrange("(kc p) f -> p kc f", p=128)[:, :, C_FB:C_FB + 4], w=['wsmall'])
                S.dma('pool', self.wg_aug[0:16, :], self.w_gate[e], w=['wg'])
                S.dma('pool', self.wg_aug[16:17, :], self.b_gate[e], w=['wg'])
                S.dma('sp', self.bfg[:], self.b_forget[e].partition_broadcast(128), w=['bfg'])
                S.op('dve', lambda e_: e_.memset(self.carry[:], 0.0), w=['carry'])
                S.op('dve', lambda e_: e_.memset(self.Sst[:], 0.0), w=['Sst'])
                S.op('dve', lambda e_: e_.memset(self.Sbf[:], 0.0), w=['Sbf'])
                for ti in range(cfg.ntt):
                    x, xk = xt(ti)
                    self.fence()
                    self.even_tile(e, 'p', x, xk, TT, ti)
                    self.fence()
                    self.mlp(l, x, xk, TT)
                if 'sample' in cfg.flags:
                    self.fence()
                    self.even_tile(e, 's', xsT[:, :, :], 'xs', NS, 0)
                    self.fence()
                    self.mlp(l, xsT[:, :, :], 'xs', NS)
            else:
                o = l // 2
                self.fence()
                S.op('dve', lambda e_: e_.memset(self.uF[:, :, 0:15], 0.0), w=['uF'])
                for ti in range(cfg.ntt):
                    x, xk = xt(ti)
                    self.pool_mix(o, x, xk, TT, first=(ti == 0))
                    if ti == cfg.ntt - 1:
                        self.transpose_out(self.uF[:, :, TT:TT + 15], 'uF', 15, self.npool[o], 15)
                    else:
                        S.op('dve', lambda e_: e_.tensor_copy(self.uF[:, :, 0:15], self.uF[:, :, TT:TT + 15]), r=['uF'], w=['uF'])
                    self.mlp(l, x, xk, TT)
                if 'sample' not in cfg.flags:
                    continue
                uvs = self.uF[:, :, 0:NB * 19].rearrange("p k (b l) -> p k b l", b=NB)
                S.dma('sp', self.tio[0:NB * 15, :], self.spool[o].rearrange("b r d -> (b r) d"), w=['tio'])
                for kc in range(KC):
                    pt, pk = self.ps()
                    S.op('pe', lambda e_, kc=kc, pt=pt: e_.transpose(pt[:, 0:NB * 15], self.tio[0:NB * 15, kc * 128:(kc + 1) * 128], self.cs('ident', NB * 15, NB * 15)),
                         r=['tio', 'cst'], w=[pk])
                    S.op('act', lambda e_, kc=kc, pt=pt: e_.activation(uvs[:, kc, :, 0:15], pt[:, 0:NB * 15].rearrange("p (b r) -> p b r", b=NB), AF.Copy), r=[pk], w=['uF'])
                S.dma('sp', self.nps[o, :, 0:11, :], self.spool[o, :, 4:15, :], out=True)
                self.pool_mix(o, xsT[:, :, :], 'xs', NS, first=False, nb=NB)
                tmp = self.wsA[:, :, :].rearrange("p k l -> p (k l)")[:, 0:KC * NS].rearrange("p (k t) -> p k t", k=KC)
                for kc in range(KC):
                    S.op('dve', lambda e_, kc=kc: e_.tensor_copy(tmp[:, kc, :].rearrange("p (b t) -> p b t", b=NB), uvs[:, kc, :, 15:19]), r=['uF'], w=['wsA'])
                for kc in range(KC):
                    pt, pk = self.ps()
                    S.op('pe', lambda e_, kc=kc, pt=pt: e_.transpose(pt[0:NS, 0:128], tmp[:, kc, :], self.cs('ident')), r=['wsA', 'cst'], w=[pk])
                    S.op('act', lambda e_, kc=kc, pt=pt: e_.activation(self.tio[0:NS, kc * 128:(kc + 1) * 128], pt[0:NS, 0:128], AF.Copy), r=[pk], w=['tio'])
                for b in range(NB):
                    S.dma('sp', self.nps[o, b, 11:15, :], self.tio[b * 4:(b + 1) * 4, :], r=['tio'], out=True)
                self.mlp(l, xsT[:, :, :], 'xs', NS)
        self.fence()
        for ti in range(cfg.ntt):
            x, xk = xt(ti)
            self.rmsnorm(x, xk, TT, 2 * L, self.uF[:, :, 0:TT], 'uF')
            for t4 in range(TT // 128):
                r0 = ti * TT + t4 * 128
                self.transpose_out(self.uF[:, :, t4 * 128:(t4 + 1) * 128], 'uF', 128, self.y[r0:r0 + 128, :], 128)
        self.rmsnorm(xsT[:, :, :], 'xs', NS, 2 * L, self.uF[:, :, 0:NS], 'uF')
        self.transpose_out(self.uF[:, :, 0:NS], 'uF', NS, self.ys, NS)
        S.emit()
        print("sbuf bytes remaining:", nc.sbuf_bytes_remaining, "ops:", len(S.ops))
        st.close()
        return nc


def shared_inputs(cfg, inp):
    L = cfg.depth
    f = lambda a: np.ascontiguousarray(np.asarray(a), dtype=np.float32)
    cols = [inp["norm_mix"][l] for l in range(L)] + [inp["norm_mlp"][l] for l in range(L)] + [inp["norm_final"]] + \
           [inp["pool_scale"][o] for o in range(2)]
    vecs = np.concatenate([np.asarray(c, np.float32).reshape(KC, 128).T for c in cols] + [np.asarray(inp["gla_norm"], np.float32).T], axis=1)
    ck = np.asarray(inp["cache_fox_k"], np.float32)
    npool = ck.shape[1]
    ktp = np.ascontiguousarray(ck.transpose(0, 1, 4, 3, 2)).reshape(2, npool * 128, 512)
    vp = f(inp["cache_fox_v"]).reshape(2, npool * 128, 512)
    lfp = f(inp["cache_fox_logf"]).reshape(2, npool, 512)
    return dict(cst=make_consts(), vecs=np.ascontiguousarray(vecs), w_in=f(inp["w_in_even"]), w_out=f(inp["w_out_even"]),
                w_gate=f(inp["w_gate_up"]), b_gate=f(inp["b_gate"]).reshape(2, 1, 256), b_forget=f(inp["b_forget"]).reshape(2, 1, 4),
                w_up=f(inp["w_mlp_up"]), w_down=f(inp["w_mlp_down"]), w_pool=f(inp["w_pool"]),
                ktp0=ktp[0], ktp1=ktp[1], vp0=vp[0], vp1=vp[1], lfp0=lfp[0], lfp1=lfp[1])


def core_inputs(cfg, c, inp, shared):
    NB = cfg.nbs
    f = lambda a: np.ascontiguousarray(np.asarray(a), dtype=np.float32)
    pt = np.asarray(inp["page_table"])[NB * c:NB * (c + 1)].astype(np.int32)
    m = dict(shared)
    m.update(xp=f(inp["x_prompt"][c]), xs=f(np.asarray(inp["x_sample"])[NB * c:NB * (c + 1)]).reshape(cfg.ns, D),
             ptab=np.ascontiguousarray(pt.reshape(1, -1)), ptabT=np.ascontiguousarray(pt.T),
             sgla=f(np.asarray(inp["state_gla"])[:, NB * c:NB * (c + 1)]), spool=f(np.asarray(inp["state_pool"])[:, NB * c:NB * (c + 1)]))
    return m


def assemble(cfg, results, B, Bs):
    NB = cfg.nbs
    T = cfg.seq
    r = results
    cat = lambda k, ax: np.concatenate([r[c][k] for c in range(len(r))], axis=ax)
    y_prompt = np.stack([r[c]["y"] for c in range(B)], 0)
    y_sample = np.concatenate([r[c]["ys"].reshape(NB, 4, D) for c in range(B)], 0)
    fk = np.stack([r[c]["fk"].reshape(2, T, 4, 128) for c in range(B)], 1)
    fv = np.stack([r[c]["fv"].reshape(2, T, 4, 128) for c in range(B)], 1)
    fl = np.stack([r[c]["fl"] for c in range(B)], 1)
    fks = np.concatenate([r[c]["fks"].reshape(2, NB, 4, 4, 128) for c in range(B)], 1)
    fvs = np.concatenate([r[c]["fvs"].reshape(2, NB, 4, 4, 128) for c in range(B)], 1)
    fls = np.concatenate([r[c]["fls"].reshape(2, NB, 4, 4) for c in range(B)], 1)
    gp = np.stack([r[c]["gp"] for c in range(B)], 1)
    gs = np.concatenate([r[c]["gs"] for c in range(B)], 1)
    npool = np.stack([r[c]["npool"] for c in range(B)], 1)
    nps = np.concatenate([r[c]["nps"] for c in range(B)], 1)
    return tuple(np.ascontiguousarray(a, dtype=np.float32) for a in (y_prompt, y_sample, fk, fv, fl, fks, fvs, fls, gp, gs, npool, nps))


_NC_CACHE = {}


def kernel(**inp):
    B, T, _ = np.asarray(inp["x_prompt"]).shape
    Bs = np.asarray(inp["x_sample"]).shape[0]
    npg = np.asarray(inp["page_table"]).shape[1]
    npool = np.asarray(inp["cache_fox_k"]).shape[1]
    cfg = Cfg(seq=T, depth=4, tt=256, npg=npg, npool=npool)
    key = (T, npg, npool)
    if key not in _NC_CACHE:
        _NC_CACHE[key] = Builder(cfg).build()
    nc = _NC_CACHE[key]
    shared = shared_inputs(cfg, inp)
    maps = [core_inputs(cfg, c, inp, shared) for c in range(B)]
    res = run_bass_kernel_spmd(nc, maps, core_ids=list(range(B)))
    return assemble(cfg, res.results, B, Bs)
```
